# Optimizing a Trainium2 kernel written in Bass

```python
import jax, jax.numpy as jnp
from jax import lax
import numpy as np

D_MODEL = 1024
BATCH = 4
SEQ = 8192
DEPTH = 1

MIX_WIDTH = D_MODEL
CONV_CH = MIX_WIDTH // 2
POOL_WIDTH = MIX_WIDTH - CONV_CH
CONV_K = 3
POOL_WINDOWS = (2, 4, 8, 16)
N_POOL_GROUPS = len(POOL_WINDOWS)
POOL_GROUP_DIM = POOL_WIDTH // N_POOL_GROUPS
IN_PROJ_WIDTH = 3 * CONV_CH + POOL_WIDTH
D_FF = 4 * D_MODEL
N_MOD = 6
LN_EPS = 1e-5
DEEPNORM_ALPHA = (2.0 * DEPTH) ** 0.25
DEEPNORM_BETA = (8.0 * DEPTH) ** -0.25
ADA_INIT = 0.25

kernel_name = "hybrid_conv_pool_sqrelu_deepnorm_adaln"


def _layer_norm(x, g, b):
    xf = x.astype(jnp.float32)
    mu = jnp.mean(xf, axis=-1, keepdims=True)
    var = jnp.mean(jnp.square(xf - mu), axis=-1, keepdims=True)
    y = (xf - mu) * lax.rsqrt(var + LN_EPS) * g.astype(jnp.float32) + b.astype(jnp.float32)
    return y.astype(x.dtype)


def _short_conv(u, w):
    s = u.shape[1]
    up = jnp.pad(u, ((0, 0), (CONV_K - 1, 0), (0, 0)))
    y = up[:, 0:s] * w[0]
    for k in range(1, CONV_K):
        y = y + up[:, k:k + s] * w[k]
    return y


def _multiscale_pool(u, w_pool, pool_scale):
    b, s, _ = u.shape
    uf = u.astype(jnp.float32)
    cs = jnp.cumsum(uf, axis=1)
    pos = jnp.arange(1, s + 1, dtype=jnp.float32)[None, :, None]
    outs = []
    for gi, win in enumerate(POOL_WINDOWS):
        sl = slice(gi * POOL_GROUP_DIM, (gi + 1) * POOL_GROUP_DIM)
        cs_g = cs[..., sl]
        prev = jnp.pad(cs_g, ((0, 0), (win, 0), (0, 0)))[:, :s]
        mean = (cs_g - prev) / jnp.minimum(pos, float(win))
        outs.append(mean - uf[..., sl])
    p = jnp.stack(outs, axis=2)
    p = jnp.einsum("bsgc,gcd->bsgd", p, w_pool.astype(jnp.float32))
    p = p.reshape(b, s, POOL_WIDTH) * pool_scale.astype(jnp.float32)
    return p.astype(u.dtype)


def setup_inputs(seed: int = 0) -> dict:
    key = jax.random.key(seed)
    ks = jax.random.split(key, 16)
    f32 = jnp.float32
    x = jax.random.normal(ks[0], (BATCH, SEQ, D_MODEL), f32)
    c = jax.random.normal(ks[1], (BATCH, D_MODEL), f32)
    w_ada = jax.random.normal(ks[2], (DEPTH, D_MODEL, N_MOD * D_MODEL), f32) * (ADA_INIT * D_MODEL ** -0.5)
    b_ada = 0.02 * jax.random.normal(ks[3], (DEPTH, N_MOD * D_MODEL), f32)
    w_in = jax.random.normal(ks[4], (DEPTH, D_MODEL, IN_PROJ_WIDTH), f32) * D_MODEL ** -0.5
    conv_w = jax.random.normal(ks[5], (DEPTH, CONV_K, CONV_CH), f32) * CONV_K ** -0.5
    w_pool = jax.random.normal(ks[6], (DEPTH, N_POOL_GROUPS, POOL_GROUP_DIM, POOL_GROUP_DIM), f32) * POOL_GROUP_DIM ** -0.5
    pool_scale = 1.0 + 0.1 * jax.random.normal(ks[7], (DEPTH, POOL_WIDTH), f32)
    w_out = jax.random.normal(ks[8], (DEPTH, MIX_WIDTH, D_MODEL), f32) * (DEEPNORM_BETA * MIX_WIDTH ** -0.5)
    ln1_g = 1.0 + 0.02 * jax.random.normal(ks[9], (DEPTH, D_MODEL), f32)
    ln1_b = 0.02 * jax.random.normal(ks[10], (DEPTH, D_MODEL), f32)
    w_mlp_in = jax.random.normal(ks[11], (DEPTH, D_MODEL, D_FF), f32) * D_MODEL ** -0.5
    w_mlp_out = jax.random.normal(ks[12], (DEPTH, D_FF, D_MODEL), f32) * (DEEPNORM_BETA * D_FF ** -0.5)
    ln2_g = 1.0 + 0.02 * jax.random.normal(ks[13], (DEPTH, D_MODEL), f32)
    ln2_b = 0.02 * jax.random.normal(ks[14], (DEPTH, D_MODEL), f32)
    return {"x": x, "c": c, "w_ada": w_ada, "b_ada": b_ada, "w_in": w_in, "conv_w": conv_w,
            "w_pool": w_pool, "pool_scale": pool_scale, "w_out": w_out, "ln1_g": ln1_g,
            "ln1_b": ln1_b, "w_mlp_in": w_mlp_in, "w_mlp_out": w_mlp_out, "ln2_g": ln2_g,
            "ln2_b": ln2_b}


def reference(x, c, w_ada, b_ada, w_in, conv_w, w_pool, pool_scale, w_out, ln1_g, ln1_b,
              w_mlp_in, w_mlp_out, ln2_g, ln2_b):
    cond = jax.nn.silu(c)
    for l in range(DEPTH):
        mod = (cond @ w_ada[l] + b_ada[l])[:, None, :]
        sh1, sc1, g1, sh2, sc2, g2 = jnp.split(mod, N_MOD, axis=-1)

        h = x * (1.0 + sc1) + sh1
        z = h @ w_in[l]
        gate_b, gate_c, v_conv, v_pool = jnp.split(
            z, [CONV_CH, 2 * CONV_CH, 3 * CONV_CH], axis=-1)
        y_conv = gate_b * _short_conv(gate_c * v_conv, conv_w[l])
        y_pool = _multiscale_pool(v_pool, w_pool[l], pool_scale[l])
        mix = jnp.concatenate([y_conv, y_pool], axis=-1) @ w_out[l]
        x = _layer_norm(DEEPNORM_ALPHA * x + (1.0 + g1) * mix, ln1_g[l], ln1_b[l])

        h = x * (1.0 + sc2) + sh2
        f = jnp.square(jax.nn.relu(h @ w_mlp_in[l])) @ w_mlp_out[l]
        x = _layer_norm(DEEPNORM_ALPHA * x + (1.0 + g2) * f, ln2_g[l], ln2_b[l])
    return x
```

```python
import numpy as np
from contextlib import ExitStack
import concourse.bass as bass
import concourse.mybir as mybir
from concourse.bass_utils import run_bass_kernel_spmd

F32 = mybir.dt.float32
BF16 = mybir.dt.bfloat16
ALU = mybir.AluOpType
AF = mybir.ActivationFunctionType

D = 1024
SEQ = 8192
BATCH = 4
NCORES = 8
TOK = 4096
HALO = 16
T = 256
NSUB = T // 128
NT = TOK // T
DFF = 4096
ALPHA = 2.0 ** 0.25
EPS = 1e-5
WINS = (2, 4, 8, 16)

C_C, C_G1, C_B1, C_CW, C_PS, C_FLAG, C_INV, NV = 0, 8, 16, 24, 36, 40, 41, 105

ENGS = ("pe", "act", "dve", "pool", "sp")


class _Op:
    __slots__ = ("idx", "eng", "fn", "deps", "dma", "dma_val", "sig", "sig_val")

    def __init__(self, idx, eng, fn, dma):
        self.idx = idx
        self.eng = eng
        self.fn = fn
        self.deps = {}
        self.dma = dma
        self.dma_val = 0
        self.sig = False
        self.sig_val = 0


class Sched:
    def __init__(self):
        self.ops = []
        self.last_w = {}
        self.readers = {}
        self.dma_cnt = {}

    def op(self, eng, fn, r=(), w=(), dma=None):
        o = _Op(len(self.ops), eng, fn, dma)
        for k in r:
            lw = self.last_w.get(k)
            if lw is not None:
                o.deps[lw] = True
        for k in w:
            lw = self.last_w.get(k)
            if lw is not None and lw not in o.deps:
                o.deps[lw] = False
            for rd in self.readers.get(k, ()):
                if rd not in o.deps:
                    o.deps[rd] = False
        for k in r:
            self.readers.setdefault(k, []).append(o.idx)
        for k in w:
            self.last_w[k] = o.idx
            self.readers[k] = []
        if dma is not None:
            c = self.dma_cnt.get(dma, 0) + 1
            self.dma_cnt[dma] = c
            o.dma_val = 16 * c
        self.ops.append(o)
        return o

    def emit(self, nc, final_eng="sp"):
        ops = self.ops
        need = []
        for o in ops:
            lst = []
            for d, is_raw in o.deps.items():
                p = ops[d]
                if p.dma is not None:
                    lst.append(p)
                elif p.eng == o.eng and o.dma is None:
                    if is_raw and o.eng != "pe":
                        lst.append(p)
                else:
                    lst.append(p)
            need.append(lst)
            for p in lst:
                if p.dma is None:
                    p.sig = True
        cnt = {e: 0 for e in ENGS}
        for o in ops:
            if o.sig:
                cnt[o.eng] += 1
                o.sig_val = cnt[o.eng]
        dma_keys = sorted(self.dma_cnt.keys())
        with ExitStack() as st:
            esem = {e: st.enter_context(nc.semaphore("sem_" + e)) for e in ENGS}
            dsem = {k: st.enter_context(nc.semaphore("dsem_" + str(k))) for k in dma_keys}
            block = st.enter_context(nc.Block())
            per_eng = {e: [o for o in ops if o.eng == e] for e in ENGS}

            def run(eng_name, eng):
                waited = {}
                for o in per_eng[eng_name]:
                    for p in need[o.idx]:
                        if p.dma is not None:
                            s, v, key = dsem[p.dma], p.dma_val, ("d", p.dma)
                        else:
                            s, v, key = esem[p.eng], p.sig_val, ("e", p.eng)
                        if waited.get(key, 0) >= v:
                            continue
                        waited[key] = v
                        eng.wait_ge(s, v)
                    ins = o.fn(eng)
                    if o.dma is not None:
                        ins.then_inc(dsem[o.dma], 16)
                    elif o.sig:
                        ins.then_inc(esem[o.eng], 1)
                if eng_name == final_eng:
                    for k in dma_keys:
                        eng.wait_ge(dsem[k], 16 * self.dma_cnt[k])

            @block.tensor
            def _(e):
                run("pe", e)

            @block.scalar
            def _(e):
                run("act", e)

            @block.vector
            def _(e):
                run("dve", e)

            @block.gpsimd
            def _(e):
                run("pool", e)

            @block.sync
            def _(e):
                run("sp", e)


def build_program():
    nc = bass.Bass("TRN2", target_bir_lowering=False)

    def din(name, shape, dt=F32):
        return nc.dram_tensor(name, shape, dt, kind="ExternalInput").ap()

    xh = din("xh", [TOK + HALO, D])
    vecs = din("vecs", [128, NV])
    rows = din("rows", [4, D])
    b_ada = din("b_ada", [6 * D])
    w_ada = din("w_ada", [D, 6 * D])
    w_in = din("w_in", [D, 2048])
    w_pool = din("w_pool", [4, 128, 128])
    w_out = din("w_out", [D, D])
    w1 = din("w1", [D, DFF])
    w2 = din("w2", [DFF, D])
    out = nc.dram_tensor("out", [TOK, D], F32, kind="ExternalOutput").ap()
    w1s = nc.dram_tensor("w1s", [128, 8, DFF], BF16, kind="Internal").ap()
    w2s = nc.dram_tensor("w2s", [128, 32, D], BF16, kind="Internal").ap()

    S = Sched()
    with ExitStack() as st:
        def sb(name, shape, dt=F32):
            return st.enter_context(nc.sbuf_tensor(name, shape, dt))

        ps = [st.enter_context(nc.psum_tensor("ps%d" % k, [128, 512], F32)) for k in range(8)]
        psb = [p.bitcast(BF16) for p in ps]

        vec = sb("vec", [128, NV])
        idf = sb("idf", [128, 128])
        idb = sb("idb", [128, 128], BF16)
        ones_f = sb("ones_f", [128, 128])
        cond = sb("cond", [128, 8])
        condrep = sb("condrep", [128, 8, 128], BF16)
        w_in_sb = sb("w_in_sb", [128, 8, 2048], BF16)
        w_out_sb = sb("w_out_sb", [128, 8, D], BF16)
        w_pool_sb = sb("w_pool_sb", [128, 4, 128], BF16)
        wada_r = [sb("wada%d" % i, [128, 8, 256], BF16) for i in range(2)]
        bada_r = [sb("bada%d" % i, [128, 256]) for i in range(2)]
        tmpbc = [sb("tmpbc%d" % i, [128, 256]) for i in range(2)]
        g1p_bc = sb("g1p_bc", [128, D])
        g2p_bc = sb("g2p_bc", [128, D])
        ag_bc = sb("ag_bc", [128, D])
        ab_bc = sb("ab_bc", [128, D])
        ln2g_bc = sb("ln2g_bc", [128, D])
        ln2b_bc = sb("ln2b_bc", [128, D])
        modT = sb("modT", [128, 32])
        GB = sb("GB", [128, 24])
        xt = [sb("xt%d" % i, [128, NSUB, D]) for i in range(2)]
        xbf = sb("xbf", [128, NSUB, D], BF16)
        h1T = sb("h1T", [128, 8, T], BF16)
        h1halo = sb("h1halo", [128, 8, HALO], BF16)
        vc = [sb("vc%d" % i, [128, T]) for i in range(2)]
        uw = [sb("uw%d" % i, [128, T + HALO]) for i in range(2)]
        cv = [sb("cv%d" % i, [128, T]) for i in range(2)]
        vpw = [sb("vpw%d" % i, [128, T + HALO]) for i in range(2)]
        sA = [sb("sA%d" % i, [128, T + HALO]) for i in range(2)]
        sB = [sb("sB%d" % i, [128, T + HALO]) for i in range(2)]
        pbf = [sb("pbf%d" % i, [128, T], BF16) for i in range(2)]
        halo_u = sb("halo_u", [128, 4, HALO])
        halo_vp = sb("halo_vp", [128, 4, HALO])
        tmp16 = sb("tmp16", [128, HALO])
        ycatT = sb("ycatT", [128, 8, T], BF16)
        NR = 3
        rb = [sb("rb%d" % i, [128, D]) for i in range(NR)]
        ub = sb("ub", [128, NSUB, D])
        ubf = [sb("ubf%d" % i, [128, D], BF16) for i in range(2)]
        h2T = sb("h2T", [128, 8, T], BF16)
        NW = 2
        w1r = [sb("w1r%d" % i, [128, 8, 512], BF16) for i in range(NW)]
        w2r = [sb("w2r%d" % i, [128, 4, D], BF16) for i in range(NW)]
        rl = [sb("rl%d" % i, [128, T]) for i in range(2)]
        aT = [sb("aT%d" % i, [128, 4, T], BF16) for i in range(2)]
        stats = [sb("stats%d" % i, [128, 12]) for i in range(2)]
        mv = [sb("mv%d" % i, [128, 8]) for i in range(2)]

        rot = [0]

        def rbank():
            k = rot[0]
            rot[0] = (k + 1) % 4
            return k

        S.op("sp", lambda e: e.dma_start(out=vec[:], in_=vecs), w=["vec"], dma="ld_vec")
        S.op("pool", lambda e: e.memset(idf[:], 0.0), w=["idf"])
        S.op("pool", lambda e: e.affine_select(out=idf[:], in_=idf[:], pattern=[[-1, 128]],
                                               compare_op=ALU.not_equal, fill=1.0, base=0,
                                               channel_multiplier=1), r=["idf"], w=["idf"])
        S.op("pool", lambda e: e.tensor_copy(out=idb[:], in_=idf[:]), r=["idf"], w=["idb"])
        S.op("pool", lambda e: e.memset(ones_f[:], 1.0), w=["ones_f"])
        S.op("act", lambda e: e.activation(out=cond[:], in_=vec[:, C_C:C_C + 8], func=AF.Silu),
             r=["vec"], w=["cond"])
        for kc in range(8):
            S.op("dve", lambda e, kc=kc: e.tensor_scalar(out=condrep[:, kc, :], in0=ones_f[:],
                                                         scalar1=cond[:, kc:kc + 1], scalar2=None,
                                                         op0=ALU.mult),
                 r=["ones_f", "cond"], w=["condrep"])

        for i, (dst, name) in enumerate(((ag_bc, "ag_bc"), (ab_bc, "ab_bc"),
                                         (ln2g_bc, "ln2g_bc"), (ln2b_bc, "ln2b_bc"))):
            S.op("sp", lambda e, dst=dst, i=i: e.dma_start(out=dst[:], in_=rows[i].partition_broadcast(128)),
                 w=[name], dma="ld_" + name)
        S.op("act", lambda e: e.mul(out=ag_bc[:], in_=ag_bc[:], mul=ALPHA), r=["ag_bc"], w=["ag_bc"])
        S.op("act", lambda e: e.mul(out=ab_bc[:], in_=ab_bc[:], mul=ALPHA), r=["ab_bc"], w=["ab_bc"])

        def cast_dma(dst, src, wkeys, key):
            S.op("pool", lambda e: e.dma_start(out=dst, in_=src), w=wkeys, dma=key)

        def mod_block(blk):
            slot = blk % 2
            col0 = blk * 256
            vi, off = col0 // D, col0 % D
            cast_dma(wada_r[slot][:], w_ada[:, col0:col0 + 256].rearrange("(kc p) n -> p kc n", p=128),
                     [("wada", slot)], "wada%d" % slot)
            S.op("sp", lambda e: e.dma_start(out=bada_r[slot][:],
                                             in_=b_ada[col0:col0 + 256].partition_broadcast(128)),
                 w=[("bada", slot)], dma="bada%d" % slot)
            k = rbank()

            def mm(e):
                for kc in range(8):
                    ins = e.matmul(ps[k][:, 0:256], lhsT=condrep[:, kc, :], rhs=wada_r[slot][:, kc, :],
                                   start=(kc == 0), stop=(kc == 7))
                return ins
            S.op("pe", mm, r=[("wada", slot), "condrep"], w=[("ps", k)])
            if vi in (2, 5):
                dst, name = (g1p_bc, "g1p_bc") if vi == 2 else (g2p_bc, "g2p_bc")
                S.op("dve", lambda e: e.scalar_tensor_tensor(out=dst[:, off:off + 256], in0=ps[k][:, 0:256],
                                                             scalar=1.0, in1=bada_r[slot][:],
                                                             op0=ALU.add, op1=ALU.add),
                     r=[("ps", k), ("bada", slot)], w=[(name, off)])
            else:
                addc = 1.0 if vi in (1, 4) else 0.0
                S.op("dve", lambda e: e.scalar_tensor_tensor(out=tmpbc[slot][:], in0=ps[k][:, 0:256],
                                                             scalar=addc, in1=bada_r[slot][:],
                                                             op0=ALU.add, op1=ALU.add),
                     r=[("ps", k), ("bada", slot)], w=[("tmpbc", slot)])
                vslot = {0: 0, 1: 1, 3: 2, 4: 3}[vi]

                def mmT(e):
                    for c in range(2):
                        col = vslot * 8 + off // 128 + c
                        ins = e.matmul(ps[7][:, col:col + 1], lhsT=tmpbc[slot][0:1, c * 128:(c + 1) * 128],
                                       rhs=ones_f[0:1, 0:1], start=True, stop=True)
                    return ins
                S.op("pe", mmT, r=[("tmpbc", slot), "ones_f"], w=[("ps", 7)])

        for blk in range(8):
            mod_block(blk)
        for h in range(2):
            cast_dma(w_in_sb[:, 4 * h:4 * h + 4, :],
                     w_in[512 * h:512 * (h + 1), :].rearrange("(kc p) n -> p kc n", p=128),
                     [("w_in", h)], "c_w_in%d" % h)
        cast_dma(w_pool_sb[:], w_pool.rearrange("g c d -> c g d"), ["w_pool"], "c_w_pool")
        cast_dma(w_out_sb[:], w_out.rearrange("(kc p) n -> p kc n", p=128), ["w_out"], "c_w_out")
        for blk in range(8, 24):
            mod_block(blk)
        for b in range(8):
            cast_dma(w1s[:, :, b * 512:(b + 1) * 512],
                     w1[:, b * 512:(b + 1) * 512].rearrange("(kc p) n -> p kc n", p=128),
                     [("w1s", b)], "c_w1_%d" % b)
            cast_dma(w2s[:, 4 * b:4 * b + 4, :],
                     w2[b * 512:(b + 1) * 512, :].rearrange("(c p) n -> p c n", p=128),
                     [("w2s", b)], "c_w2_%d" % b)
        S.op("act", lambda e: e.copy(out=modT[:], in_=ps[7][:, 0:32]), r=[("ps", 7)], w=["modT"])
        S.op("dve", lambda e: e.tensor_tensor(out=GB[:, 0:8], in0=vec[:, C_G1:C_G1 + 8], in1=modT[:, 24:32],
                                              op=ALU.mult), r=["vec", "modT"], w=["G2"])
        S.op("dve", lambda e: e.tensor_tensor(out=GB[:, 16:24], in0=vec[:, C_B1:C_B1 + 8], in1=modT[:, 24:32],
                                              op=ALU.mult), r=["vec", "modT"], w=["B2t"])
        S.op("dve", lambda e: e.tensor_tensor(out=GB[:, 8:16], in0=GB[:, 16:24], in1=modT[:, 16:24],
                                              op=ALU.add), r=["B2t", "modT"], w=["B2"])
        G1PK = [("g1p_bc", o) for o in range(0, D, 256)]
        G2PK = [("g2p_bc", o) for o in range(0, D, 256)]

        def layer_norm_stats(src_ap, skey, par):
            st_, m_ = stats[par], mv[par]
            S.op("dve", lambda e: e.bn_stats(out=st_[:, 0:6], in_=src_ap[:, 0:512]), r=[skey], w=[("st0", par)])
            S.op("dve", lambda e: e.bn_stats(out=st_[:, 6:12], in_=src_ap[:, 512:1024]), r=[skey], w=[("st1", par)])
            S.op("dve", lambda e: e.bn_aggr(out=m_[:, 0:2], in_=st_[:, 0:12]),
                 r=[("st0", par), ("st1", par)], w=[("mv01", par)])
            S.op("dve", lambda e: e.tensor_scalar(out=m_[:, 4:5], in0=m_[:, 1:2], scalar1=EPS, scalar2=None,
                                                  op0=ALU.add), r=[("mv01", par)], w=[("mv4", par)])
            S.op("act", lambda e: e.activation(out=m_[:, 5:6], in_=m_[:, 4:5], func=AF.Sqrt),
                 r=[("mv4", par)], w=[("mv5", par)])
            S.op("dve", lambda e: e.reciprocal(out=m_[:, 2:3], in_=m_[:, 5:6]), r=[("mv5", par)], w=[("mv2", par)])
            S.op("dve", lambda e: e.scalar_tensor_tensor(out=m_[:, 3:4], in0=m_[:, 0:1], scalar=-1.0,
                                                         in1=m_[:, 2:3], op0=ALU.mult, op1=ALU.mult),
                 r=[("mv01", par), ("mv2", par)], w=[("mv3", par)])

        lnc = [0]
        rbc = [0]

        def load_x(i):
            slot = i % 2
            S.op("sp", lambda e: e.dma_start(
                out=xt[slot][:], in_=xh[HALO + i * T:HALO + (i + 1) * T, :].rearrange("(s p) f -> p s f", p=128)),
                w=[("xt", slot, s) for s in range(NSUB)], dma="ld_x%d" % slot)

        load_x(0)

        def do_tile(i):
            xs = i % 2
            for s in range(NSUB):
                S.op("act", lambda e, s=s: e.copy(out=xbf[:, s, :], in_=xt[xs][:, s, :]),
                     r=[("xt", xs, s)], w=[("xbf", s)])
            for kq in range(2):
                k = rbank()

                def trx(e, kq=kq, k=k):
                    for kk in range(4):
                        kc = kq * 4 + kk
                        for s in range(NSUB):
                            ins = e.transpose(psb[k][:, kk * T + s * 128:kk * T + (s + 1) * 128],
                                              xbf[:, s, kc * 128:(kc + 1) * 128], idb[:])
                    return ins
                S.op("pe", trx, r=[("xbf", s) for s in range(NSUB)] + ["idb"], w=[("ps", k)])
                for kk in range(4):
                    kc = kq * 4 + kk
                    S.op("dve", lambda e, kk=kk, kc=kc, k=k: e.tensor_scalar(
                        out=h1T[:, kc, :], in0=psb[k][:, kk * T:(kk + 1) * T],
                        scalar1=modT[:, 8 + kc:9 + kc], scalar2=modT[:, kc:kc + 1],
                        op0=ALU.mult, op1=ALU.add),
                        r=[("ps", k), "modT"], w=[("h1T", kc)])
            if i == 0:
                S.op("sp", lambda e: e.dma_start(out=rb[0][0:HALO, :], in_=xh[0:HALO, :]), w=[("rb", 0)], dma="ld_halo")
                S.op("act", lambda e: e.copy(out=ubf[0][0:HALO, :], in_=rb[0][0:HALO, :]), r=[("rb", 0)], w=[("ubf", 0)])
                k = rbank()

                def trh(e, k=k):
                    for kc in range(8):
                        ins = e.transpose(psb[k][:, kc * HALO:(kc + 1) * HALO],
                                          ubf[0][0:HALO, kc * 128:(kc + 1) * 128], idb[0:HALO, 0:HALO])
                    return ins
                S.op("pe", trh, r=[("ubf", 0), "idb"], w=[("ps", k)])
                for kc in range(8):
                    S.op("dve", lambda e, kc=kc, k=k: e.tensor_scalar(
                        out=h1halo[:, kc, :], in0=psb[k][:, kc * HALO:(kc + 1) * HALO],
                        scalar1=modT[:, 8 + kc:9 + kc], scalar2=modT[:, kc:kc + 1],
                        op0=ALU.mult, op1=ALU.add), r=[("ps", k), "modT"], w=[("h1halo", kc)])
                    S.op("dve", lambda e, kc=kc: e.tensor_scalar(
                        out=h1halo[:, kc, :], in0=h1halo[:, kc, :], scalar1=vec[:, C_FLAG:C_FLAG + 1],
                        scalar2=None, op0=ALU.mult), r=[("h1halo", kc), "vec"], w=[("h1halo", kc)])
            if i + 1 < NT:
                load_x(i + 1)

            H1K = [("h1T", kc) for kc in range(8)]
            H1HK = [("h1halo", kc) for kc in range(8)]
            WINK = [("w_in", 0), ("w_in", 1)]

            def inproj(col0, halo=False):
                k = rbank()
                n = HALO if halo else T

                def mm(e):
                    for kc in range(8):
                        rhs = h1halo[:, kc, :] if halo else h1T[:, kc, :]
                        ins = e.matmul(ps[k][:, 0:n], lhsT=w_in_sb[:, kc, col0:col0 + 128], rhs=rhs,
                                       start=(kc == 0), stop=(kc == 7))
                    return ins
                S.op("pe", mm, r=WINK + (H1HK if halo else H1K), w=[("ps", k)])
                return k

            def do_chunk(j):
                par = j % 2
                k_vc = inproj(1024 + 128 * j)
                S.op("act", lambda e, k=k_vc: e.copy(out=vc[par][:], in_=ps[k][:, 0:T]),
                     r=[("ps", k_vc)], w=[("vc", par)])
                k_gc = inproj(512 + 128 * j)
                S.op("dve", lambda e, k=k_gc: e.tensor_tensor(out=uw[par][:, HALO:HALO + T], in0=ps[k][:, 0:T],
                                                              in1=vc[par][:], op=ALU.mult),
                     r=[("ps", k_gc), ("vc", par)], w=[("uw", par)])
                if i == 0:
                    k_h = inproj(1024 + 128 * j, halo=True)
                    S.op("act", lambda e, k=k_h: e.copy(out=tmp16[:], in_=ps[k][:, 0:HALO]),
                         r=[("ps", k_h)], w=["tmp16"])
                    k_h2 = inproj(512 + 128 * j, halo=True)
                    S.op("dve", lambda e, k=k_h2: e.tensor_tensor(out=uw[par][:, 0:HALO], in0=ps[k][:, 0:HALO],
                                                                  in1=tmp16[:], op=ALU.mult),
                         r=[("ps", k_h2), "tmp16"], w=[("uwh", par)])
                else:
                    S.op("pool", lambda e, j=j: e.tensor_copy(out=uw[par][:, 0:HALO], in_=halo_u[:, j, :]),
                         r=[("halo_u", j)], w=[("uwh", par)])
                cw = C_CW + 3 * j
                S.op("pool", lambda e, cw=cw: e.tensor_scalar(out=cv[par][:], in0=uw[par][:, HALO:HALO + T],
                                                              scalar1=vec[:, cw + 2:cw + 3], scalar2=None,
                                                              op0=ALU.mult),
                     r=[("uw", par), "vec"], w=[("cv", par)])
                S.op("dve", lambda e, cw=cw: e.scalar_tensor_tensor(out=cv[par][:], in0=uw[par][:, HALO - 1:HALO - 1 + T],
                                                                     scalar=vec[:, cw + 1:cw + 2], in1=cv[par][:],
                                                                     op0=ALU.mult, op1=ALU.add),
                     r=[("uw", par), ("uwh", par), ("cv", par), "vec"], w=[("cv", par)])
                S.op("dve", lambda e, cw=cw: e.scalar_tensor_tensor(out=cv[par][:], in0=uw[par][:, HALO - 2:HALO - 2 + T],
                                                                     scalar=vec[:, cw:cw + 1], in1=cv[par][:],
                                                                     op0=ALU.mult, op1=ALU.add),
                     r=[("uw", par), ("uwh", par), ("cv", par), "vec"], w=[("cv", par)])
                S.op("pool", lambda e, j=j: e.tensor_copy(out=halo_u[:, j, :], in_=uw[par][:, T:T + HALO]),
                     r=[("uw", par)], w=[("halo_u", j)])
                k_gb = inproj(128 * j)
                S.op("dve", lambda e, k=k_gb, j=j: e.tensor_tensor(out=ycatT[:, j, :], in0=ps[k][:, 0:T],
                                                                   in1=cv[par][:], op=ALU.mult),
                     r=[("ps", k_gb), ("cv", par)], w=[("ycatT", j)])
                k_vp = inproj(1536 + 128 * j)
                S.op("act", lambda e, k=k_vp: e.copy(out=vpw[par][:, HALO:HALO + T], in_=ps[k][:, 0:T]),
                     r=[("ps", k_vp)], w=[("vpw", par)])
                if i == 0:
                    k_h3 = inproj(1536 + 128 * j, halo=True)
                    S.op("act", lambda e, k=k_h3: e.copy(out=vpw[par][:, 0:HALO], in_=ps[k][:, 0:HALO]),
                         r=[("ps", k_h3)], w=[("vpwh", par)])
                else:
                    S.op("pool", lambda e, j=j: e.tensor_copy(out=vpw[par][:, 0:HALO], in_=halo_vp[:, j, :]),
                         r=[("halo_vp", j)], w=[("vpwh", par)])
                L = T + HALO
                VK = [("vpw", par), ("vpwh", par)]
                S.op("pool", lambda e: e.tensor_tensor(out=sA[par][:, 1:L], in0=vpw[par][:, 1:L],
                                                       in1=vpw[par][:, 0:L - 1], op=ALU.add),
                     r=VK, w=[("sA", par)])
                cur, curk = sA, "sA"
                if j >= 1:
                    S.op("pool", lambda e: e.tensor_tensor(out=sB[par][:, 3:L], in0=sA[par][:, 3:L],
                                                           in1=sA[par][:, 1:L - 2], op=ALU.add),
                         r=[("sA", par)], w=[("sB", par)])
                    cur, curk = sB, "sB"
                if j >= 2:
                    S.op("pool", lambda e: e.tensor_tensor(out=sA[par][:, 7:L], in0=sB[par][:, 7:L],
                                                           in1=sB[par][:, 3:L - 4], op=ALU.add),
                         r=[("sB", par)], w=[("sA", par)])
                    cur, curk = sA, "sA"
                if j >= 3:
                    S.op("pool", lambda e: e.tensor_tensor(out=sB[par][:, 15:L], in0=sA[par][:, 15:L],
                                                           in1=sA[par][:, 7:L - 8], op=ALU.add),
                         r=[("sA", par)], w=[("sB", par)])
                    cur, curk = sB, "sB"
                win = float(WINS[j])
                S.op("dve", lambda e, cur=cur: e.scalar_tensor_tensor(
                    out=pbf[par][:], in0=cur[par][:, HALO:HALO + T], scalar=1.0 / win,
                    in1=vpw[par][:, HALO:HALO + T], op0=ALU.mult, op1=ALU.subtract),
                    r=[(curk, par), ("vpw", par)], w=[("pbf", par)])
                if i == 0:
                    ic = C_INV + 16 * j
                    S.op("pool", lambda e, cur=cur, ic=ic: e.tensor_tensor(
                        out=tmp16[:], in0=cur[par][:, HALO:2 * HALO], in1=vec[:, ic:ic + 16], op=ALU.mult),
                        r=[(curk, par), "vec"], w=["tmp16"])
                    S.op("pool", lambda e: e.tensor_tensor(
                        out=pbf[par][:, 0:HALO], in0=tmp16[:], in1=vpw[par][:, HALO:2 * HALO], op=ALU.subtract),
                        r=["tmp16", ("vpw", par), ("pbf", par)], w=[("pbf", par)])
                S.op("pool", lambda e, j=j: e.tensor_copy(out=halo_vp[:, j, :], in_=vpw[par][:, T:T + HALO]),
                     r=[("vpw", par)], w=[("halo_vp", j)])
                k_q = rbank()
                S.op("pe", lambda e, k=k_q, j=j: e.matmul(ps[k][:, 0:T], lhsT=w_pool_sb[:, j, :], rhs=pbf[par][:],
                                                          start=True, stop=True),
                     r=["w_pool", ("pbf", par)], w=[("ps", k_q)])
                S.op("act", lambda e, k=k_q, j=j: e.mul(out=ycatT[:, 4 + j, :], in_=ps[k][:, 0:T],
                                                        mul=vec[:, C_PS + j:C_PS + j + 1]),
                     r=[("ps", k_q), "vec"], w=[("ycatT", 4 + j)])

            for j in range(4):
                do_chunk(j)

            YK = [("ycatT", c) for c in range(8)]
            uT_banks = [rbank(), rbank()]

            def do_sub1(s):
                ri = rbc[0] % NR
                rbc[0] += 1
                r_ = rb[ri]
                for hf in range(2):
                    ka = 4 + (2 * s + hf) % 4

                    def mmo(e, ka=ka, hf=hf, s=s):
                        for kc in range(8):
                            ins = e.matmul(ps[ka][:, :], lhsT=ycatT[:, kc, s * 128:(s + 1) * 128],
                                           rhs=w_out_sb[:, kc, hf * 512:(hf + 1) * 512],
                                           start=(kc == 0), stop=(kc == 7))
                        return ins
                    S.op("pe", mmo, r=YK + ["w_out"], w=[("ps", ka)])
                    S.op("dve", lambda e, ka=ka, hf=hf, r_=r_: e.tensor_tensor(
                        out=r_[:, hf * 512:(hf + 1) * 512], in0=ps[ka][:, :], in1=g1p_bc[:, hf * 512:(hf + 1) * 512],
                        op=ALU.mult), r=[("ps", ka)] + G1PK, w=[("rb", ri, hf), ("rb", ri)])
                S.op("dve", lambda e, r_=r_, s=s: e.scalar_tensor_tensor(
                    out=r_[:], in0=xt[xs][:, s, :], scalar=ALPHA, in1=r_[:], op0=ALU.mult, op1=ALU.add),
                    r=[("xt", xs, s), ("rb", ri, 0), ("rb", ri, 1)], w=[("rb", ri)])
                par = lnc[0] % 2
                lnc[0] += 1
                layer_norm_stats(r_, ("rb", ri), par)
                m_ = mv[par]
                MK = [("mv2", par), ("mv3", par)]
                S.op("act", lambda e, r_=r_, m_=m_, s=s: e.activation(out=ub[:, s, :], in_=r_[:], func=AF.Identity,
                                                                     bias=m_[:, 3:4], scale=m_[:, 2:3]),
                     r=[("rb", ri)] + MK, w=[("ub", s)])
                up = s % 2
                S.op("act", lambda e, r_=r_, m_=m_, up=up: e.activation(out=ubf[up][:], in_=r_[:], func=AF.Identity,
                                                                       bias=m_[:, 3:4], scale=m_[:, 2:3]),
                     r=[("rb", ri)] + MK, w=[("ubf", up)])
                S.op("pool", lambda e, s=s: e.tensor_tensor(out=ub[:, s, :], in0=ub[:, s, :], in1=ag_bc[:], op=ALU.mult),
                     r=[("ub", s), "ag_bc"], w=[("ub", s)])
                S.op("pool", lambda e, s=s: e.tensor_tensor(out=ub[:, s, :], in0=ub[:, s, :], in1=ab_bc[:], op=ALU.add),
                     r=[("ub", s), "ab_bc"], w=[("ub", s)])
                for kq in range(2):
                    k = uT_banks[kq]

                    def tru(e, kq=kq, k=k, up=up, s=s):
                        for kk in range(4):
                            kc = kq * 4 + kk
                            ins = e.transpose(psb[k][:, kk * T + s * 128:kk * T + (s + 1) * 128],
                                              ubf[up][:, kc * 128:(kc + 1) * 128], idb[:])
                        return ins
                    S.op("pe", tru, r=[("ubf", up), "idb"], w=[("ps", k)])
            for s in range(NSUB):
                do_sub1(s)
            for kq in range(2):
                k = uT_banks[kq]
                for kk in range(4):
                    kc = kq * 4 + kk
                    S.op("dve", lambda e, kk=kk, kc=kc, k=k: e.tensor_scalar(
                        out=h2T[:, kc, :], in0=psb[k][:, kk * T:(kk + 1) * T],
                        scalar1=GB[:, kc:kc + 1], scalar2=GB[:, 8 + kc:9 + kc], op0=ALU.mult, op1=ALU.add),
                        r=[("ps", k), "G2", "B2"], w=[("h2T", kc)])

            H2K = [("h2T", kc) for kc in range(8)]

            def load_w(b):
                slot = (i * 8 + b) % NW
                S.op("sp", lambda e: e.dma_start(out=w1r[slot][:], in_=w1s[:, :, b * 512:(b + 1) * 512]),
                     r=[("w1s", b)], w=[("w1r", slot)], dma="ld_w1_%d" % slot)
                S.op("sp", lambda e: e.dma_start(out=w2r[slot][:], in_=w2s[:, 4 * b:4 * b + 4, :]),
                     r=[("w2s", b)], w=[("w2r", slot)], dma="ld_w2_%d" % slot)

            def stage_a(b):
                slot = (i * 8 + b) % NW
                ap_ = b % 2
                for c in range(4):
                    k = rbank()
                    rp = c % 2

                    def mma(e, k=k, c=c):
                        for kc in range(8):
                            ins = e.matmul(ps[k][:, 0:T], lhsT=w1r[slot][:, kc, c * 128:(c + 1) * 128],
                                           rhs=h2T[:, kc, :], start=(kc == 0), stop=(kc == 7))
                        return ins
                    S.op("pe", mma, r=[("w1r", slot)] + H2K, w=[("ps", k)])
                    S.op("act", lambda e, k=k, rp=rp: e.activation(out=rl[rp][:], in_=ps[k][:, 0:T], func=AF.Relu),
                         r=[("ps", k)], w=[("rl", rp)])
                    S.op("dve", lambda e, rp=rp, c=c: e.tensor_tensor(out=aT[ap_][:, c, :], in0=rl[rp][:], in1=rl[rp][:],
                                                                      op=ALU.mult),
                         r=[("rl", rp)], w=[("aT", ap_, c)])

            def stage_b(b):
                slot = (i * 8 + b) % NW
                ap_ = b % 2
                for s in range(NSUB):
                    for hf in range(2):
                        ka = 4 + (2 * s + hf) % 4

                        def mmb(e, ka=ka, s=s, hf=hf):
                            for c in range(4):
                                ins = e.matmul(ps[ka][:, :], lhsT=aT[ap_][:, c, s * 128:(s + 1) * 128],
                                               rhs=w2r[slot][:, c, hf * 512:(hf + 1) * 512],
                                               start=(b == 0 and c == 0), stop=(b == 7 and c == 3))
                            return ins
                        S.op("pe", mmb, r=[("w2r", slot)] + [("aT", ap_, c) for c in range(4)], w=[("ps", ka)])

            load_w(0)
            stage_a(0)
            for b in range(1, 8):
                load_w(b)
                stage_a(b)
                stage_b(b - 1)
            stage_b(7)

            def do_sub2(s):
                ri = rbc[0] % NR
                rbc[0] += 1
                r_ = rb[ri]
                for hf in range(2):
                    ka = 4 + (2 * s + hf) % 4
                    S.op("dve", lambda e, ka=ka, hf=hf, r_=r_: e.tensor_tensor(
                        out=r_[:, hf * 512:(hf + 1) * 512], in0=ps[ka][:, :], in1=g2p_bc[:, hf * 512:(hf + 1) * 512],
                        op=ALU.mult), r=[("ps", ka)] + G2PK, w=[("rb", ri, hf), ("rb", ri)])
                S.op("dve", lambda e, r_=r_, s=s: e.tensor_tensor(out=r_[:], in0=r_[:], in1=ub[:, s, :], op=ALU.add),
                     r=[("ub", s), ("rb", ri, 0), ("rb", ri, 1)], w=[("rb", ri)])
                par = lnc[0] % 2
                lnc[0] += 1
                layer_norm_stats(r_, ("rb", ri), par)
                m_ = mv[par]
                MK = [("mv2", par), ("mv3", par)]
                S.op("act", lambda e, r_=r_, m_=m_: e.activation(out=r_[:], in_=r_[:], func=AF.Identity,
                                                                bias=m_[:, 3:4], scale=m_[:, 2:3]),
                     r=[("rb", ri)] + MK, w=[("rb", ri)])
                S.op("pool", lambda e, r_=r_: e.tensor_tensor(out=r_[:], in0=r_[:], in1=ln2g_bc[:], op=ALU.mult),
                     r=[("rb", ri), "ln2g_bc"], w=[("rb", ri)])
                S.op("pool", lambda e, r_=r_: e.tensor_tensor(out=r_[:], in0=r_[:], in1=ln2b_bc[:], op=ALU.add),
                     r=[("rb", ri), "ln2b_bc"], w=[("rb", ri)])
                row0 = i * T + s * 128
                S.op("pool", lambda e, r_=r_, row0=row0: e.dma_start(out=out[row0:row0 + 128, :], in_=r_[:]),
                     r=[("rb", ri)], w=[("out", row0)], dma="st_%d" % ri)

            for s in range(NSUB):
                do_sub2(s)

        for i in range(NT):
            do_tile(i)

        S.emit(nc)
    return nc


_CACHE = {}


def _layout_inputs(x, c, w_ada, b_ada, w_in, conv_w, w_pool, pool_scale, w_out, ln1_g, ln1_b,
                   w_mlp_in, w_mlp_out, ln2_g, ln2_b):
    f = np.float32
    x = np.asarray(x, f)
    c = np.asarray(c, f)
    shared = {
        "rows": np.ascontiguousarray(np.stack([np.asarray(ln1_g, f)[0], np.asarray(ln1_b, f)[0],
                                               np.asarray(ln2_g, f)[0], np.asarray(ln2_b, f)[0]])),
        "b_ada": np.ascontiguousarray(np.asarray(b_ada, f)[0]),
        "w_ada": np.ascontiguousarray(np.asarray(w_ada, f)[0]),
        "w_in": np.ascontiguousarray(np.asarray(w_in, f)[0]),
        "w_pool": np.ascontiguousarray(np.asarray(w_pool, f)[0]),
        "w_out": np.ascontiguousarray(np.asarray(w_out, f)[0]),
        "w1": np.ascontiguousarray(np.asarray(w_mlp_in, f)[0]),
        "w2": np.ascontiguousarray(np.asarray(w_mlp_out, f)[0]),
    }
    g1T = np.asarray(ln1_g, f)[0].reshape(8, 128).T
    b1T = np.asarray(ln1_b, f)[0].reshape(8, 128).T
    cw = np.asarray(conv_w, f)[0]
    cwT = cw.reshape(3, 4, 128).transpose(2, 1, 0).reshape(128, 12)
    psT = np.asarray(pool_scale, f)[0].reshape(4, 128).T
    in_maps = []
    for core in range(NCORES):
        b, half = core // 2, core % 2
        start = half * TOK
        xh = np.zeros((TOK + HALO, D), f)
        xh[HALO:] = x[b, start:start + TOK]
        if half:
            xh[:HALO] = x[b, start - HALO:start]
        vecs = np.zeros((128, NV), f)
        vecs[:, C_C:C_C + 8] = c[b].reshape(8, 128).T
        vecs[:, C_G1:C_G1 + 8] = g1T
        vecs[:, C_B1:C_B1 + 8] = b1T
        vecs[:, C_CW:C_CW + 12] = cwT
        vecs[:, C_PS:C_PS + 4] = psT
        vecs[:, C_FLAG] = 1.0 if half else 0.0
        for g, win in enumerate(WINS):
            for t in range(16):
                vecs[:, C_INV + 16 * g + t] = (1.0 / win) if half else (1.0 / min(t + 1, win))
        m = dict(shared)
        m["xh"] = xh
        m["vecs"] = vecs
        in_maps.append(m)
    return in_maps


def kernel(**inputs):
    if "nc" not in _CACHE:
        _CACHE["nc"] = build_program()
    nc = _CACHE["nc"]
    in_maps = _layout_inputs(**inputs)
    res = run_bass_kernel_spmd(nc, in_maps, core_ids=list(range(NCORES)))
    outp = np.empty((BATCH, SEQ, D), np.float32)
    for core in range(NCORES):
        b, half = core // 2, core % 2
        outp[b, half * TOK:(half + 1) * TOK] = res.results[core]["out"]
    return outp
```

```python
import numpy as np
from contextlib import ExitStack
import concourse.bass as bass
import concourse.mybir as mybir
from concourse.bass_utils import run_bass_kernel_spmd

F32 = mybir.dt.float32
BF16 = mybir.dt.bfloat16
ALU = mybir.AluOpType
AF = mybir.ActivationFunctionType

D = 1024
SEQ = 8192
BATCH = 4
NCORES = 8
TOK = 4096
HALO = 16
T = 256
NSUB = T // 128
NT = TOK // T
DFF = 4096
ALPHA = 2.0 ** 0.25
EPS = 1e-5
WINS = (2, 4, 8, 16)

C_C, C_G1, C_B1, C_CW, C_PS, C_FLAG, C_INV, NV = 0, 8, 16, 24, 36, 40, 41, 105

ENGS = ("pe", "act", "dve", "pool", "sp")


class _Op:
    __slots__ = ("idx", "eng", "fn", "deps", "dma", "dma_val", "sig", "sig_val")

    def __init__(self, idx, eng, fn, dma):
        self.idx = idx
        self.eng = eng
        self.fn = fn
        self.deps = {}
        self.dma = dma
        self.dma_val = 0
        self.sig = False
        self.sig_val = 0


class Sched:
    def __init__(self):
        self.ops = []
        self.last_w = {}
        self.readers = {}
        self.dma_cnt = {}

    def op(self, eng, fn, r=(), w=(), dma=None):
        o = _Op(len(self.ops), eng, fn, dma)
        for k in r:
            lw = self.last_w.get(k)
            if lw is not None:
                o.deps[lw] = True
        for k in w:
            lw = self.last_w.get(k)
            if lw is not None and lw not in o.deps:
                o.deps[lw] = False
            for rd in self.readers.get(k, ()):
                if rd not in o.deps:
                    o.deps[rd] = False
        for k in r:
            self.readers.setdefault(k, []).append(o.idx)
        for k in w:
            self.last_w[k] = o.idx
            self.readers[k] = []
        if dma is not None:
            c = self.dma_cnt.get(dma, 0) + 1
            self.dma_cnt[dma] = c
            o.dma_val = 16 * c
        self.ops.append(o)
        return o

    def emit(self, nc, final_eng="sp"):
        ops = self.ops
        need = []
        for o in ops:
            lst = []
            for d, is_raw in o.deps.items():
                p = ops[d]
                if p.dma is not None:
                    lst.append(p)
                elif p.eng == o.eng and o.dma is None:
                    if is_raw and o.eng != "pe":
                        lst.append(p)
                else:
                    lst.append(p)
            need.append(lst)
            for p in lst:
                if p.dma is None:
                    p.sig = True
        cnt = {e: 0 for e in ENGS}
        for o in ops:
            if o.sig:
                cnt[o.eng] += 1
                o.sig_val = cnt[o.eng]
        dma_keys = sorted(self.dma_cnt.keys())
        with ExitStack() as st:
            esem = {e: st.enter_context(nc.semaphore("sem_" + e)) for e in ENGS}
            dsem = {k: st.enter_context(nc.semaphore("dsem_" + str(k))) for k in dma_keys}
            block = st.enter_context(nc.Block())
            per_eng = {e: [o for o in ops if o.eng == e] for e in ENGS}

            def run(eng_name, eng):
                waited = {}
                for o in per_eng[eng_name]:
                    for p in need[o.idx]:
                        if p.dma is not None:
                            s, v, key = dsem[p.dma], p.dma_val, ("d", p.dma)
                        else:
                            s, v, key = esem[p.eng], p.sig_val, ("e", p.eng)
                        if waited.get(key, 0) >= v:
                            continue
                        waited[key] = v
                        eng.wait_ge(s, v)
                    ins = o.fn(eng)
                    if o.dma is not None:
                        ins.then_inc(dsem[o.dma], 16)
                    elif o.sig:
                        ins.then_inc(esem[o.eng], 1)
                if eng_name == final_eng:
                    for k in dma_keys:
                        eng.wait_ge(dsem[k], 16 * self.dma_cnt[k])

            @block.tensor
            def _(e):
                run("pe", e)

            @block.scalar
            def _(e):
                run("act", e)

            @block.vector
            def _(e):
                run("dve", e)

            @block.gpsimd
            def _(e):
                run("pool", e)

            @block.sync
            def _(e):
                run("sp", e)


def build_program():
    nc = bass.Bass("TRN2", target_bir_lowering=False)

    def din(name, shape, dt=F32):
        return nc.dram_tensor(name, shape, dt, kind="ExternalInput").ap()

    xh = din("xh", [TOK + HALO, D])
    vecs = din("vecs", [128, NV])
    rows = din("rows", [4, D])
    b_ada = din("b_ada", [6 * D])
    w_ada = din("w_ada", [D, 6 * D])
    w_in = din("w_in", [D, 2048])
    w_pool = din("w_pool", [4, 128, 128])
    w_out = din("w_out", [D, D])
    w1 = din("w1", [D, DFF])
    w2 = din("w2", [DFF, D])
    out = nc.dram_tensor("out", [TOK, D], F32, kind="ExternalOutput").ap()
    w1s = nc.dram_tensor("w1s", [128, 8, DFF], BF16, kind="Internal").ap()
    w2s = nc.dram_tensor("w2s", [128, 32, D], BF16, kind="Internal").ap()

    S = Sched()
    with ExitStack() as st:
        def sb(name, shape, dt=F32):
            return st.enter_context(nc.sbuf_tensor(name, shape, dt))

        ps = [st.enter_context(nc.psum_tensor("ps%d" % k, [128, 512], F32)) for k in range(8)]
        psb = [p.bitcast(BF16) for p in ps]

        vec = sb("vec", [128, NV])
        idf = sb("idf", [128, 128])
        idb = sb("idb", [128, 128], BF16)
        ones_f = sb("ones_f", [128, 128])
        cond = sb("cond", [128, 8])
        condrep = sb("condrep", [128, 8, 128], BF16)
        w_in_sb = sb("w_in_sb", [128, 8, 2048], BF16)
        w_out_sb = sb("w_out_sb", [128, 8, D], BF16)
        w_pool_sb = sb("w_pool_sb", [128, 4, 128], BF16)
        g1p_bc = sb("g1p_bc", [128, D])
        g2p_bc = sb("g2p_bc", [128, D])
        ag_bc = sb("ag_bc", [128, D])
        ab_bc = sb("ab_bc", [128, D])
        ln2g_bc = sb("ln2g_bc", [128, D])
        ln2b_bc = sb("ln2b_bc", [128, D])
        modT = sb("modT", [128, 32])
        GB = sb("GB", [128, 24])
        xt = [sb("xt%d" % i, [128, NSUB, D]) for i in range(2)]
        xbf = sb("xbf", [128, NSUB, D], BF16)
        h1T = sb("h1T", [128, 8, T], BF16)
        h1halo = sb("h1halo", [128, 8, HALO], BF16)
        vc = [sb("vc%d" % i, [128, T]) for i in range(2)]
        uw = [sb("uw%d" % i, [128, T + HALO]) for i in range(2)]
        cv = [sb("cv%d" % i, [128, T]) for i in range(2)]
        vpw = [sb("vpw%d" % i, [128, T + HALO]) for i in range(2)]
        sA = [sb("sA%d" % i, [128, T + HALO]) for i in range(2)]
        sB = [sb("sB%d" % i, [128, T + HALO]) for i in range(2)]
        pbf = [sb("pbf%d" % i, [128, T], BF16) for i in range(2)]
        halo_u = sb("halo_u", [128, 4, HALO])
        halo_vp = sb("halo_vp", [128, 4, HALO])
        tmp16 = sb("tmp16", [128, HALO])
        ycatT = sb("ycatT", [128, 8, T], BF16)
        NR = 3
        rb = [sb("rb%d" % i, [128, D]) for i in range(NR)]
        ub2 = [sb("ub%d" % i, [128, NSUB, D]) for i in range(2)]
        ubf = [sb("ubf%d" % i, [128, D], BF16) for i in range(2)]
        h2T2 = [sb("h2T%d" % i, [128, 8, T], BF16) for i in range(2)]
        NW = 2
        w1r = [sb("w1r%d" % i, [128, 8, 512], BF16) for i in range(NW)]
        w2r = [sb("w2r%d" % i, [128, 4, D], BF16) for i in range(NW)]
        rl = [sb("rl%d" % i, [128, T]) for i in range(2)]
        aT = [sb("aT%d" % i, [128, 4, T], BF16) for i in range(2)]
        stats = [sb("stats%d" % i, [128, 12]) for i in range(2)]
        mv = [sb("mv%d" % i, [128, 8]) for i in range(2)]

        wada_r = [w1r[i] for i in range(2)]
        bada_r = [vc[i] for i in range(2)]
        tmpbc = [cv[i] for i in range(2)]
        rot = [0]

        def rbank():
            k = rot[0]
            rot[0] = (k + 1) % 4
            return k

        S.op("sp", lambda e: e.dma_start(out=vec[:], in_=vecs), w=["vec"], dma="ld_vec")
        S.op("pool", lambda e: e.memset(idf[:], 0.0), w=["idf"])
        S.op("pool", lambda e: e.affine_select(out=idf[:], in_=idf[:], pattern=[[-1, 128]],
                                               compare_op=ALU.not_equal, fill=1.0, base=0,
                                               channel_multiplier=1), r=["idf"], w=["idf"])
        S.op("pool", lambda e: e.tensor_copy(out=idb[:], in_=idf[:]), r=["idf"], w=["idb"])
        S.op("pool", lambda e: e.memset(ones_f[:], 1.0), w=["ones_f"])
        S.op("act", lambda e: e.activation(out=cond[:], in_=vec[:, C_C:C_C + 8], func=AF.Silu),
             r=["vec"], w=["cond"])
        for kc in range(8):
            S.op("dve", lambda e, kc=kc: e.tensor_scalar(out=condrep[:, kc, :], in0=ones_f[:],
                                                         scalar1=cond[:, kc:kc + 1], scalar2=None,
                                                         op0=ALU.mult),
                 r=["ones_f", "cond"], w=["condrep"])

        for i, (dst, name) in enumerate(((ag_bc, "ag_bc"), (ab_bc, "ab_bc"),
                                         (ln2g_bc, "ln2g_bc"), (ln2b_bc, "ln2b_bc"))):
            S.op("sp", lambda e, dst=dst, i=i: e.dma_start(out=dst[:], in_=rows[i].partition_broadcast(128)),
                 w=[name], dma="ld_" + name)
        S.op("act", lambda e: e.mul(out=ag_bc[:], in_=ag_bc[:], mul=ALPHA), r=["ag_bc"], w=["ag_bc"])
        S.op("act", lambda e: e.mul(out=ab_bc[:], in_=ab_bc[:], mul=ALPHA), r=["ab_bc"], w=["ab_bc"])

        def cast_dma(dst, src, wkeys, key):
            S.op("pool", lambda e: e.dma_start(out=dst, in_=src), w=wkeys, dma=key)

        def mod_block(blk):
            slot = blk % 2
            col0 = blk * 256
            vi, off = col0 // D, col0 % D
            cast_dma(wada_r[slot][:, :, 0:256], w_ada[:, col0:col0 + 256].rearrange("(kc p) n -> p kc n", p=128),
                     [("w1r", slot)], "wada%d" % slot)
            S.op("sp", lambda e: e.dma_start(out=bada_r[slot][:],
                                             in_=b_ada[col0:col0 + 256].partition_broadcast(128)),
                 w=[("vc", slot)], dma="bada%d" % slot)
            k = rbank()

            def mm(e):
                for kc in range(8):
                    ins = e.matmul(ps[k][:, 0:256], lhsT=condrep[:, kc, :], rhs=wada_r[slot][:, kc, 0:256],
                                   start=(kc == 0), stop=(kc == 7))
                return ins
            S.op("pe", mm, r=[("w1r", slot), "condrep"], w=[("ps", k)])
            if vi in (2, 5):
                dst, name = (g1p_bc, "g1p_bc") if vi == 2 else (g2p_bc, "g2p_bc")
                S.op("dve", lambda e: e.scalar_tensor_tensor(out=dst[:, off:off + 256], in0=ps[k][:, 0:256],
                                                             scalar=1.0, in1=bada_r[slot][:],
                                                             op0=ALU.add, op1=ALU.add),
                     r=[("ps", k), ("vc", slot)], w=[(name, off)])
            else:
                addc = 1.0 if vi in (1, 4) else 0.0
                S.op("dve", lambda e: e.scalar_tensor_tensor(out=tmpbc[slot][:], in0=ps[k][:, 0:256],
                                                             scalar=addc, in1=bada_r[slot][:],
                                                             op0=ALU.add, op1=ALU.add),
                     r=[("ps", k), ("vc", slot)], w=[("cv", slot)])
                vslot = {0: 0, 1: 1, 3: 2, 4: 3}[vi]

                def mmT(e):
                    for c in range(2):
                        col = vslot * 8 + off // 128 + c
                        ins = e.matmul(ps[7][:, col:col + 1], lhsT=tmpbc[slot][0:1, c * 128:(c + 1) * 128],
                                       rhs=ones_f[0:1, 0:1], start=True, stop=True)
                    return ins
                S.op("pe", mmT, r=[("cv", slot), "ones_f"], w=[("ps", 7)])

        for blk in range(8):
            mod_block(blk)
        for h in range(2):
            cast_dma(w_in_sb[:, 4 * h:4 * h + 4, :],
                     w_in[512 * h:512 * (h + 1), :].rearrange("(kc p) n -> p kc n", p=128),
                     [("w_in", h)], "c_w_in%d" % h)
        cast_dma(w_pool_sb[:], w_pool.rearrange("g c d -> c g d"), ["w_pool"], "c_w_pool")
        cast_dma(w_out_sb[:], w_out.rearrange("(kc p) n -> p kc n", p=128), ["w_out"], "c_w_out")
        for blk in range(8, 24):
            mod_block(blk)
        for b in range(8):
            cast_dma(w1s[:, :, b * 512:(b + 1) * 512],
                     w1[:, b * 512:(b + 1) * 512].rearrange("(kc p) n -> p kc n", p=128),
                     [("w1s", b)], "c_w1_%d" % b)
            cast_dma(w2s[:, 4 * b:4 * b + 4, :],
                     w2[b * 512:(b + 1) * 512, :].rearrange("(c p) n -> p c n", p=128),
                     [("w2s", b)], "c_w2_%d" % b)
        S.op("act", lambda e: e.copy(out=modT[:], in_=ps[7][:, 0:32]), r=[("ps", 7)], w=["modT"])
        S.op("dve", lambda e: e.tensor_tensor(out=GB[:, 0:8], in0=vec[:, C_G1:C_G1 + 8], in1=modT[:, 24:32],
                                              op=ALU.mult), r=["vec", "modT"], w=["G2"])
        S.op("dve", lambda e: e.tensor_tensor(out=GB[:, 16:24], in0=vec[:, C_B1:C_B1 + 8], in1=modT[:, 24:32],
                                              op=ALU.mult), r=["vec", "modT"], w=["B2t"])
        S.op("dve", lambda e: e.tensor_tensor(out=GB[:, 8:16], in0=GB[:, 16:24], in1=modT[:, 16:24],
                                              op=ALU.add), r=["B2t", "modT"], w=["B2"])
        G1PK = [("g1p_bc", o) for o in range(0, D, 256)]
        G2PK = [("g2p_bc", o) for o in range(0, D, 256)]

        def layer_norm_stats(src_ap, skey, par):
            st_, m_ = stats[par], mv[par]
            S.op("dve", lambda e: e.bn_stats(out=st_[:, 0:6], in_=src_ap[:, 0:512]), r=[skey], w=[("st0", par)])
            S.op("dve", lambda e: e.bn_stats(out=st_[:, 6:12], in_=src_ap[:, 512:1024]), r=[skey], w=[("st1", par)])
            S.op("dve", lambda e: e.bn_aggr(out=m_[:, 0:2], in_=st_[:, 0:12]),
                 r=[("st0", par), ("st1", par)], w=[("mv01", par)])
            S.op("dve", lambda e: e.tensor_scalar(out=m_[:, 4:5], in0=m_[:, 1:2], scalar1=EPS, scalar2=None,
                                                  op0=ALU.add), r=[("mv01", par)], w=[("mv4", par)])
            S.op("act", lambda e: e.activation(out=m_[:, 5:6], in_=m_[:, 4:5], func=AF.Sqrt),
                 r=[("mv4", par)], w=[("mv5", par)])
            S.op("dve", lambda e: e.reciprocal(out=m_[:, 2:3], in_=m_[:, 5:6]), r=[("mv5", par)], w=[("mv2", par)])
            S.op("dve", lambda e: e.scalar_tensor_tensor(out=m_[:, 3:4], in0=m_[:, 0:1], scalar=-1.0,
                                                         in1=m_[:, 2:3], op0=ALU.mult, op1=ALU.mult),
                 r=[("mv01", par), ("mv2", par)], w=[("mv3", par)])

        lnc = [0]
        rbc = [0]

        def load_x(i):
            slot = i % 2
            S.op("sp", lambda e: e.dma_start(
                out=xt[slot][:], in_=xh[HALO + i * T:HALO + (i + 1) * T, :].rearrange("(s p) f -> p s f", p=128)),
                w=[("xt", slot, s) for s in range(NSUB)], dma="ld_x%d" % slot)

        load_x(0)

        def make_tile(i):
            xs = i % 2
            tp = i % 2
            ub = ub2[tp]
            h2T = h2T2[tp]
            XP, YP = [], []

            def x_front():
              for s in range(NSUB):
                S.op("act", lambda e, s=s: e.copy(out=xbf[:, s, :], in_=xt[xs][:, s, :]),
                     r=[("xt", xs, s)], w=[("xbf", s)])
              for kq in range(2):
                k = rbank()

                def trx(e, kq=kq, k=k):
                    for kk in range(4):
                        kc = kq * 4 + kk
                        for s in range(NSUB):
                            ins = e.transpose(psb[k][:, kk * T + s * 128:kk * T + (s + 1) * 128],
                                              xbf[:, s, kc * 128:(kc + 1) * 128], idb[:])
                    return ins
                S.op("pe", trx, r=[("xbf", s) for s in range(NSUB)] + ["idb"], w=[("ps", k)])
                for kk in range(4):
                    kc = kq * 4 + kk
                    S.op("dve", lambda e, kk=kk, kc=kc, k=k: e.tensor_scalar(
                        out=h1T[:, kc, :], in0=psb[k][:, kk * T:(kk + 1) * T],
                        scalar1=modT[:, 8 + kc:9 + kc], scalar2=modT[:, kc:kc + 1],
                        op0=ALU.mult, op1=ALU.add),
                        r=[("ps", k), "modT"], w=[("h1T", kc)])
              if i == 0:
                S.op("sp", lambda e: e.dma_start(out=rb[0][0:HALO, :], in_=xh[0:HALO, :]), w=[("rb", 0)], dma="ld_halo")
                S.op("act", lambda e: e.copy(out=ubf[0][0:HALO, :], in_=rb[0][0:HALO, :]), r=[("rb", 0)], w=[("ubf", 0)])
                k = rbank()

                def trh(e, k=k):
                    for kc in range(8):
                        ins = e.transpose(psb[k][:, kc * HALO:(kc + 1) * HALO],
                                          ubf[0][0:HALO, kc * 128:(kc + 1) * 128], idb[0:HALO, 0:HALO])
                    return ins
                S.op("pe", trh, r=[("ubf", 0), "idb"], w=[("ps", k)])
                for kc in range(8):
                    S.op("dve", lambda e, kc=kc, k=k: e.tensor_scalar(
                        out=h1halo[:, kc, :], in0=psb[k][:, kc * HALO:(kc + 1) * HALO],
                        scalar1=modT[:, 8 + kc:9 + kc], scalar2=modT[:, kc:kc + 1],
                        op0=ALU.mult, op1=ALU.add), r=[("ps", k), "modT"], w=[("h1halo", kc)])
                    S.op("dve", lambda e, kc=kc: e.tensor_scalar(
                        out=h1halo[:, kc, :], in0=h1halo[:, kc, :], scalar1=vec[:, C_FLAG:C_FLAG + 1],
                        scalar2=None, op0=ALU.mult), r=[("h1halo", kc), "vec"], w=[("h1halo", kc)])
              if i + 1 < NT:
                load_x(i + 1)
            XP.append(x_front)

            H1K = [("h1T", kc) for kc in range(8)]
            H1HK = [("h1halo", kc) for kc in range(8)]
            WINK = [("w_in", 0), ("w_in", 1)]

            def inproj(col0, halo=False):
                k = rbank()
                n = HALO if halo else T

                def mm(e):
                    for kc in range(8):
                        rhs = h1halo[:, kc, :] if halo else h1T[:, kc, :]
                        ins = e.matmul(ps[k][:, 0:n], lhsT=w_in_sb[:, kc, col0:col0 + 128], rhs=rhs,
                                       start=(kc == 0), stop=(kc == 7))
                    return ins
                S.op("pe", mm, r=WINK + (H1HK if halo else H1K), w=[("ps", k)])
                return k

            def do_chunk(j):
                par = j % 2
                k_vc = inproj(1024 + 128 * j)
                S.op("act", lambda e, k=k_vc: e.copy(out=vc[par][:], in_=ps[k][:, 0:T]),
                     r=[("ps", k_vc)], w=[("vc", par)])
                k_gc = inproj(512 + 128 * j)
                S.op("dve", lambda e, k=k_gc: e.tensor_tensor(out=uw[par][:, HALO:HALO + T], in0=ps[k][:, 0:T],
                                                              in1=vc[par][:], op=ALU.mult),
                     r=[("ps", k_gc), ("vc", par)], w=[("uw", par)])
                if i == 0:
                    k_h = inproj(1024 + 128 * j, halo=True)
                    S.op("act", lambda e, k=k_h: e.copy(out=tmp16[:], in_=ps[k][:, 0:HALO]),
                         r=[("ps", k_h)], w=["tmp16"])
                    k_h2 = inproj(512 + 128 * j, halo=True)
                    S.op("dve", lambda e, k=k_h2: e.tensor_tensor(out=uw[par][:, 0:HALO], in0=ps[k][:, 0:HALO],
                                                                  in1=tmp16[:], op=ALU.mult),
                         r=[("ps", k_h2), "tmp16"], w=[("uwh", par)])
                else:
                    S.op("pool", lambda e, j=j: e.tensor_copy(out=uw[par][:, 0:HALO], in_=halo_u[:, j, :]),
                         r=[("halo_u", j)], w=[("uwh", par)])
                cw = C_CW + 3 * j
                S.op("pool", lambda e, cw=cw: e.tensor_scalar(out=cv[par][:], in0=uw[par][:, HALO:HALO + T],
                                                              scalar1=vec[:, cw + 2:cw + 3], scalar2=None,
                                                              op0=ALU.mult),
                     r=[("uw", par), "vec"], w=[("cv", par)])
                S.op("dve", lambda e, cw=cw: e.scalar_tensor_tensor(out=cv[par][:], in0=uw[par][:, HALO - 1:HALO - 1 + T],
                                                                     scalar=vec[:, cw + 1:cw + 2], in1=cv[par][:],
                                                                     op0=ALU.mult, op1=ALU.add),
                     r=[("uw", par), ("uwh", par), ("cv", par), "vec"], w=[("cv", par)])
                S.op("dve", lambda e, cw=cw: e.scalar_tensor_tensor(out=cv[par][:], in0=uw[par][:, HALO - 2:HALO - 2 + T],
                                                                     scalar=vec[:, cw:cw + 1], in1=cv[par][:],
                                                                     op0=ALU.mult, op1=ALU.add),
                     r=[("uw", par), ("uwh", par), ("cv", par), "vec"], w=[("cv", par)])
                S.op("pool", lambda e, j=j: e.tensor_copy(out=halo_u[:, j, :], in_=uw[par][:, T:T + HALO]),
                     r=[("uw", par)], w=[("halo_u", j)])
                k_gb = inproj(128 * j)
                S.op("dve", lambda e, k=k_gb, j=j: e.tensor_tensor(out=ycatT[:, j, :], in0=ps[k][:, 0:T],
                                                                   in1=cv[par][:], op=ALU.mult),
                     r=[("ps", k_gb), ("cv", par)], w=[("ycatT", j)])
                k_vp = inproj(1536 + 128 * j)
                S.op("act", lambda e, k=k_vp: e.copy(out=vpw[par][:, HALO:HALO + T], in_=ps[k][:, 0:T]),
                     r=[("ps", k_vp)], w=[("vpw", par)])
                if i == 0:
                    k_h3 = inproj(1536 + 128 * j, halo=True)
                    S.op("act", lambda e, k=k_h3: e.copy(out=vpw[par][:, 0:HALO], in_=ps[k][:, 0:HALO]),
                         r=[("ps", k_h3)], w=[("vpwh", par)])
                else:
                    S.op("pool", lambda e, j=j: e.tensor_copy(out=vpw[par][:, 0:HALO], in_=halo_vp[:, j, :]),
                         r=[("halo_vp", j)], w=[("vpwh", par)])
                L = T + HALO
                VK = [("vpw", par), ("vpwh", par)]
                S.op("pool", lambda e: e.tensor_tensor(out=sA[par][:, 1:L], in0=vpw[par][:, 1:L],
                                                       in1=vpw[par][:, 0:L - 1], op=ALU.add),
                     r=VK, w=[("sA", par)])
                cur, curk = sA, "sA"
                if j >= 1:
                    S.op("pool", lambda e: e.tensor_tensor(out=sB[par][:, 3:L], in0=sA[par][:, 3:L],
                                                           in1=sA[par][:, 1:L - 2], op=ALU.add),
                         r=[("sA", par)], w=[("sB", par)])
                    cur, curk = sB, "sB"
                if j >= 2:
                    S.op("pool", lambda e: e.tensor_tensor(out=sA[par][:, 7:L], in0=sB[par][:, 7:L],
                                                           in1=sB[par][:, 3:L - 4], op=ALU.add),
                         r=[("sB", par)], w=[("sA", par)])
                    cur, curk = sA, "sA"
                if j >= 3:
                    S.op("pool", lambda e: e.tensor_tensor(out=sB[par][:, 15:L], in0=sA[par][:, 15:L],
                                                           in1=sA[par][:, 7:L - 8], op=ALU.add),
                         r=[("sA", par)], w=[("sB", par)])
                    cur, curk = sB, "sB"
                win = float(WINS[j])
                S.op("dve", lambda e, cur=cur: e.scalar_tensor_tensor(
                    out=pbf[par][:], in0=cur[par][:, HALO:HALO + T], scalar=1.0 / win,
                    in1=vpw[par][:, HALO:HALO + T], op0=ALU.mult, op1=ALU.subtract),
                    r=[(curk, par), ("vpw", par)], w=[("pbf", par)])
                if i == 0:
                    ic = C_INV + 16 * j
                    S.op("pool", lambda e, cur=cur, ic=ic: e.tensor_tensor(
                        out=tmp16[:], in0=cur[par][:, HALO:2 * HALO], in1=vec[:, ic:ic + 16], op=ALU.mult),
                        r=[(curk, par), "vec"], w=["tmp16"])
                    S.op("pool", lambda e: e.tensor_tensor(
                        out=pbf[par][:, 0:HALO], in0=tmp16[:], in1=vpw[par][:, HALO:2 * HALO], op=ALU.subtract),
                        r=["tmp16", ("vpw", par), ("pbf", par)], w=[("pbf", par)])
                S.op("pool", lambda e, j=j: e.tensor_copy(out=halo_vp[:, j, :], in_=vpw[par][:, T:T + HALO]),
                     r=[("vpw", par)], w=[("halo_vp", j)])
                k_q = rbank()
                S.op("pe", lambda e, k=k_q, j=j: e.matmul(ps[k][:, 0:T], lhsT=w_pool_sb[:, j, :], rhs=pbf[par][:],
                                                          start=True, stop=True),
                     r=["w_pool", ("pbf", par)], w=[("ps", k_q)])
                S.op("act", lambda e, k=k_q, j=j: e.mul(out=ycatT[:, 4 + j, :], in_=ps[k][:, 0:T],
                                                        mul=vec[:, C_PS + j:C_PS + j + 1]),
                     r=[("ps", k_q), "vec"], w=[("ycatT", 4 + j)])

            for j in range(4):
                XP.append(lambda j=j: do_chunk(j))

            YK = [("ycatT", c) for c in range(8)]
            uT_banks = []

            def do_sub1(s):
                ri = rbc[0] % NR
                rbc[0] += 1
                r_ = rb[ri]
                for hf in range(2):
                    ka = rbank()

                    def mmo(e, ka=ka, hf=hf, s=s):
                        for kc in range(8):
                            ins = e.matmul(ps[ka][:, :], lhsT=ycatT[:, kc, s * 128:(s + 1) * 128],
                                           rhs=w_out_sb[:, kc, hf * 512:(hf + 1) * 512],
                                           start=(kc == 0), stop=(kc == 7))
                        return ins
                    S.op("pe", mmo, r=YK + ["w_out"], w=[("ps", ka)])
                    S.op("dve", lambda e, ka=ka, hf=hf, r_=r_: e.tensor_tensor(
                        out=r_[:, hf * 512:(hf + 1) * 512], in0=ps[ka][:, :], in1=g1p_bc[:, hf * 512:(hf + 1) * 512],
                        op=ALU.mult), r=[("ps", ka)] + G1PK, w=[("rb", ri, hf), ("rb", ri)])
                S.op("dve", lambda e, r_=r_, s=s: e.scalar_tensor_tensor(
                    out=r_[:], in0=xt[xs][:, s, :], scalar=ALPHA, in1=r_[:], op0=ALU.mult, op1=ALU.add),
                    r=[("xt", xs, s), ("rb", ri, 0), ("rb", ri, 1)], w=[("rb", ri)])
                par = lnc[0] % 2
                lnc[0] += 1
                layer_norm_stats(r_, ("rb", ri), par)
                m_ = mv[par]
                MK = [("mv2", par), ("mv3", par)]
                S.op("act", lambda e, r_=r_, m_=m_, s=s: e.activation(out=ub[:, s, :], in_=r_[:], func=AF.Identity,
                                                                     bias=m_[:, 3:4], scale=m_[:, 2:3]),
                     r=[("rb", ri)] + MK, w=[("ub", tp, s)])
                up = s % 2
                S.op("act", lambda e, r_=r_, m_=m_, up=up: e.activation(out=ubf[up][:], in_=r_[:], func=AF.Identity,
                                                                       bias=m_[:, 3:4], scale=m_[:, 2:3]),
                     r=[("rb", ri)] + MK, w=[("ubf", up)])
                S.op("pool", lambda e, s=s: e.tensor_tensor(out=ub[:, s, :], in0=ub[:, s, :], in1=ag_bc[:], op=ALU.mult),
                     r=[("ub", tp, s), "ag_bc"], w=[("ub", tp, s)])
                S.op("pool", lambda e, s=s: e.tensor_tensor(out=ub[:, s, :], in0=ub[:, s, :], in1=ab_bc[:], op=ALU.add),
                     r=[("ub", tp, s), "ab_bc"], w=[("ub", tp, s)])
                for kq in range(2):
                    k = rbank()

                    def tru(e, kq=kq, k=k, up=up, s=s):
                        for kk in range(4):
                            kc = kq * 4 + kk
                            ins = e.transpose(psb[k][:, kk * 128:(kk + 1) * 128],
                                              ubf[up][:, kc * 128:(kc + 1) * 128], idb[:])
                        return ins
                    S.op("pe", tru, r=[("ubf", up), "idb"], w=[("ps", k)])
                    for kk in range(4):
                        kc = kq * 4 + kk
                        S.op("dve", lambda e, kk=kk, kc=kc, k=k, s=s: e.tensor_scalar(
                            out=h2T[:, kc, s * 128:(s + 1) * 128], in0=psb[k][:, kk * 128:(kk + 1) * 128],
                            scalar1=GB[:, kc:kc + 1], scalar2=GB[:, 8 + kc:9 + kc], op0=ALU.mult, op1=ALU.add),
                            r=[("ps", k), "G2", "B2"], w=[("h2T", tp, kc, s)])
            for s in range(NSUB):
                XP.append(lambda s=s: do_sub1(s))


            H2K = [("h2T", tp, kc, s) for kc in range(8) for s in range(NSUB)]

            def load_w(b):
                slot = (i * 8 + b) % NW
                S.op("sp", lambda e: e.dma_start(out=w1r[slot][:], in_=w1s[:, :, b * 512:(b + 1) * 512]),
                     r=[("w1s", b)], w=[("w1r", slot)], dma="ld_w1_%d" % slot)
                S.op("sp", lambda e: e.dma_start(out=w2r[slot][:], in_=w2s[:, 4 * b:4 * b + 4, :]),
                     r=[("w2s", b)], w=[("w2r", slot)], dma="ld_w2_%d" % slot)

            def stage_a(b):
                slot = (i * 8 + b) % NW
                ap_ = b % 2
                for c in range(4):
                    k = rbank()
                    rp = c % 2

                    def mma(e, k=k, c=c):
                        for kc in range(8):
                            ins = e.matmul(ps[k][:, 0:T], lhsT=w1r[slot][:, kc, c * 128:(c + 1) * 128],
                                           rhs=h2T[:, kc, :], start=(kc == 0), stop=(kc == 7))
                        return ins
                    S.op("pe", mma, r=[("w1r", slot)] + H2K, w=[("ps", k)])
                    S.op("act", lambda e, k=k, rp=rp: e.activation(out=rl[rp][:], in_=ps[k][:, 0:T], func=AF.Relu),
                         r=[("ps", k)], w=[("rl", rp)])
                    S.op("dve", lambda e, rp=rp, c=c: e.tensor_tensor(out=aT[ap_][:, c, :], in0=rl[rp][:], in1=rl[rp][:],
                                                                      op=ALU.mult),
                         r=[("rl", rp)], w=[("aT", ap_, c)])

            def stage_b(b):
                slot = (i * 8 + b) % NW
                ap_ = b % 2
                for s in range(NSUB):
                    for hf in range(2):
                        ka = 4 + (2 * s + hf) % 4

                        def mmb(e, ka=ka, s=s, hf=hf):
                            for c in range(4):
                                ins = e.matmul(ps[ka][:, :], lhsT=aT[ap_][:, c, s * 128:(s + 1) * 128],
                                               rhs=w2r[slot][:, c, hf * 512:(hf + 1) * 512],
                                               start=(b == 0 and c == 0), stop=(b == 7 and c == 3))
                            return ins
                        S.op("pe", mmb, r=[("w2r", slot)] + [("aT", ap_, c) for c in range(4)], w=[("ps", ka)])

            def y_first():
                load_w(0)
                stage_a(0)
            YP.append(y_first)

            def y_mid(b):
                load_w(b)
                stage_a(b)
                stage_b(b - 1)
            for b in range(1, 8):
                YP.append(lambda b=b: y_mid(b))
            YP.append(lambda: stage_b(7))

            def do_sub2(s):
                ri = rbc[0] % NR
                rbc[0] += 1
                r_ = rb[ri]
                for hf in range(2):
                    ka = 4 + (2 * s + hf) % 4
                    S.op("dve", lambda e, ka=ka, hf=hf, r_=r_: e.tensor_tensor(
                        out=r_[:, hf * 512:(hf + 1) * 512], in0=ps[ka][:, :], in1=g2p_bc[:, hf * 512:(hf + 1) * 512],
                        op=ALU.mult), r=[("ps", ka)] + G2PK, w=[("rb", ri, hf), ("rb", ri)])
                S.op("dve", lambda e, r_=r_, s=s: e.tensor_tensor(out=r_[:], in0=r_[:], in1=ub[:, s, :], op=ALU.add),
                     r=[("ub", tp, s), ("rb", ri, 0), ("rb", ri, 1)], w=[("rb", ri)])
                par = lnc[0] % 2
                lnc[0] += 1
                layer_norm_stats(r_, ("rb", ri), par)
                m_ = mv[par]
                MK = [("mv2", par), ("mv3", par)]
                S.op("act", lambda e, r_=r_, m_=m_: e.activation(out=r_[:], in_=r_[:], func=AF.Identity,
                                                                bias=m_[:, 3:4], scale=m_[:, 2:3]),
                     r=[("rb", ri)] + MK, w=[("rb", ri)])
                S.op("pool", lambda e, r_=r_: e.tensor_tensor(out=r_[:], in0=r_[:], in1=ln2g_bc[:], op=ALU.mult),
                     r=[("rb", ri), "ln2g_bc"], w=[("rb", ri)])
                S.op("pool", lambda e, r_=r_: e.tensor_tensor(out=r_[:], in0=r_[:], in1=ln2b_bc[:], op=ALU.add),
                     r=[("rb", ri), "ln2b_bc"], w=[("rb", ri)])
                row0 = i * T + s * 128
                S.op("pool", lambda e, r_=r_, row0=row0: e.dma_start(out=out[row0:row0 + 128, :], in_=r_[:]),
                     r=[("rb", ri)], w=[("out", row0)], dma="st_%d" % ri)

            def y_tail():
                for s in range(NSUB):
                    do_sub2(s)
            YP.append(y_tail)
            return XP, YP

        tiles = [make_tile(i) for i in range(NT)]
        for rnd in range(NT + 1):
            XP = tiles[rnd][0] if rnd < NT else []
            YP = tiles[rnd - 1][1] if rnd >= 1 else []
            n = max(len(XP), len(YP))
            for q in range(n):
                if q < len(YP):
                    YP[q]()
                if q < len(XP):
                    XP[q]()

        S.emit(nc)
    return nc


_CACHE = {}


def _layout_inputs(x, c, w_ada, b_ada, w_in, conv_w, w_pool, pool_scale, w_out, ln1_g, ln1_b,
                   w_mlp_in, w_mlp_out, ln2_g, ln2_b):
    f = np.float32
    x = np.asarray(x, f)
    c = np.asarray(c, f)
    shared = {
        "rows": np.ascontiguousarray(np.stack([np.asarray(ln1_g, f)[0], np.asarray(ln1_b, f)[0],
                                               np.asarray(ln2_g, f)[0], np.asarray(ln2_b, f)[0]])),
        "b_ada": np.ascontiguousarray(np.asarray(b_ada, f)[0]),
        "w_ada": np.ascontiguousarray(np.asarray(w_ada, f)[0]),
        "w_in": np.ascontiguousarray(np.asarray(w_in, f)[0]),
        "w_pool": np.ascontiguousarray(np.asarray(w_pool, f)[0]),
        "w_out": np.ascontiguousarray(np.asarray(w_out, f)[0]),
        "w1": np.ascontiguousarray(np.asarray(w_mlp_in, f)[0]),
        "w2": np.ascontiguousarray(np.asarray(w_mlp_out, f)[0]),
    }
    g1T = np.asarray(ln1_g, f)[0].reshape(8, 128).T
    b1T = np.asarray(ln1_b, f)[0].reshape(8, 128).T
    cw = np.asarray(conv_w, f)[0]
    cwT = cw.reshape(3, 4, 128).transpose(2, 1, 0).reshape(128, 12)
    psT = np.asarray(pool_scale, f)[0].reshape(4, 128).T
    in_maps = []
    for core in range(NCORES):
        b, half = core // 2, core % 2
        start = half * TOK
        xh = np.zeros((TOK + HALO, D), f)
        xh[HALO:] = x[b, start:start + TOK]
        if half:
            xh[:HALO] = x[b, start - HALO:start]
        vecs = np.zeros((128, NV), f)
        vecs[:, C_C:C_C + 8] = c[b].reshape(8, 128).T
        vecs[:, C_G1:C_G1 + 8] = g1T
        vecs[:, C_B1:C_B1 + 8] = b1T
        vecs[:, C_CW:C_CW + 12] = cwT
        vecs[:, C_PS:C_PS + 4] = psT
        vecs[:, C_FLAG] = 1.0 if half else 0.0
        for g, win in enumerate(WINS):
            for t in range(16):
                vecs[:, C_INV + 16 * g + t] = (1.0 / win) if half else (1.0 / min(t + 1, win))
        m = dict(shared)
        m["xh"] = xh
        m["vecs"] = vecs
        in_maps.append(m)
    return in_maps


def kernel(**inputs):
    if "nc" not in _CACHE:
        _CACHE["nc"] = build_program()
    nc = _CACHE["nc"]
    in_maps = _layout_inputs(**inputs)
    res = run_bass_kernel_spmd(nc, in_maps, core_ids=list(range(NCORES)))
    outp = np.empty((BATCH, SEQ, D), np.float32)
    for core in range(NCORES):
        b, half = core // 2, core % 2
        outp[b, half * TOK:(half + 1) * TOK] = res.results[core]["out"]
    return outp
```

```python
import numpy as np
from contextlib import ExitStack
import concourse.bass as bass
import concourse.mybir as mybir
from concourse.bass_utils import run_bass_kernel_spmd

F32 = mybir.dt.float32
BF16 = mybir.dt.bfloat16
ALU = mybir.AluOpType
AF = mybir.ActivationFunctionType

D = 1024
SEQ = 8192
BATCH = 4
NCORES = 8
TOK = 4096
HALO = 16
T = 256
NSUB = T // 128
NT = TOK // T
DFF = 4096
ALPHA = 2.0 ** 0.25
EPS = 1e-5
WINS = (2, 4, 8, 16)

C_C, C_G1, C_B1, C_CW, C_PS, C_FLAG, C_INV, NV = 0, 8, 16, 24, 36, 40, 41, 105

ENGS = ("pe", "act", "dve", "pool", "sp")


class _Op:
    __slots__ = ("idx", "eng", "fn", "deps", "dma", "dma_val", "sig", "sig_val")

    def __init__(self, idx, eng, fn, dma):
        self.idx = idx
        self.eng = eng
        self.fn = fn
        self.deps = {}
        self.dma = dma
        self.dma_val = 0
        self.sig = False
        self.sig_val = 0


class Sched:
    def __init__(self):
        self.ops = []
        self.last_w = {}
        self.readers = {}
        self.dma_cnt = {}

    def op(self, eng, fn, r=(), w=(), dma=None):
        o = _Op(len(self.ops), eng, fn, dma)
        for k in r:
            lw = self.last_w.get(k)
            if lw is not None:
                o.deps[lw] = True
        for k in w:
            lw = self.last_w.get(k)
            if lw is not None and lw not in o.deps:
                o.deps[lw] = False
            for rd in self.readers.get(k, ()):
                if rd not in o.deps:
                    o.deps[rd] = False
        for k in r:
            self.readers.setdefault(k, []).append(o.idx)
        for k in w:
            self.last_w[k] = o.idx
            self.readers[k] = []
        if dma is not None:
            c = self.dma_cnt.get(dma, 0) + 1
            self.dma_cnt[dma] = c
            o.dma_val = 16 * c
        self.ops.append(o)
        return o

    def emit(self, nc, final_eng="sp"):
        ops = self.ops
        need = []
        for o in ops:
            lst = []
            for d, is_raw in o.deps.items():
                p = ops[d]
                if p.dma is not None:
                    lst.append(p)
                elif p.eng == o.eng and o.dma is None:
                    if is_raw and o.eng != "pe":
                        lst.append(p)
                else:
                    lst.append(p)
            need.append(lst)
            for p in lst:
                if p.dma is None:
                    p.sig = True
        cnt = {e: 0 for e in ENGS}
        for o in ops:
            if o.sig:
                cnt[o.eng] += 1
                o.sig_val = cnt[o.eng]
        dma_keys = sorted(self.dma_cnt.keys())
        with ExitStack() as st:
            esem = {e: st.enter_context(nc.semaphore("sem_" + e)) for e in ENGS}
            dsem = {k: st.enter_context(nc.semaphore("dsem_" + str(k))) for k in dma_keys}
            block = st.enter_context(nc.Block())
            per_eng = {e: [o for o in ops if o.eng == e] for e in ENGS}

            def run(eng_name, eng):
                waited = {}
                for o in per_eng[eng_name]:
                    for p in need[o.idx]:
                        if p.dma is not None:
                            s, v, key = dsem[p.dma], p.dma_val, ("d", p.dma)
                        else:
                            s, v, key = esem[p.eng], p.sig_val, ("e", p.eng)
                        if waited.get(key, 0) >= v:
                            continue
                        waited[key] = v
                        eng.wait_ge(s, v)
                    ins = o.fn(eng)
                    if o.dma is not None:
                        ins.then_inc(dsem[o.dma], 16)
                    elif o.sig:
                        ins.then_inc(esem[o.eng], 1)
                if eng_name == final_eng:
                    for k in dma_keys:
                        eng.wait_ge(dsem[k], 16 * self.dma_cnt[k])

            @block.tensor
            def _(e):
                run("pe", e)

            @block.scalar
            def _(e):
                run("act", e)

            @block.vector
            def _(e):
                run("dve", e)

            @block.gpsimd
            def _(e):
                run("pool", e)

            @block.sync
            def _(e):
                run("sp", e)


def build_program():
    nc = bass.Bass("TRN2", target_bir_lowering=False)

    def din(name, shape, dt=F32):
        return nc.dram_tensor(name, shape, dt, kind="ExternalInput").ap()

    xh = din("xh", [TOK + HALO, D])
    vecs = din("vecs", [128, NV])
    rows = din("rows", [4, D])
    b_ada = din("b_ada", [6 * D])
    w_ada = din("w_ada", [D, 6 * D])
    w_in = din("w_in", [D, 2048])
    w_pool = din("w_pool", [4, 128, 128])
    w_out = din("w_out", [D, D])
    w1 = din("w1", [D, DFF])
    w2 = din("w2", [DFF, D])
    out = nc.dram_tensor("out", [TOK, D], F32, kind="ExternalOutput").ap()
    w1s = nc.dram_tensor("w1s", [128, 8, DFF], BF16, kind="Internal").ap()
    w2s = nc.dram_tensor("w2s", [128, 32, D], BF16, kind="Internal").ap()

    S = Sched()
    with ExitStack() as st:
        def sb(name, shape, dt=F32):
            return st.enter_context(nc.sbuf_tensor(name, shape, dt))

        ps = [st.enter_context(nc.psum_tensor("ps%d" % k, [128, 512], F32)) for k in range(8)]
        psb = [p.bitcast(BF16) for p in ps]

        vec = sb("vec", [128, NV])
        idf = sb("idf", [128, 128])
        idb = sb("idb", [128, 128], BF16)
        ones_f = sb("ones_f", [128, 128])
        cond = sb("cond", [128, 8])
        condrep = sb("condrep", [128, 8, 128], BF16)
        w_in_sb = sb("w_in_sb", [128, 8, 2048], BF16)
        w_out_sb = sb("w_out_sb", [128, 8, D], BF16)
        w_pool_sb = sb("w_pool_sb", [128, 4, 128], BF16)
        g1p_bc = sb("g1p_bc", [128, D])
        g2p_bc = sb("g2p_bc", [128, D])
        ag_bc = sb("ag_bc", [128, D])
        ab_bc = sb("ab_bc", [128, D])
        ln2g_bc = sb("ln2g_bc", [128, D])
        ln2b_bc = sb("ln2b_bc", [128, D])
        modT = sb("modT", [128, 32])
        GB = sb("GB", [128, 24])
        xt = [sb("xt%d" % i, [128, NSUB, D]) for i in range(2)]
        xbf = sb("xbf", [128, NSUB, D], BF16)
        h1T = sb("h1T", [128, 8, T], BF16)
        h1halo = sb("h1halo", [128, 8, HALO], BF16)
        vc = [sb("vc%d" % i, [128, T]) for i in range(2)]
        uw = [sb("uw%d" % i, [128, T + HALO]) for i in range(4)]
        cv = [sb("cv%d" % i, [128, T]) for i in range(2)]
        vpw = [sb("vpw%d" % i, [128, T + HALO]) for i in range(4)]
        sA = [sb("sA%d" % i, [128, T + HALO]) for i in range(2)]
        sB = [sb("sB%d" % i, [128, T + HALO]) for i in range(2)]
        pbf = [sb("pbf%d" % i, [128, T], BF16) for i in range(2)]
        tmp16 = sb("tmp16", [128, HALO])
        ycatT = sb("ycatT", [128, 8, T], BF16)
        NR = 3
        rb = [sb("rb%d" % i, [128, D]) for i in range(NR)]
        ub2 = [sb("ub%d" % i, [128, NSUB, D]) for i in range(2)]
        ubf = [sb("ubf%d" % i, [128, D], BF16) for i in range(2)]
        h2T2 = [sb("h2T%d" % i, [128, 8, T], BF16) for i in range(2)]
        NW = 2
        w1r = [sb("w1r%d" % i, [128, 8, 512], BF16) for i in range(NW)]
        w2r = [sb("w2r%d" % i, [128, 4, D], BF16) for i in range(NW)]
        rl = [sb("rl%d" % i, [128, T]) for i in range(2)]
        aT = [sb("aT%d" % i, [128, 4, T], BF16) for i in range(2)]
        stats = [sb("stats%d" % i, [128, 12]) for i in range(2)]
        mv = [sb("mv%d" % i, [128, 8]) for i in range(2)]

        wada_r = [w1r[i] for i in range(2)]
        bada_r = [vc[i] for i in range(2)]
        tmpbc = [cv[i] for i in range(2)]
        rot = [0]

        def rbank():
            k = rot[0]
            rot[0] = (k + 1) % 4
            return k

        S.op("sp", lambda e: e.dma_start(out=vec[:], in_=vecs), w=["vec"], dma="ld_vec")
        S.op("pool", lambda e: e.memset(idf[:], 0.0), w=["idf"])
        S.op("pool", lambda e: e.affine_select(out=idf[:], in_=idf[:], pattern=[[-1, 128]],
                                               compare_op=ALU.not_equal, fill=1.0, base=0,
                                               channel_multiplier=1), r=["idf"], w=["idf"])
        S.op("pool", lambda e: e.tensor_copy(out=idb[:], in_=idf[:]), r=["idf"], w=["idb"])
        S.op("pool", lambda e: e.memset(ones_f[:], 1.0), w=["ones_f"])
        S.op("act", lambda e: e.activation(out=cond[:], in_=vec[:, C_C:C_C + 8], func=AF.Silu),
             r=["vec"], w=["cond"])
        for kc in range(8):
            S.op("dve", lambda e, kc=kc: e.tensor_scalar(out=condrep[:, kc, :], in0=ones_f[:],
                                                         scalar1=cond[:, kc:kc + 1], scalar2=None,
                                                         op0=ALU.mult),
                 r=["ones_f", "cond"], w=["condrep"])

        for i, (dst, name) in enumerate(((ag_bc, "ag_bc"), (ab_bc, "ab_bc"),
                                         (ln2g_bc, "ln2g_bc"), (ln2b_bc, "ln2b_bc"))):
            S.op("sp", lambda e, dst=dst, i=i: e.dma_start(out=dst[:], in_=rows[i].partition_broadcast(128)),
                 w=[name], dma="ld_" + name)
        S.op("act", lambda e: e.mul(out=ag_bc[:], in_=ag_bc[:], mul=ALPHA), r=["ag_bc"], w=["ag_bc"])
        S.op("act", lambda e: e.mul(out=ab_bc[:], in_=ab_bc[:], mul=ALPHA), r=["ab_bc"], w=["ab_bc"])

        def cast_dma(dst, src, wkeys, key):
            S.op("pool", lambda e: e.dma_start(out=dst, in_=src), w=wkeys, dma=key)

        def mod_block(blk):
            slot = blk % 2
            col0 = blk * 256
            vi, off = col0 // D, col0 % D
            cast_dma(wada_r[slot][:, :, 0:256], w_ada[:, col0:col0 + 256].rearrange("(kc p) n -> p kc n", p=128),
                     [("w1r", slot)], "wada%d" % slot)
            S.op("sp", lambda e: e.dma_start(out=bada_r[slot][:],
                                             in_=b_ada[col0:col0 + 256].partition_broadcast(128)),
                 w=[("vc", slot)], dma="bada%d" % slot)
            k = rbank()

            def mm(e):
                for kc in range(8):
                    ins = e.matmul(ps[k][:, 0:256], lhsT=condrep[:, kc, :], rhs=wada_r[slot][:, kc, 0:256],
                                   start=(kc == 0), stop=(kc == 7))
                return ins
            S.op("pe", mm, r=[("w1r", slot), "condrep"], w=[("ps", k)])
            if vi in (2, 5):
                dst, name = (g1p_bc, "g1p_bc") if vi == 2 else (g2p_bc, "g2p_bc")
                S.op("dve", lambda e: e.scalar_tensor_tensor(out=dst[:, off:off + 256], in0=ps[k][:, 0:256],
                                                             scalar=1.0, in1=bada_r[slot][:],
                                                             op0=ALU.add, op1=ALU.add),
                     r=[("ps", k), ("vc", slot)], w=[(name, off)])
            else:
                addc = 1.0 if vi in (1, 4) else 0.0
                S.op("dve", lambda e: e.scalar_tensor_tensor(out=tmpbc[slot][:], in0=ps[k][:, 0:256],
                                                             scalar=addc, in1=bada_r[slot][:],
                                                             op0=ALU.add, op1=ALU.add),
                     r=[("ps", k), ("vc", slot)], w=[("cv", slot)])
                vslot = {0: 0, 1: 1, 3: 2, 4: 3}[vi]

                def mmT(e):
                    for c in range(2):
                        col = vslot * 8 + off // 128 + c
                        ins = e.matmul(ps[7][:, col:col + 1], lhsT=tmpbc[slot][0:1, c * 128:(c + 1) * 128],
                                       rhs=ones_f[0:1, 0:1], start=True, stop=True)
                    return ins
                S.op("pe", mmT, r=[("cv", slot), "ones_f"], w=[("ps", 7)])

        for blk in range(8):
            mod_block(blk)
        for h in range(2):
            cast_dma(w_in_sb[:, 4 * h:4 * h + 4, :],
                     w_in[512 * h:512 * (h + 1), :].rearrange("(kc p) n -> p kc n", p=128),
                     [("w_in", h)], "c_w_in%d" % h)
        cast_dma(w_pool_sb[:], w_pool.rearrange("g c d -> c g d"), ["w_pool"], "c_w_pool")
        cast_dma(w_out_sb[:], w_out.rearrange("(kc p) n -> p kc n", p=128), ["w_out"], "c_w_out")
        for blk in range(8, 24):
            mod_block(blk)
        for b in range(8):
            cast_dma(w1s[:, :, b * 512:(b + 1) * 512],
                     w1[:, b * 512:(b + 1) * 512].rearrange("(kc p) n -> p kc n", p=128),
                     [("w1s", b)], "c_w1_%d" % b)
            cast_dma(w2s[:, 4 * b:4 * b + 4, :],
                     w2[b * 512:(b + 1) * 512, :].rearrange("(c p) n -> p c n", p=128),
                     [("w2s", b)], "c_w2_%d" % b)
        S.op("act", lambda e: e.copy(out=modT[:], in_=ps[7][:, 0:32]), r=[("ps", 7)], w=["modT"])
        S.op("dve", lambda e: e.tensor_tensor(out=GB[:, 0:8], in0=vec[:, C_G1:C_G1 + 8], in1=modT[:, 24:32],
                                              op=ALU.mult), r=["vec", "modT"], w=["G2"])
        S.op("dve", lambda e: e.tensor_tensor(out=GB[:, 16:24], in0=vec[:, C_B1:C_B1 + 8], in1=modT[:, 24:32],
                                              op=ALU.mult), r=["vec", "modT"], w=["B2t"])
        S.op("dve", lambda e: e.tensor_tensor(out=GB[:, 8:16], in0=GB[:, 16:24], in1=modT[:, 16:24],
                                              op=ALU.add), r=["B2t", "modT"], w=["B2"])
        G1PK = [("g1p_bc", o) for o in range(0, D, 256)]
        G2PK = [("g2p_bc", o) for o in range(0, D, 256)]

        def layer_norm_stats(src_ap, skey, par):
            st_, m_ = stats[par], mv[par]
            S.op("dve", lambda e: e.bn_stats(out=st_[:, 0:6], in_=src_ap[:, 0:512]), r=[skey], w=[("st0", par)])
            S.op("dve", lambda e: e.bn_stats(out=st_[:, 6:12], in_=src_ap[:, 512:1024]), r=[skey], w=[("st1", par)])
            S.op("dve", lambda e: e.bn_aggr(out=m_[:, 0:2], in_=st_[:, 0:12]),
                 r=[("st0", par), ("st1", par)], w=[("mv01", par)])
            S.op("dve", lambda e: e.tensor_scalar(out=m_[:, 4:5], in0=m_[:, 1:2], scalar1=EPS, scalar2=None,
                                                  op0=ALU.add), r=[("mv01", par)], w=[("mv4", par)])
            S.op("act", lambda e: e.activation(out=m_[:, 5:6], in_=m_[:, 4:5], func=AF.Sqrt),
                 r=[("mv4", par)], w=[("mv5", par)])
            S.op("dve", lambda e: e.reciprocal(out=m_[:, 2:3], in_=m_[:, 5:6]), r=[("mv5", par)], w=[("mv2", par)])
            S.op("dve", lambda e: e.scalar_tensor_tensor(out=m_[:, 3:4], in0=m_[:, 0:1], scalar=-1.0,
                                                         in1=m_[:, 2:3], op0=ALU.mult, op1=ALU.mult),
                 r=[("mv01", par), ("mv2", par)], w=[("mv3", par)])

        lnc = [0]
        rbc = [0]

        def load_x(i):
            slot = i % 2
            S.op("sp", lambda e: e.dma_start(
                out=xt[slot][:], in_=xh[HALO + i * T:HALO + (i + 1) * T, :].rearrange("(s p) f -> p s f", p=128)),
                w=[("xt", slot, s) for s in range(NSUB)], dma="ld_x%d" % slot)

        load_x(0)

        def make_tile(i):
            xs = i % 2
            tp = i % 2
            ub = ub2[tp]
            h2T = h2T2[tp]
            XP, YP = [], []

            def x_front():
              for s in range(NSUB):
                S.op("act", lambda e, s=s: e.copy(out=xbf[:, s, :], in_=xt[xs][:, s, :]),
                     r=[("xt", xs, s)], w=[("xbf", s)])
              for kq in range(2):
                k = rbank()

                def trx(e, kq=kq, k=k):
                    for kk in range(4):
                        kc = kq * 4 + kk
                        for s in range(NSUB):
                            ins = e.transpose(psb[k][:, kk * T + s * 128:kk * T + (s + 1) * 128],
                                              xbf[:, s, kc * 128:(kc + 1) * 128], idb[:])
                    return ins
                S.op("pe", trx, r=[("xbf", s) for s in range(NSUB)] + ["idb"], w=[("ps", k)])
                for kk in range(4):
                    kc = kq * 4 + kk
                    S.op("act", lambda e, kk=kk, kc=kc, k=k: e.activation(
                        out=h1T[:, kc, :], in_=psb[k][:, kk * T:(kk + 1) * T], func=AF.Identity,
                        bias=modT[:, kc:kc + 1], scale=modT[:, 8 + kc:9 + kc]),
                        r=[("ps", k), "modT"], w=[("h1T", kc)])
              if i == 0:
                S.op("sp", lambda e: e.dma_start(out=rb[0][0:HALO, :], in_=xh[0:HALO, :]), w=[("rb", 0)], dma="ld_halo")
                S.op("act", lambda e: e.copy(out=ubf[0][0:HALO, :], in_=rb[0][0:HALO, :]), r=[("rb", 0)], w=[("ubf", 0)])
                k = rbank()

                def trh(e, k=k):
                    for kc in range(8):
                        ins = e.transpose(psb[k][:, kc * HALO:(kc + 1) * HALO],
                                          ubf[0][0:HALO, kc * 128:(kc + 1) * 128], idb[0:HALO, 0:HALO])
                    return ins
                S.op("pe", trh, r=[("ubf", 0), "idb"], w=[("ps", k)])
                for kc in range(8):
                    S.op("dve", lambda e, kc=kc, k=k: e.tensor_scalar(
                        out=h1halo[:, kc, :], in0=psb[k][:, kc * HALO:(kc + 1) * HALO],
                        scalar1=modT[:, 8 + kc:9 + kc], scalar2=modT[:, kc:kc + 1],
                        op0=ALU.mult, op1=ALU.add), r=[("ps", k), "modT"], w=[("h1halo", kc)])
                    S.op("dve", lambda e, kc=kc: e.tensor_scalar(
                        out=h1halo[:, kc, :], in0=h1halo[:, kc, :], scalar1=vec[:, C_FLAG:C_FLAG + 1],
                        scalar2=None, op0=ALU.mult), r=[("h1halo", kc), "vec"], w=[("h1halo", kc)])
              if i + 1 < NT:
                load_x(i + 1)
            XP.append(x_front)

            H1K = [("h1T", kc) for kc in range(8)]
            H1HK = [("h1halo", kc) for kc in range(8)]
            WINK = [("w_in", 0), ("w_in", 1)]

            def inproj(col0, halo=False):
                k = rbank()
                n = HALO if halo else T

                def mm(e):
                    for kc in range(8):
                        rhs = h1halo[:, kc, :] if halo else h1T[:, kc, :]
                        ins = e.matmul(ps[k][:, 0:n], lhsT=w_in_sb[:, kc, col0:col0 + 128], rhs=rhs,
                                       start=(kc == 0), stop=(kc == 7))
                    return ins
                S.op("pe", mm, r=WINK + (H1HK if halo else H1K), w=[("ps", k)])
                return k

            def chunk_q(j):
                par = j % 2
                k_q = rbank()
                S.op("pe", lambda e, k=k_q: e.matmul(ps[k][:, 0:T], lhsT=w_pool_sb[:, j, :], rhs=pbf[par][:],
                                                     start=True, stop=True),
                     r=["w_pool", ("pbf", par)], w=[("ps", k_q)])
                S.op("act", lambda e, k=k_q: e.mul(out=ycatT[:, 4 + j, :], in_=ps[k][:, 0:T],
                                                   mul=vec[:, C_PS + j:C_PS + j + 1]),
                     r=[("ps", k_q), "vec"], w=[("ycatT", 4 + j)])

            def do_chunk(j):
                par = j % 2
                if j > 0:
                    chunk_q(j - 1)
                U, V = uw[j], vpw[j]
                if i > 0:
                    S.op("act", lambda e: e.copy(out=U[:, 0:HALO], in_=U[:, T:T + HALO]),
                         r=[("uw", j)], w=[("uwh", j)])
                    S.op("act", lambda e: e.copy(out=V[:, 0:HALO], in_=V[:, T:T + HALO]),
                         r=[("vpw", j)], w=[("vpwh", j)])
                k_vc = inproj(1024 + 128 * j)
                S.op("act", lambda e, k=k_vc: e.copy(out=vc[par][:], in_=ps[k][:, 0:T]),
                     r=[("ps", k_vc)], w=[("vc", par)])
                k_gc = inproj(512 + 128 * j)
                S.op("dve", lambda e, k=k_gc: e.tensor_tensor(out=U[:, HALO:HALO + T], in0=ps[k][:, 0:T],
                                                              in1=vc[par][:], op=ALU.mult),
                     r=[("ps", k_gc), ("vc", par)], w=[("uw", j)])
                if i == 0:
                    k_h = inproj(1024 + 128 * j, halo=True)
                    S.op("act", lambda e, k=k_h: e.copy(out=tmp16[:], in_=ps[k][:, 0:HALO]),
                         r=[("ps", k_h)], w=["tmp16"])
                    k_h2 = inproj(512 + 128 * j, halo=True)
                    S.op("dve", lambda e, k=k_h2: e.tensor_tensor(out=U[:, 0:HALO], in0=ps[k][:, 0:HALO],
                                                                  in1=tmp16[:], op=ALU.mult),
                         r=[("ps", k_h2), "tmp16"], w=[("uwh", j)])
                k_vp = inproj(1536 + 128 * j)
                S.op("act", lambda e, k=k_vp: e.copy(out=V[:, HALO:HALO + T], in_=ps[k][:, 0:T]),
                     r=[("ps", k_vp)], w=[("vpw", j)])
                if i == 0:
                    k_h3 = inproj(1536 + 128 * j, halo=True)
                    S.op("act", lambda e, k=k_h3: e.copy(out=V[:, 0:HALO], in_=ps[k][:, 0:HALO]),
                         r=[("ps", k_h3)], w=[("vpwh", j)])
                cw = C_CW + 3 * j
                S.op("act", lambda e: e.mul(out=cv[par][:], in_=U[:, HALO:HALO + T], mul=vec[:, cw + 2:cw + 3]),
                     r=[("uw", j), "vec"], w=[("cv", par)])
                S.op("dve", lambda e: e.scalar_tensor_tensor(out=cv[par][:], in0=U[:, HALO - 1:HALO - 1 + T],
                                                             scalar=vec[:, cw + 1:cw + 2], in1=cv[par][:],
                                                             op0=ALU.mult, op1=ALU.add),
                     r=[("uw", j), ("uwh", j), ("cv", par), "vec"], w=[("cv", par)])
                S.op("dve", lambda e: e.scalar_tensor_tensor(out=cv[par][:], in0=U[:, HALO - 2:HALO - 2 + T],
                                                             scalar=vec[:, cw:cw + 1], in1=cv[par][:],
                                                             op0=ALU.mult, op1=ALU.add),
                     r=[("uw", j), ("uwh", j), ("cv", par), "vec"], w=[("cv", par)])
                k_gb = inproj(128 * j)
                S.op("dve", lambda e, k=k_gb: e.tensor_tensor(out=ycatT[:, j, :], in0=ps[k][:, 0:T],
                                                              in1=cv[par][:], op=ALU.mult),
                     r=[("ps", k_gb), ("cv", par)], w=[("ycatT", j)])
                L = T + HALO
                VK = [("vpw", j), ("vpwh", j)]
                S.op("dve", lambda e: e.tensor_tensor(out=sA[par][:, 1:L], in0=V[:, 1:L], in1=V[:, 0:L - 1], op=ALU.add),
                     r=VK, w=[("sA", par)])
                cur, curk = sA, "sA"
                if j >= 1:
                    S.op("dve", lambda e: e.tensor_tensor(out=sB[par][:, 3:L], in0=sA[par][:, 3:L],
                                                          in1=sA[par][:, 1:L - 2], op=ALU.add),
                         r=[("sA", par)], w=[("sB", par)])
                    cur, curk = sB, "sB"
                if j >= 2:
                    S.op("dve", lambda e: e.tensor_tensor(out=sA[par][:, 7:L], in0=sB[par][:, 7:L],
                                                          in1=sB[par][:, 3:L - 4], op=ALU.add),
                         r=[("sB", par)], w=[("sA", par)])
                    cur, curk = sA, "sA"
                if j >= 3:
                    S.op("dve", lambda e: e.tensor_tensor(out=sB[par][:, 15:L], in0=sA[par][:, 15:L],
                                                          in1=sA[par][:, 7:L - 8], op=ALU.add),
                         r=[("sA", par)], w=[("sB", par)])
                    cur, curk = sB, "sB"
                win = float(WINS[j])
                S.op("dve", lambda e, cur=cur: e.scalar_tensor_tensor(
                    out=pbf[par][:], in0=cur[par][:, HALO:HALO + T], scalar=1.0 / win,
                    in1=V[:, HALO:HALO + T], op0=ALU.mult, op1=ALU.subtract),
                    r=[(curk, par), ("vpw", j)], w=[("pbf", par)])
                if i == 0:
                    ic = C_INV + 16 * j
                    S.op("dve", lambda e, cur=cur: e.tensor_tensor(
                        out=tmp16[:], in0=cur[par][:, HALO:2 * HALO], in1=vec[:, ic:ic + 16], op=ALU.mult),
                        r=[(curk, par), "vec"], w=["tmp16"])
                    S.op("dve", lambda e: e.tensor_tensor(
                        out=pbf[par][:, 0:HALO], in0=tmp16[:], in1=V[:, HALO:2 * HALO], op=ALU.subtract),
                        r=["tmp16", ("vpw", j), ("pbf", par)], w=[("pbf", par)])

            for j in range(4):
                XP.append(lambda j=j: do_chunk(j))
            XP.append(lambda: chunk_q(3))

            YK = [("ycatT", c) for c in range(8)]
            uT_banks = []

            def do_sub1(s):
                ri = rbc[0] % NR
                rbc[0] += 1
                r_ = rb[ri]
                for hf in range(2):
                    ka = rbank()

                    def mmo(e, ka=ka, hf=hf, s=s):
                        for kc in range(8):
                            ins = e.matmul(ps[ka][:, :], lhsT=ycatT[:, kc, s * 128:(s + 1) * 128],
                                           rhs=w_out_sb[:, kc, hf * 512:(hf + 1) * 512],
                                           start=(kc == 0), stop=(kc == 7))
                        return ins
                    S.op("pe", mmo, r=YK + ["w_out"], w=[("ps", ka)])
                    S.op("dve", lambda e, ka=ka, hf=hf, r_=r_: e.tensor_tensor(
                        out=r_[:, hf * 512:(hf + 1) * 512], in0=ps[ka][:, :], in1=g1p_bc[:, hf * 512:(hf + 1) * 512],
                        op=ALU.mult), r=[("ps", ka)] + G1PK, w=[("rb", ri, hf), ("rb", ri)])
                S.op("dve", lambda e, r_=r_, s=s: e.scalar_tensor_tensor(
                    out=r_[:], in0=xt[xs][:, s, :], scalar=ALPHA, in1=r_[:], op0=ALU.mult, op1=ALU.add),
                    r=[("xt", xs, s), ("rb", ri, 0), ("rb", ri, 1)], w=[("rb", ri)])
                par = lnc[0] % 2
                lnc[0] += 1
                layer_norm_stats(r_, ("rb", ri), par)
                m_ = mv[par]
                MK = [("mv2", par), ("mv3", par)]
                S.op("act", lambda e, r_=r_, m_=m_, s=s: e.activation(out=ub[:, s, :], in_=r_[:], func=AF.Identity,
                                                                     bias=m_[:, 3:4], scale=m_[:, 2:3]),
                     r=[("rb", ri)] + MK, w=[("ub", tp, s)])
                up = s % 2
                S.op("act", lambda e, r_=r_, m_=m_, up=up: e.activation(out=ubf[up][:], in_=r_[:], func=AF.Identity,
                                                                       bias=m_[:, 3:4], scale=m_[:, 2:3]),
                     r=[("rb", ri)] + MK, w=[("ubf", up)])
                S.op("dve", lambda e, s=s: e.tensor_tensor(out=ub[:, s, :], in0=ub[:, s, :], in1=ag_bc[:], op=ALU.mult),
                     r=[("ub", tp, s), "ag_bc"], w=[("ub", tp, s)])
                S.op("dve", lambda e, s=s: e.tensor_tensor(out=ub[:, s, :], in0=ub[:, s, :], in1=ab_bc[:], op=ALU.add),
                     r=[("ub", tp, s), "ab_bc"], w=[("ub", tp, s)])
                for kq in range(2):
                    k = rbank()

                    def tru(e, kq=kq, k=k, up=up, s=s):
                        for kk in range(4):
                            kc = kq * 4 + kk
                            ins = e.transpose(psb[k][:, kk * 128:(kk + 1) * 128],
                                              ubf[up][:, kc * 128:(kc + 1) * 128], idb[:])
                        return ins
                    S.op("pe", tru, r=[("ubf", up), "idb"], w=[("ps", k)])
                    for kk in range(4):
                        kc = kq * 4 + kk
                        S.op("act", lambda e, kk=kk, kc=kc, k=k, s=s: e.activation(
                            out=h2T[:, kc, s * 128:(s + 1) * 128], in_=psb[k][:, kk * 128:(kk + 1) * 128],
                            func=AF.Identity, bias=GB[:, 8 + kc:9 + kc], scale=GB[:, kc:kc + 1]),
                            r=[("ps", k), "G2", "B2"], w=[("h2T", tp, kc, s)])
            for s in range(NSUB):
                XP.append(lambda s=s: do_sub1(s))


            H2K = [("h2T", tp, kc, s) for kc in range(8) for s in range(NSUB)]

            def load_w(b):
                slot = (i * 8 + b) % NW
                S.op("sp", lambda e: e.dma_start(out=w1r[slot][:], in_=w1s[:, :, b * 512:(b + 1) * 512]),
                     r=[("w1s", b)], w=[("w1r", slot)], dma="ld_w1_%d" % slot)
                S.op("sp", lambda e: e.dma_start(out=w2r[slot][:], in_=w2s[:, 4 * b:4 * b + 4, :]),
                     r=[("w2s", b)], w=[("w2r", slot)], dma="ld_w2_%d" % slot)

            def stage_a(b):
                slot = (i * 8 + b) % NW
                ap_ = b % 2
                for c in range(4):
                    k = rbank()
                    rp = c % 2

                    def mma(e, k=k, c=c):
                        for kc in range(8):
                            ins = e.matmul(ps[k][:, 0:T], lhsT=w1r[slot][:, kc, c * 128:(c + 1) * 128],
                                           rhs=h2T[:, kc, :], start=(kc == 0), stop=(kc == 7))
                        return ins
                    S.op("pe", mma, r=[("w1r", slot)] + H2K, w=[("ps", k)])
                    S.op("act", lambda e, k=k, rp=rp: e.activation(out=rl[rp][:], in_=ps[k][:, 0:T], func=AF.Relu),
                         r=[("ps", k)], w=[("rl", rp)])
                    S.op("act", lambda e, rp=rp, c=c: e.activation(out=aT[ap_][:, c, :], in_=rl[rp][:], func=AF.Square),
                         r=[("rl", rp)], w=[("aT", ap_, c)])

            def stage_b(b):
                slot = (i * 8 + b) % NW
                ap_ = b % 2
                for s in range(NSUB):
                    for hf in range(2):
                        ka = 4 + (2 * s + hf) % 4

                        def mmb(e, ka=ka, s=s, hf=hf):
                            for c in range(4):
                                ins = e.matmul(ps[ka][:, :], lhsT=aT[ap_][:, c, s * 128:(s + 1) * 128],
                                               rhs=w2r[slot][:, c, hf * 512:(hf + 1) * 512],
                                               start=(b == 0 and c == 0), stop=(b == 7 and c == 3))
                            return ins
                        S.op("pe", mmb, r=[("w2r", slot)] + [("aT", ap_, c) for c in range(4)], w=[("ps", ka)])

            def y_first():
                load_w(0)
                stage_a(0)
            YP.append(y_first)

            def y_mid(b):
                load_w(b)
                stage_a(b)
                stage_b(b - 1)
            for b in range(1, 8):
                YP.append(lambda b=b: y_mid(b))
            YP.append(lambda: stage_b(7))

            def do_sub2(s):
                ri = rbc[0] % NR
                rbc[0] += 1
                r_ = rb[ri]
                for hf in range(2):
                    ka = 4 + (2 * s + hf) % 4
                    S.op("dve", lambda e, ka=ka, hf=hf, r_=r_: e.tensor_tensor(
                        out=r_[:, hf * 512:(hf + 1) * 512], in0=ps[ka][:, :], in1=g2p_bc[:, hf * 512:(hf + 1) * 512],
                        op=ALU.mult), r=[("ps", ka)] + G2PK, w=[("rb", ri, hf), ("rb", ri)])
                S.op("dve", lambda e, r_=r_, s=s: e.tensor_tensor(out=r_[:], in0=r_[:], in1=ub[:, s, :], op=ALU.add),
                     r=[("ub", tp, s), ("rb", ri, 0), ("rb", ri, 1)], w=[("rb", ri)])
                par = lnc[0] % 2
                lnc[0] += 1
                layer_norm_stats(r_, ("rb", ri), par)
                m_ = mv[par]
                MK = [("mv2", par), ("mv3", par)]
                S.op("act", lambda e, r_=r_, m_=m_: e.activation(out=r_[:], in_=r_[:], func=AF.Identity,
                                                                bias=m_[:, 3:4], scale=m_[:, 2:3]),
                     r=[("rb", ri)] + MK, w=[("rb", ri)])
                S.op("dve", lambda e, r_=r_: e.tensor_tensor(out=r_[:], in0=r_[:], in1=ln2g_bc[:], op=ALU.mult),
                     r=[("rb", ri), "ln2g_bc"], w=[("rb", ri)])
                S.op("dve", lambda e, r_=r_: e.tensor_tensor(out=r_[:], in0=r_[:], in1=ln2b_bc[:], op=ALU.add),
                     r=[("rb", ri), "ln2b_bc"], w=[("rb", ri)])
                row0 = i * T + s * 128
                S.op("pool", lambda e, r_=r_, row0=row0: e.dma_start(out=out[row0:row0 + 128, :], in_=r_[:]),
                     r=[("rb", ri)], w=[("out", row0)], dma="st_%d" % ri)

            def y_tail():
                for s in range(NSUB):
                    do_sub2(s)
            YP.append(y_tail)
            return XP, YP

        tiles = [make_tile(i) for i in range(NT)]
        for rnd in range(NT + 1):
            XP = tiles[rnd][0] if rnd < NT else []
            YP = tiles[rnd - 1][1] if rnd >= 1 else []
            n = max(len(XP), len(YP))
            for q in range(n):
                if q < len(YP):
                    YP[q]()
                if q < len(XP):
                    XP[q]()

        S.emit(nc)
    return nc


_CACHE = {}


def _layout_inputs(x, c, w_ada, b_ada, w_in, conv_w, w_pool, pool_scale, w_out, ln1_g, ln1_b,
                   w_mlp_in, w_mlp_out, ln2_g, ln2_b):
    f = np.float32
    x = np.asarray(x, f)
    c = np.asarray(c, f)
    shared = {
        "rows": np.ascontiguousarray(np.stack([np.asarray(ln1_g, f)[0], np.asarray(ln1_b, f)[0],
                                               np.asarray(ln2_g, f)[0], np.asarray(ln2_b, f)[0]])),
        "b_ada": np.ascontiguousarray(np.asarray(b_ada, f)[0]),
        "w_ada": np.ascontiguousarray(np.asarray(w_ada, f)[0]),
        "w_in": np.ascontiguousarray(np.asarray(w_in, f)[0]),
        "w_pool": np.ascontiguousarray(np.asarray(w_pool, f)[0]),
        "w_out": np.ascontiguousarray(np.asarray(w_out, f)[0]),
        "w1": np.ascontiguousarray(np.asarray(w_mlp_in, f)[0]),
        "w2": np.ascontiguousarray(np.asarray(w_mlp_out, f)[0]),
    }
    g1T = np.asarray(ln1_g, f)[0].reshape(8, 128).T
    b1T = np.asarray(ln1_b, f)[0].reshape(8, 128).T
    cw = np.asarray(conv_w, f)[0]
    cwT = cw.reshape(3, 4, 128).transpose(2, 1, 0).reshape(128, 12)
    psT = np.asarray(pool_scale, f)[0].reshape(4, 128).T
    in_maps = []
    for core in range(NCORES):
        b, half = core // 2, core % 2
        start = half * TOK
        xh = np.zeros((TOK + HALO, D), f)
        xh[HALO:] = x[b, start:start + TOK]
        if half:
            xh[:HALO] = x[b, start - HALO:start]
        vecs = np.zeros((128, NV), f)
        vecs[:, C_C:C_C + 8] = c[b].reshape(8, 128).T
        vecs[:, C_G1:C_G1 + 8] = g1T
        vecs[:, C_B1:C_B1 + 8] = b1T
        vecs[:, C_CW:C_CW + 12] = cwT
        vecs[:, C_PS:C_PS + 4] = psT
        vecs[:, C_FLAG] = 1.0 if half else 0.0
        for g, win in enumerate(WINS):
            for t in range(16):
                vecs[:, C_INV + 16 * g + t] = (1.0 / win) if half else (1.0 / min(t + 1, win))
        m = dict(shared)
        m["xh"] = xh
        m["vecs"] = vecs
        in_maps.append(m)
    return in_maps


def kernel(**inputs):
    if "nc" not in _CACHE:
        _CACHE["nc"] = build_program()
    nc = _CACHE["nc"]
    in_maps = _layout_inputs(**inputs)
    res = run_bass_kernel_spmd(nc, in_maps, core_ids=list(range(NCORES)))
    outp = np.empty((BATCH, SEQ, D), np.float32)
    for core in range(NCORES):
        b, half = core // 2, core % 2
        outp[b, half * TOK:(half + 1) * TOK] = res.results[core]["out"]
    return outp
```

```python
import numpy as np
from contextlib import ExitStack
import concourse.bass as bass
import concourse.mybir as mybir
from concourse.bass_utils import run_bass_kernel_spmd

F32 = mybir.dt.float32
BF16 = mybir.dt.bfloat16
ALU = mybir.AluOpType
AF = mybir.ActivationFunctionType

D = 1024
SEQ = 8192
BATCH = 4
NCORES = 8
TOK = 4096
HALO = 16
T = 256
NSUB = T // 128
NT = TOK // T
DFF = 4096
ALPHA = 2.0 ** 0.25
EPS = 1e-5
WINS = (2, 4, 8, 16)

C_C, C_G1, C_B1, C_CW, C_PS, C_FLAG, C_INV, NV = 0, 8, 16, 24, 36, 40, 41, 105

ENGS = ("pe", "act", "dve", "pool", "sp")


class _Op:
    __slots__ = ("idx", "eng", "fn", "deps", "dma", "dma_val", "sig", "sig_val")

    def __init__(self, idx, eng, fn, dma):
        self.idx = idx
        self.eng = eng
        self.fn = fn
        self.deps = {}
        self.dma = dma
        self.dma_val = 0
        self.sig = False
        self.sig_val = 0


class Sched:
    def __init__(self):
        self.ops = []
        self.last_w = {}
        self.readers = {}
        self.dma_cnt = {}

    def op(self, eng, fn, r=(), w=(), dma=None):
        o = _Op(len(self.ops), eng, fn, dma)
        for k in r:
            lw = self.last_w.get(k)
            if lw is not None:
                o.deps[lw] = True
        for k in w:
            lw = self.last_w.get(k)
            if lw is not None and lw not in o.deps:
                o.deps[lw] = False
            for rd in self.readers.get(k, ()):
                if rd not in o.deps:
                    o.deps[rd] = False
        for k in r:
            self.readers.setdefault(k, []).append(o.idx)
        for k in w:
            self.last_w[k] = o.idx
            self.readers[k] = []
        if dma is not None:
            c = self.dma_cnt.get(dma, 0) + 1
            self.dma_cnt[dma] = c
            o.dma_val = 16 * c
        self.ops.append(o)
        return o

    def emit(self, nc, final_eng="sp"):
        ops = self.ops
        need = []
        for o in ops:
            lst = []
            for d, is_raw in o.deps.items():
                p = ops[d]
                if p.dma is not None:
                    lst.append(p)
                elif p.eng == o.eng and o.dma is None:
                    if is_raw and o.eng != "pe":
                        lst.append(p)
                else:
                    lst.append(p)
            need.append(lst)
            for p in lst:
                if p.dma is None:
                    p.sig = True
        cnt = {e: 0 for e in ENGS}
        for o in ops:
            if o.sig:
                cnt[o.eng] += 1
                o.sig_val = cnt[o.eng]
        dma_keys = sorted(self.dma_cnt.keys())
        with ExitStack() as st:
            esem = {e: st.enter_context(nc.semaphore("sem_" + e)) for e in ENGS}
            dsem = {k: st.enter_context(nc.semaphore("dsem_" + str(k))) for k in dma_keys}
            block = st.enter_context(nc.Block())
            per_eng = {e: [o for o in ops if o.eng == e] for e in ENGS}

            def run(eng_name, eng):
                waited = {}
                for o in per_eng[eng_name]:
                    for p in need[o.idx]:
                        if p.dma is not None:
                            s, v, key = dsem[p.dma], p.dma_val, ("d", p.dma)
                        else:
                            s, v, key = esem[p.eng], p.sig_val, ("e", p.eng)
                        if waited.get(key, 0) >= v:
                            continue
                        waited[key] = v
                        eng.wait_ge(s, v)
                    ins = o.fn(eng)
                    if o.dma is not None:
                        ins.then_inc(dsem[o.dma], 16)
                    elif o.sig:
                        ins.then_inc(esem[o.eng], 1)
                if eng_name == final_eng:
                    for k in dma_keys:
                        eng.wait_ge(dsem[k], 16 * self.dma_cnt[k])

            @block.tensor
            def _(e):
                run("pe", e)

            @block.scalar
            def _(e):
                run("act", e)

            @block.vector
            def _(e):
                run("dve", e)

            @block.gpsimd
            def _(e):
                run("pool", e)

            @block.sync
            def _(e):
                run("sp", e)


def build_program():
    nc = bass.Bass("TRN2", target_bir_lowering=False)

    def din(name, shape, dt=F32):
        return nc.dram_tensor(name, shape, dt, kind="ExternalInput").ap()

    xh = din("xh", [TOK + HALO, D])
    vecs = din("vecs", [128, NV])
    rows = din("rows", [4, D])
    b_ada = din("b_ada", [6 * D])
    w_ada = din("w_ada", [D, 6 * D])
    w_in = din("w_in", [D, 2048])
    w_pool = din("w_pool", [4, 128, 128])
    w_out = din("w_out", [D, D])
    w1 = din("w1", [D, DFF])
    w2 = din("w2", [DFF, D])
    out = nc.dram_tensor("out", [TOK, D], F32, kind="ExternalOutput").ap()
    w1s = nc.dram_tensor("w1s", [128, 8, DFF], BF16, kind="Internal").ap()
    w2s = nc.dram_tensor("w2s", [128, 32, D], BF16, kind="Internal").ap()

    S = Sched()
    with ExitStack() as st:
        def sb(name, shape, dt=F32):
            return st.enter_context(nc.sbuf_tensor(name, shape, dt))

        ps = [st.enter_context(nc.psum_tensor("ps%d" % k, [128, 512], F32)) for k in range(8)]
        psb = [p.bitcast(BF16) for p in ps]

        vec = sb("vec", [128, NV])
        idf = sb("idf", [128, 128])
        idb = sb("idb", [128, 128], BF16)
        ones_f = sb("ones_f", [128, 128])
        cond = sb("cond", [128, 8])
        condrep = sb("condrep", [128, 8, 128], BF16)
        w_in_sb = sb("w_in_sb", [128, 8, 2048], BF16)
        w_out_sb = sb("w_out_sb", [128, 8, D], BF16)
        w_pool_sb = sb("w_pool_sb", [128, 4, 128], BF16)
        g1p_bc = sb("g1p_bc", [128, D])
        g2p_bc = sb("g2p_bc", [128, D])
        ag_bc = sb("ag_bc", [128, D])
        ab_bc = sb("ab_bc", [128, D])
        ln2g_bc = sb("ln2g_bc", [128, D])
        ln2b_bc = sb("ln2b_bc", [128, D])
        modT = sb("modT", [128, 32])
        GB = sb("GB", [128, 24])
        xt = [sb("xt%d" % i, [128, NSUB, D]) for i in range(2)]
        xbf = sb("xbf", [128, NSUB, D], BF16)
        h1T = sb("h1T", [128, 8, T], BF16)
        h1halo = sb("h1halo", [128, 8, HALO], BF16)
        vc = [sb("vc%d" % i, [128, T]) for i in range(2)]
        uw = [sb("uw%d" % i, [128, T + HALO]) for i in range(4)]
        cv = [sb("cv%d" % i, [128, T]) for i in range(2)]
        vpw = [sb("vpw%d" % i, [128, T + HALO]) for i in range(4)]
        sA = [sb("sA%d" % i, [128, T + HALO]) for i in range(2)]
        sB = [sb("sB%d" % i, [128, T + HALO]) for i in range(2)]
        pbf = [sb("pbf%d" % i, [128, T], BF16) for i in range(2)]
        tmp16 = sb("tmp16", [128, HALO])
        ycatT = sb("ycatT", [128, 8, T], BF16)
        NR = 3
        rb = [sb("rb%d" % i, [128, D]) for i in range(NR)]
        ub2 = [sb("ub%d" % i, [128, NSUB, D]) for i in range(2)]
        ubf = [sb("ubf%d" % i, [128, D], BF16) for i in range(2)]
        h2T2 = [sb("h2T%d" % i, [128, 8, T], BF16) for i in range(2)]
        NW = 2
        w1r = [sb("w1r%d" % i, [128, 8, 512], BF16) for i in range(NW)]
        w2r = [sb("w2r%d" % i, [128, 4, D], BF16) for i in range(NW)]
        rl = [sb("rl%d" % i, [128, T]) for i in range(2)]
        aT = [sb("aT%d" % i, [128, 4, T], BF16) for i in range(2)]
        stats = [sb("stats%d" % i, [128, 12]) for i in range(2)]
        mv = [sb("mv%d" % i, [128, 8]) for i in range(2)]

        wada_r = [w1r[i] for i in range(2)]
        bada_r = [vc[i] for i in range(2)]
        tmpbc = [cv[i] for i in range(2)]
        rot = [0]

        def rbank():
            k = rot[0]
            rot[0] = (k + 1) % 4
            return k

        S.op("sp", lambda e: e.dma_start(out=vec[:], in_=vecs), w=["vec"], dma="ld_vec")
        S.op("pool", lambda e: e.memset(idf[:], 0.0), w=["idf"])
        S.op("pool", lambda e: e.affine_select(out=idf[:], in_=idf[:], pattern=[[-1, 128]],
                                               compare_op=ALU.not_equal, fill=1.0, base=0,
                                               channel_multiplier=1), r=["idf"], w=["idf"])
        S.op("pool", lambda e: e.tensor_copy(out=idb[:], in_=idf[:]), r=["idf"], w=["idb"])
        S.op("pool", lambda e: e.memset(ones_f[:], 1.0), w=["ones_f"])
        S.op("act", lambda e: e.activation(out=cond[:], in_=vec[:, C_C:C_C + 8], func=AF.Silu),
             r=["vec"], w=["cond"])
        for kc in range(8):
            S.op("dve", lambda e, kc=kc: e.tensor_scalar(out=condrep[:, kc, :], in0=ones_f[:],
                                                         scalar1=cond[:, kc:kc + 1], scalar2=None,
                                                         op0=ALU.mult),
                 r=["ones_f", "cond"], w=["condrep"])

        for i, (dst, name) in enumerate(((ag_bc, "ag_bc"), (ab_bc, "ab_bc"),
                                         (ln2g_bc, "ln2g_bc"), (ln2b_bc, "ln2b_bc"))):
            S.op("sp", lambda e, dst=dst, i=i: e.dma_start(out=dst[:], in_=rows[i].partition_broadcast(128)),
                 w=[name], dma="ld_" + name)
        S.op("act", lambda e: e.mul(out=ag_bc[:], in_=ag_bc[:], mul=ALPHA), r=["ag_bc"], w=["ag_bc"])
        S.op("act", lambda e: e.mul(out=ab_bc[:], in_=ab_bc[:], mul=ALPHA), r=["ab_bc"], w=["ab_bc"])

        def cast_dma(dst, src, wkeys, key):
            S.op("pool", lambda e: e.dma_start(out=dst, in_=src), w=wkeys, dma=key)

        def mod_block(blk):
            slot = blk % 2
            col0 = blk * 256
            vi, off = col0 // D, col0 % D
            cast_dma(wada_r[slot][:, :, 0:256], w_ada[:, col0:col0 + 256].rearrange("(kc p) n -> p kc n", p=128),
                     [("w1r", slot)], "wada%d" % slot)
            S.op("sp", lambda e: e.dma_start(out=bada_r[slot][:],
                                             in_=b_ada[col0:col0 + 256].partition_broadcast(128)),
                 w=[("vc", slot)], dma="bada%d" % slot)
            k = rbank()

            def mm(e):
                for kc in range(8):
                    ins = e.matmul(ps[k][:, 0:256], lhsT=condrep[:, kc, :], rhs=wada_r[slot][:, kc, 0:256],
                                   start=(kc == 0), stop=(kc == 7))
                return ins
            S.op("pe", mm, r=[("w1r", slot), "condrep"], w=[("ps", k)])
            if vi in (2, 5):
                dst, name = (g1p_bc, "g1p_bc") if vi == 2 else (g2p_bc, "g2p_bc")
                S.op("dve", lambda e: e.scalar_tensor_tensor(out=dst[:, off:off + 256], in0=ps[k][:, 0:256],
                                                             scalar=1.0, in1=bada_r[slot][:],
                                                             op0=ALU.add, op1=ALU.add),
                     r=[("ps", k), ("vc", slot)], w=[(name, off)])
            else:
                addc = 1.0 if vi in (1, 4) else 0.0
                S.op("dve", lambda e: e.scalar_tensor_tensor(out=tmpbc[slot][:], in0=ps[k][:, 0:256],
                                                             scalar=addc, in1=bada_r[slot][:],
                                                             op0=ALU.add, op1=ALU.add),
                     r=[("ps", k), ("vc", slot)], w=[("cv", slot)])
                vslot = {0: 0, 1: 1, 3: 2, 4: 3}[vi]

                def mmT(e):
                    for c in range(2):
                        col = vslot * 8 + off // 128 + c
                        ins = e.matmul(ps[7][:, col:col + 1], lhsT=tmpbc[slot][0:1, c * 128:(c + 1) * 128],
                                       rhs=ones_f[0:1, 0:1], start=True, stop=True)
                    return ins
                S.op("pe", mmT, r=[("cv", slot), "ones_f"], w=[("ps", 7)])

        for blk in range(8):
            mod_block(blk)
        for h in range(2):
            cast_dma(w_in_sb[:, 4 * h:4 * h + 4, :],
                     w_in[512 * h:512 * (h + 1), :].rearrange("(kc p) n -> p kc n", p=128),
                     [("w_in", h)], "c_w_in%d" % h)
        cast_dma(w_pool_sb[:], w_pool.rearrange("g c d -> c g d"), ["w_pool"], "c_w_pool")
        cast_dma(w_out_sb[:], w_out.rearrange("(kc p) n -> p kc n", p=128), ["w_out"], "c_w_out")
        for blk in range(8, 24):
            mod_block(blk)
        for b in range(8):
            cast_dma(w1s[:, :, b * 512:(b + 1) * 512],
                     w1[:, b * 512:(b + 1) * 512].rearrange("(kc p) n -> p kc n", p=128),
                     [("w1s", b)], "c_w1_%d" % b)
            cast_dma(w2s[:, 4 * b:4 * b + 4, :],
                     w2[b * 512:(b + 1) * 512, :].rearrange("(c p) n -> p c n", p=128),
                     [("w2s", b)], "c_w2_%d" % b)
        S.op("act", lambda e: e.copy(out=modT[:], in_=ps[7][:, 0:32]), r=[("ps", 7)], w=["modT"])
        S.op("dve", lambda e: e.tensor_tensor(out=GB[:, 0:8], in0=vec[:, C_G1:C_G1 + 8], in1=modT[:, 24:32],
                                              op=ALU.mult), r=["vec", "modT"], w=["G2"])
        S.op("dve", lambda e: e.tensor_tensor(out=GB[:, 16:24], in0=vec[:, C_B1:C_B1 + 8], in1=modT[:, 24:32],
                                              op=ALU.mult), r=["vec", "modT"], w=["B2t"])
        S.op("dve", lambda e: e.tensor_tensor(out=GB[:, 8:16], in0=GB[:, 16:24], in1=modT[:, 16:24],
                                              op=ALU.add), r=["B2t", "modT"], w=["B2"])
        G1PK = [("g1p_bc", o) for o in range(0, D, 256)]
        G2PK = [("g2p_bc", o) for o in range(0, D, 256)]

        def layer_norm_stats(src_ap, skey, par):
            st_, m_ = stats[par], mv[par]
            S.op("dve", lambda e: e.bn_stats(out=st_[:, 0:6], in_=src_ap[:, 0:512]), r=[skey], w=[("st0", par)])
            S.op("dve", lambda e: e.bn_stats(out=st_[:, 6:12], in_=src_ap[:, 512:1024]), r=[skey], w=[("st1", par)])
            S.op("dve", lambda e: e.bn_aggr(out=m_[:, 0:2], in_=st_[:, 0:12]),
                 r=[("st0", par), ("st1", par)], w=[("mv01", par)])
            S.op("dve", lambda e: e.tensor_scalar(out=m_[:, 4:5], in0=m_[:, 1:2], scalar1=EPS, scalar2=None,
                                                  op0=ALU.add), r=[("mv01", par)], w=[("mv4", par)])
            S.op("act", lambda e: e.activation(out=m_[:, 5:6], in_=m_[:, 4:5], func=AF.Sqrt),
                 r=[("mv4", par)], w=[("mv5", par)])
            S.op("dve", lambda e: e.reciprocal(out=m_[:, 2:3], in_=m_[:, 5:6]), r=[("mv5", par)], w=[("mv2", par)])
            S.op("dve", lambda e: e.scalar_tensor_tensor(out=m_[:, 3:4], in0=m_[:, 0:1], scalar=-1.0,
                                                         in1=m_[:, 2:3], op0=ALU.mult, op1=ALU.mult),
                 r=[("mv01", par), ("mv2", par)], w=[("mv3", par)])

        lnc = [0]
        rbc = [0]

        def load_x(i):
            slot = i % 2
            S.op("sp", lambda e: e.dma_start(
                out=xt[slot][:], in_=xh[HALO + i * T:HALO + (i + 1) * T, :].rearrange("(s p) f -> p s f", p=128)),
                w=[("xt", slot, s) for s in range(NSUB)], dma="ld_x%d" % slot)

        load_x(0)

        def make_tile(i):
            xs = i % 2
            tp = i % 2
            ub = ub2[tp]
            h2T = h2T2[tp]
            XP, YP = [], []

            def x_cast():
              for s in range(NSUB):
                S.op("act", lambda e, s=s: e.copy(out=xbf[:, s, :], in_=xt[xs][:, s, :]),
                     r=[("xt", xs, s)], w=[("xbf", s)])

            def x_front():
              for kq in range(2):
                k = rbank()

                def trx(e, kq=kq, k=k):
                    for kk in range(4):
                        kc = kq * 4 + kk
                        for s in range(NSUB):
                            ins = e.transpose(psb[k][:, kk * T + s * 128:kk * T + (s + 1) * 128],
                                              xbf[:, s, kc * 128:(kc + 1) * 128], idb[:])
                    return ins
                S.op("pe", trx, r=[("xbf", s) for s in range(NSUB)] + ["idb"], w=[("ps", k)])
                for kk in range(4):
                    kc = kq * 4 + kk
                    S.op("act", lambda e, kk=kk, kc=kc, k=k: e.activation(
                        out=h1T[:, kc, :], in_=psb[k][:, kk * T:(kk + 1) * T], func=AF.Identity,
                        bias=modT[:, kc:kc + 1], scale=modT[:, 8 + kc:9 + kc]),
                        r=[("ps", k), "modT"], w=[("h1T", kc)])
              if i == 0:
                S.op("sp", lambda e: e.dma_start(out=rb[0][0:HALO, :], in_=xh[0:HALO, :]), w=[("rb", 0)], dma="ld_halo")
                S.op("act", lambda e: e.copy(out=ubf[0][0:HALO, :], in_=rb[0][0:HALO, :]), r=[("rb", 0)], w=[("ubf", 0)])
                k = rbank()

                def trh(e, k=k):
                    for kc in range(8):
                        ins = e.transpose(psb[k][:, kc * HALO:(kc + 1) * HALO],
                                          ubf[0][0:HALO, kc * 128:(kc + 1) * 128], idb[0:HALO, 0:HALO])
                    return ins
                S.op("pe", trh, r=[("ubf", 0), "idb"], w=[("ps", k)])
                for kc in range(8):
                    S.op("dve", lambda e, kc=kc, k=k: e.tensor_scalar(
                        out=h1halo[:, kc, :], in0=psb[k][:, kc * HALO:(kc + 1) * HALO],
                        scalar1=modT[:, 8 + kc:9 + kc], scalar2=modT[:, kc:kc + 1],
                        op0=ALU.mult, op1=ALU.add), r=[("ps", k), "modT"], w=[("h1halo", kc)])
                    S.op("dve", lambda e, kc=kc: e.tensor_scalar(
                        out=h1halo[:, kc, :], in0=h1halo[:, kc, :], scalar1=vec[:, C_FLAG:C_FLAG + 1],
                        scalar2=None, op0=ALU.mult), r=[("h1halo", kc), "vec"], w=[("h1halo", kc)])
              if i + 1 < NT:
                load_x(i + 1)
            XP.append(x_front)

            H1K = [("h1T", kc) for kc in range(8)]
            H1HK = [("h1halo", kc) for kc in range(8)]
            WINK = [("w_in", 0), ("w_in", 1)]

            def inproj(col0, halo=False):
                k = rbank()
                n = HALO if halo else T

                def mm(e):
                    for kc in range(8):
                        rhs = h1halo[:, kc, :] if halo else h1T[:, kc, :]
                        ins = e.matmul(ps[k][:, 0:n], lhsT=w_in_sb[:, kc, col0:col0 + 128], rhs=rhs,
                                       start=(kc == 0), stop=(kc == 7))
                    return ins
                S.op("pe", mm, r=WINK + (H1HK if halo else H1K), w=[("ps", k)])
                return k

            def chunk_q(j):
                par = j % 2
                k_q = rbank()
                S.op("pe", lambda e, k=k_q: e.matmul(ps[k][:, 0:T], lhsT=w_pool_sb[:, j, :], rhs=pbf[par][:],
                                                     start=True, stop=True),
                     r=["w_pool", ("pbf", par)], w=[("ps", k_q)])
                S.op("act", lambda e, k=k_q: e.mul(out=ycatT[:, 4 + j, :], in_=ps[k][:, 0:T],
                                                   mul=vec[:, C_PS + j:C_PS + j + 1]),
                     r=[("ps", k_q), "vec"], w=[("ycatT", 4 + j)])

            def do_chunk(j):
                par = j % 2
                if j > 0:
                    chunk_q(j - 1)
                U, V = uw[j], vpw[j]
                if i > 0:
                    S.op("act", lambda e: e.copy(out=U[:, 0:HALO], in_=U[:, T:T + HALO]),
                         r=[("uw", j)], w=[("uwh", j)])
                    S.op("act", lambda e: e.copy(out=V[:, 0:HALO], in_=V[:, T:T + HALO]),
                         r=[("vpw", j)], w=[("vpwh", j)])
                k_vc = inproj(1024 + 128 * j)
                S.op("act", lambda e, k=k_vc: e.copy(out=vc[par][:], in_=ps[k][:, 0:T]),
                     r=[("ps", k_vc)], w=[("vc", par)])
                k_gc = inproj(512 + 128 * j)
                S.op("dve", lambda e, k=k_gc: e.tensor_tensor(out=U[:, HALO:HALO + T], in0=ps[k][:, 0:T],
                                                              in1=vc[par][:], op=ALU.mult),
                     r=[("ps", k_gc), ("vc", par)], w=[("uw", j)])
                if i == 0:
                    k_h = inproj(1024 + 128 * j, halo=True)
                    S.op("act", lambda e, k=k_h: e.copy(out=tmp16[:], in_=ps[k][:, 0:HALO]),
                         r=[("ps", k_h)], w=["tmp16"])
                    k_h2 = inproj(512 + 128 * j, halo=True)
                    S.op("dve", lambda e, k=k_h2: e.tensor_tensor(out=U[:, 0:HALO], in0=ps[k][:, 0:HALO],
                                                                  in1=tmp16[:], op=ALU.mult),
                         r=[("ps", k_h2), "tmp16"], w=[("uwh", j)])
                k_vp = inproj(1536 + 128 * j)
                S.op("act", lambda e, k=k_vp: e.copy(out=V[:, HALO:HALO + T], in_=ps[k][:, 0:T]),
                     r=[("ps", k_vp)], w=[("vpw", j)])
                if i == 0:
                    k_h3 = inproj(1536 + 128 * j, halo=True)
                    S.op("act", lambda e, k=k_h3: e.copy(out=V[:, 0:HALO], in_=ps[k][:, 0:HALO]),
                         r=[("ps", k_h3)], w=[("vpwh", j)])
                cw = C_CW + 3 * j
                S.op("act", lambda e: e.mul(out=cv[par][:], in_=U[:, HALO:HALO + T], mul=vec[:, cw + 2:cw + 3]),
                     r=[("uw", j), "vec"], w=[("cv", par)])
                S.op("dve", lambda e: e.scalar_tensor_tensor(out=cv[par][:], in0=U[:, HALO - 1:HALO - 1 + T],
                                                             scalar=vec[:, cw + 1:cw + 2], in1=cv[par][:],
                                                             op0=ALU.mult, op1=ALU.add),
                     r=[("uw", j), ("uwh", j), ("cv", par), "vec"], w=[("cv", par)])
                S.op("dve", lambda e: e.scalar_tensor_tensor(out=cv[par][:], in0=U[:, HALO - 2:HALO - 2 + T],
                                                             scalar=vec[:, cw:cw + 1], in1=cv[par][:],
                                                             op0=ALU.mult, op1=ALU.add),
                     r=[("uw", j), ("uwh", j), ("cv", par), "vec"], w=[("cv", par)])
                k_gb = inproj(128 * j)
                S.op("dve", lambda e, k=k_gb: e.tensor_tensor(out=ycatT[:, j, :], in0=ps[k][:, 0:T],
                                                              in1=cv[par][:], op=ALU.mult),
                     r=[("ps", k_gb), ("cv", par)], w=[("ycatT", j)])
                L = T + HALO
                VK = [("vpw", j), ("vpwh", j)]
                S.op("dve", lambda e: e.tensor_tensor(out=sA[par][:, 1:L], in0=V[:, 1:L], in1=V[:, 0:L - 1], op=ALU.add),
                     r=VK, w=[("sA", par)])
                cur, curk = sA, "sA"
                if j >= 1:
                    S.op("dve", lambda e: e.tensor_tensor(out=sB[par][:, 3:L], in0=sA[par][:, 3:L],
                                                          in1=sA[par][:, 1:L - 2], op=ALU.add),
                         r=[("sA", par)], w=[("sB", par)])
                    cur, curk = sB, "sB"
                if j >= 2:
                    S.op("dve", lambda e: e.tensor_tensor(out=sA[par][:, 7:L], in0=sB[par][:, 7:L],
                                                          in1=sB[par][:, 3:L - 4], op=ALU.add),
                         r=[("sB", par)], w=[("sA", par)])
                    cur, curk = sA, "sA"
                if j >= 3:
                    S.op("dve", lambda e: e.tensor_tensor(out=sB[par][:, 15:L], in0=sA[par][:, 15:L],
                                                          in1=sA[par][:, 7:L - 8], op=ALU.add),
                         r=[("sA", par)], w=[("sB", par)])
                    cur, curk = sB, "sB"
                win = float(WINS[j])
                S.op("dve", lambda e, cur=cur: e.scalar_tensor_tensor(
                    out=pbf[par][:], in0=cur[par][:, HALO:HALO + T], scalar=1.0 / win,
                    in1=V[:, HALO:HALO + T], op0=ALU.mult, op1=ALU.subtract),
                    r=[(curk, par), ("vpw", j)], w=[("pbf", par)])
                if i == 0:
                    ic = C_INV + 16 * j
                    S.op("dve", lambda e, cur=cur: e.tensor_tensor(
                        out=tmp16[:], in0=cur[par][:, HALO:2 * HALO], in1=vec[:, ic:ic + 16], op=ALU.mult),
                        r=[(curk, par), "vec"], w=["tmp16"])
                    S.op("dve", lambda e: e.tensor_tensor(
                        out=pbf[par][:, 0:HALO], in0=tmp16[:], in1=V[:, HALO:2 * HALO], op=ALU.subtract),
                        r=["tmp16", ("vpw", j), ("pbf", par)], w=[("pbf", par)])

            for j in range(4):
                XP.append(lambda j=j: do_chunk(j))
            XP.append(lambda: chunk_q(3))

            YK = [("ycatT", c) for c in range(8)]
            uT_banks = []

            def sub1A(s):
                ri = rbc[0] % NR
                rbc[0] += 1
                r_ = rb[ri]
                for hf in range(2):
                    ka = rbank()

                    def mmo(e, ka=ka, hf=hf, s=s):
                        for kc in range(8):
                            ins = e.matmul(ps[ka][:, :], lhsT=ycatT[:, kc, s * 128:(s + 1) * 128],
                                           rhs=w_out_sb[:, kc, hf * 512:(hf + 1) * 512],
                                           start=(kc == 0), stop=(kc == 7))
                        return ins
                    S.op("pe", mmo, r=YK + ["w_out"], w=[("ps", ka)])
                    S.op("dve", lambda e, ka=ka, hf=hf, r_=r_: e.tensor_tensor(
                        out=r_[:, hf * 512:(hf + 1) * 512], in0=ps[ka][:, :], in1=g1p_bc[:, hf * 512:(hf + 1) * 512],
                        op=ALU.mult), r=[("ps", ka)] + G1PK, w=[("rb", ri, hf), ("rb", ri)])
                S.op("dve", lambda e, r_=r_, s=s: e.scalar_tensor_tensor(
                    out=r_[:], in0=xt[xs][:, s, :], scalar=ALPHA, in1=r_[:], op0=ALU.mult, op1=ALU.add),
                    r=[("xt", xs, s), ("rb", ri, 0), ("rb", ri, 1)], w=[("rb", ri)])
                par = lnc[0] % 2
                lnc[0] += 1
                layer_norm_stats(r_, ("rb", ri), par)
                m_ = mv[par]
                MK = [("mv2", par), ("mv3", par)]
                up = s % 2
                S.op("act", lambda e, r_=r_, m_=m_, up=up: e.activation(out=ubf[up][:], in_=r_[:], func=AF.Identity,
                                                                       bias=m_[:, 3:4], scale=m_[:, 2:3]),
                     r=[("rb", ri)] + MK, w=[("ubf", up)])
                S.op("act", lambda e, r_=r_, m_=m_, s=s: e.activation(out=ub[:, s, :], in_=r_[:], func=AF.Identity,
                                                                     bias=m_[:, 3:4], scale=m_[:, 2:3]),
                     r=[("rb", ri)] + MK, w=[("ub", tp, s)])
                S.op("dve", lambda e, s=s: e.tensor_tensor(out=ub[:, s, :], in0=ub[:, s, :], in1=ag_bc[:], op=ALU.mult),
                     r=[("ub", tp, s), "ag_bc"], w=[("ub", tp, s)])
                S.op("dve", lambda e, s=s: e.tensor_tensor(out=ub[:, s, :], in0=ub[:, s, :], in1=ab_bc[:], op=ALU.add),
                     r=[("ub", tp, s), "ab_bc"], w=[("ub", tp, s)])

            def sub1B(s):
                up = s % 2
                for kq in range(2):
                    k = rbank()

                    def tru(e, kq=kq, k=k, up=up, s=s):
                        for kk in range(4):
                            kc = kq * 4 + kk
                            ins = e.transpose(psb[k][:, kk * 128:(kk + 1) * 128],
                                              ubf[up][:, kc * 128:(kc + 1) * 128], idb[:])
                        return ins
                    S.op("pe", tru, r=[("ubf", up), "idb"], w=[("ps", k)])
                    for kk in range(4):
                        kc = kq * 4 + kk
                        S.op("act", lambda e, kk=kk, kc=kc, k=k, s=s: e.activation(
                            out=h2T[:, kc, s * 128:(s + 1) * 128], in_=psb[k][:, kk * 128:(kk + 1) * 128],
                            func=AF.Identity, bias=GB[:, 8 + kc:9 + kc], scale=GB[:, kc:kc + 1]),
                            r=[("ps", k), "G2", "B2"], w=[("h2T", tp, kc, s)])
            XP.append(lambda: sub1A(0))

            def p7():
                sub1A(1)
                sub1B(0)
            XP.append(p7)
            XP.append(lambda: sub1B(1))
            XP.append(None)


            H2K = [("h2T", tp, kc, s) for kc in range(8) for s in range(NSUB)]

            def load_w(b):
                slot = (i * 8 + b) % NW
                S.op("sp", lambda e: e.dma_start(out=w1r[slot][:], in_=w1s[:, :, b * 512:(b + 1) * 512]),
                     r=[("w1s", b)], w=[("w1r", slot)], dma="ld_w1_%d" % slot)
                S.op("sp", lambda e: e.dma_start(out=w2r[slot][:], in_=w2s[:, 4 * b:4 * b + 4, :]),
                     r=[("w2s", b)], w=[("w2r", slot)], dma="ld_w2_%d" % slot)

            def stage_a(b):
                slot = (i * 8 + b) % NW
                ap_ = b % 2
                for c in range(4):
                    k = rbank()
                    rp = c % 2

                    def mma(e, k=k, c=c):
                        for kc in range(8):
                            ins = e.matmul(ps[k][:, 0:T], lhsT=w1r[slot][:, kc, c * 128:(c + 1) * 128],
                                           rhs=h2T[:, kc, :], start=(kc == 0), stop=(kc == 7))
                        return ins
                    S.op("pe", mma, r=[("w1r", slot)] + H2K, w=[("ps", k)])
                    S.op("act", lambda e, k=k, rp=rp: e.activation(out=rl[rp][:], in_=ps[k][:, 0:T], func=AF.Relu),
                         r=[("ps", k)], w=[("rl", rp)])
                    S.op("act", lambda e, rp=rp, c=c: e.activation(out=aT[ap_][:, c, :], in_=rl[rp][:], func=AF.Square),
                         r=[("rl", rp)], w=[("aT", ap_, c)])

            def stage_b(b):
                slot = (i * 8 + b) % NW
                ap_ = b % 2
                for s in range(NSUB):
                    for hf in range(2):
                        ka = 4 + (2 * s + hf) % 4

                        def mmb(e, ka=ka, s=s, hf=hf):
                            for c in range(4):
                                ins = e.matmul(ps[ka][:, :], lhsT=aT[ap_][:, c, s * 128:(s + 1) * 128],
                                               rhs=w2r[slot][:, c, hf * 512:(hf + 1) * 512],
                                               start=(b == 0 and c == 0), stop=(b == 7 and c == 3))
                            return ins
                        S.op("pe", mmb, r=[("w2r", slot)] + [("aT", ap_, c) for c in range(4)], w=[("ps", ka)])

            def y_first():
                load_w(0)
                stage_a(0)
            YP.append(y_first)

            def y_mid(b):
                load_w(b)
                stage_a(b)
                stage_b(b - 1)
            for b in range(1, 8):
                YP.append(lambda b=b: y_mid(b))
            YP.append(lambda: stage_b(7))

            def do_sub2(s):
                ri = rbc[0] % NR
                rbc[0] += 1
                r_ = rb[ri]
                for hf in range(2):
                    ka = 4 + (2 * s + hf) % 4
                    S.op("dve", lambda e, ka=ka, hf=hf, r_=r_: e.tensor_tensor(
                        out=r_[:, hf * 512:(hf + 1) * 512], in0=ps[ka][:, :], in1=g2p_bc[:, hf * 512:(hf + 1) * 512],
                        op=ALU.mult), r=[("ps", ka)] + G2PK, w=[("rb", ri, hf), ("rb", ri)])
                S.op("dve", lambda e, r_=r_, s=s: e.tensor_tensor(out=r_[:], in0=r_[:], in1=ub[:, s, :], op=ALU.add),
                     r=[("ub", tp, s), ("rb", ri, 0), ("rb", ri, 1)], w=[("rb", ri)])
                par = lnc[0] % 2
                lnc[0] += 1
                layer_norm_stats(r_, ("rb", ri), par)
                m_ = mv[par]
                MK = [("mv2", par), ("mv3", par)]
                S.op("act", lambda e, r_=r_, m_=m_: e.activation(out=r_[:], in_=r_[:], func=AF.Identity,
                                                                bias=m_[:, 3:4], scale=m_[:, 2:3]),
                     r=[("rb", ri)] + MK, w=[("rb", ri)])
                S.op("dve", lambda e, r_=r_: e.tensor_tensor(out=r_[:], in0=r_[:], in1=ln2g_bc[:], op=ALU.mult),
                     r=[("rb", ri), "ln2g_bc"], w=[("rb", ri)])
                S.op("dve", lambda e, r_=r_: e.tensor_tensor(out=r_[:], in0=r_[:], in1=ln2b_bc[:], op=ALU.add),
                     r=[("rb", ri), "ln2b_bc"], w=[("rb", ri)])
                row0 = i * T + s * 128
                S.op("pool", lambda e, r_=r_, row0=row0: e.dma_start(out=out[row0:row0 + 128, :], in_=r_[:]),
                     r=[("rb", ri)], w=[("out", row0)], dma="st_%d" % ri)

            def y_tail():
                for s in range(NSUB):
                    do_sub2(s)
            YP.append(y_tail)
            return XP, YP, x_cast

        tiles = [make_tile(i) for i in range(NT)]
        tiles[0][2]()
        for rnd in range(NT + 1):
            XP = list(tiles[rnd][0]) if rnd < NT else []
            if rnd < NT:
                XP[9] = tiles[rnd + 1][2] if rnd + 1 < NT else None
            YP = tiles[rnd - 1][1] if rnd >= 1 else []
            n = max(len(XP), len(YP))
            for q in range(n):
                if q < len(XP) and XP[q] is not None:
                    XP[q]()
                if q < len(YP):
                    YP[q]()

        S.emit(nc)
    return nc


_CACHE = {}


def _layout_inputs(x, c, w_ada, b_ada, w_in, conv_w, w_pool, pool_scale, w_out, ln1_g, ln1_b,
                   w_mlp_in, w_mlp_out, ln2_g, ln2_b):
    f = np.float32
    x = np.asarray(x, f)
    c = np.asarray(c, f)
    shared = {
        "rows": np.ascontiguousarray(np.stack([np.asarray(ln1_g, f)[0], np.asarray(ln1_b, f)[0],
                                               np.asarray(ln2_g, f)[0], np.asarray(ln2_b, f)[0]])),
        "b_ada": np.ascontiguousarray(np.asarray(b_ada, f)[0]),
        "w_ada": np.ascontiguousarray(np.asarray(w_ada, f)[0]),
        "w_in": np.ascontiguousarray(np.asarray(w_in, f)[0]),
        "w_pool": np.ascontiguousarray(np.asarray(w_pool, f)[0]),
        "w_out": np.ascontiguousarray(np.asarray(w_out, f)[0]),
        "w1": np.ascontiguousarray(np.asarray(w_mlp_in, f)[0]),
        "w2": np.ascontiguousarray(np.asarray(w_mlp_out, f)[0]),
    }
    g1T = np.asarray(ln1_g, f)[0].reshape(8, 128).T
    b1T = np.asarray(ln1_b, f)[0].reshape(8, 128).T
    cw = np.asarray(conv_w, f)[0]
    cwT = cw.reshape(3, 4, 128).transpose(2, 1, 0).reshape(128, 12)
    psT = np.asarray(pool_scale, f)[0].reshape(4, 128).T
    in_maps = []
    for core in range(NCORES):
        b, half = core // 2, core % 2
        start = half * TOK
        xh = np.zeros((TOK + HALO, D), f)
        xh[HALO:] = x[b, start:start + TOK]
        if half:
            xh[:HALO] = x[b, start - HALO:start]
        vecs = np.zeros((128, NV), f)
        vecs[:, C_C:C_C + 8] = c[b].reshape(8, 128).T
        vecs[:, C_G1:C_G1 + 8] = g1T
        vecs[:, C_B1:C_B1 + 8] = b1T
        vecs[:, C_CW:C_CW + 12] = cwT
        vecs[:, C_PS:C_PS + 4] = psT
        vecs[:, C_FLAG] = 1.0 if half else 0.0
        for g, win in enumerate(WINS):
            for t in range(16):
                vecs[:, C_INV + 16 * g + t] = (1.0 / win) if half else (1.0 / min(t + 1, win))
        m = dict(shared)
        m["xh"] = xh
        m["vecs"] = vecs
        in_maps.append(m)
    return in_maps


def kernel(**inputs):
    if "nc" not in _CACHE:
        _CACHE["nc"] = build_program()
    nc = _CACHE["nc"]
    in_maps = _layout_inputs(**inputs)
    res = run_bass_kernel_spmd(nc, in_maps, core_ids=list(range(NCORES)))
    outp = np.empty((BATCH, SEQ, D), np.float32)
    for core in range(NCORES):
        b, half = core // 2, core % 2
        outp[b, half * TOK:(half + 1) * TOK] = res.results[core]["out"]
    return outp
```

```python
import numpy as np
from contextlib import ExitStack
import concourse.bass as bass
import concourse.mybir as mybir
from concourse.bass_utils import run_bass_kernel_spmd

F32 = mybir.dt.float32
BF16 = mybir.dt.bfloat16
ALU = mybir.AluOpType
AF = mybir.ActivationFunctionType

D = 1024
SEQ = 8192
BATCH = 4
NCORES = 8
TOK = 4096
HALO = 16
T = 256
NSUB = T // 128
NT = TOK // T
DFF = 4096
ALPHA = 2.0 ** 0.25
EPS = 1e-5
WINS = (2, 4, 8, 16)

C_C, C_G1, C_B1, C_CW, C_PS, C_FLAG, C_INV, NV = 0, 8, 16, 24, 36, 40, 41, 105

ENGS = ("pe", "act", "dve", "pool", "sp")


class _Op:
    __slots__ = ("idx", "eng", "fn", "deps", "dma", "dma_val", "sig", "sig_val")

    def __init__(self, idx, eng, fn, dma):
        self.idx = idx
        self.eng = eng
        self.fn = fn
        self.deps = {}
        self.dma = dma
        self.dma_val = 0
        self.sig = False
        self.sig_val = 0


class Sched:
    def __init__(self):
        self.ops = []
        self.last_w = {}
        self.readers = {}
        self.dma_cnt = {}

    def op(self, eng, fn, r=(), w=(), dma=None):
        o = _Op(len(self.ops), eng, fn, dma)
        for k in r:
            lw = self.last_w.get(k)
            if lw is not None:
                o.deps[lw] = True
        for k in w:
            lw = self.last_w.get(k)
            if lw is not None and lw not in o.deps:
                o.deps[lw] = False
            for rd in self.readers.get(k, ()):
                if rd not in o.deps:
                    o.deps[rd] = False
        for k in r:
            self.readers.setdefault(k, []).append(o.idx)
        for k in w:
            self.last_w[k] = o.idx
            self.readers[k] = []
        if dma is not None:
            c = self.dma_cnt.get(dma, 0) + 1
            self.dma_cnt[dma] = c
            o.dma_val = 16 * c
        self.ops.append(o)
        return o

    def emit(self, nc, final_eng="sp"):
        ops = self.ops
        need = []
        for o in ops:
            lst = []
            for d, is_raw in o.deps.items():
                p = ops[d]
                if p.dma is not None:
                    lst.append(p)
                elif p.eng == o.eng and o.dma is None:
                    if is_raw and o.eng != "pe":
                        lst.append(p)
                else:
                    lst.append(p)
            need.append(lst)
            for p in lst:
                if p.dma is None:
                    p.sig = True
        cnt = {e: 0 for e in ENGS}
        for o in ops:
            if o.sig:
                cnt[o.eng] += 1
                o.sig_val = cnt[o.eng]
        dma_keys = sorted(self.dma_cnt.keys())
        with ExitStack() as st:
            esem = {e: st.enter_context(nc.semaphore("sem_" + e)) for e in ENGS}
            dsem = {k: st.enter_context(nc.semaphore("dsem_" + str(k))) for k in dma_keys}
            block = st.enter_context(nc.Block())
            per_eng = {e: [o for o in ops if o.eng == e] for e in ENGS}

            def run(eng_name, eng):
                waited = {}
                for o in per_eng[eng_name]:
                    for p in need[o.idx]:
                        if p.dma is not None:
                            s, v, key = dsem[p.dma], p.dma_val, ("d", p.dma)
                        else:
                            s, v, key = esem[p.eng], p.sig_val, ("e", p.eng)
                        if waited.get(key, 0) >= v:
                            continue
                        waited[key] = v
                        eng.wait_ge(s, v)
                    ins = o.fn(eng)
                    if o.dma is not None:
                        ins.then_inc(dsem[o.dma], 16)
                    elif o.sig:
                        ins.then_inc(esem[o.eng], 1)
                if eng_name == final_eng:
                    for k in dma_keys:
                        eng.wait_ge(dsem[k], 16 * self.dma_cnt[k])

            @block.tensor
            def _(e):
                run("pe", e)

            @block.scalar
            def _(e):
                run("act", e)

            @block.vector
            def _(e):
                run("dve", e)

            @block.gpsimd
            def _(e):
                run("pool", e)

            @block.sync
            def _(e):
                run("sp", e)


def build_program():
    nc = bass.Bass("TRN2", target_bir_lowering=False)

    def din(name, shape, dt=F32):
        return nc.dram_tensor(name, shape, dt, kind="ExternalInput").ap()

    xh = din("xh", [TOK + HALO, D])
    vecs = din("vecs", [128, NV])
    rows = din("rows", [4, D])
    b_ada = din("b_ada", [6 * D])
    w_ada = din("w_ada", [D, 6 * D])
    w_in = din("w_in", [D, 2048])
    w_pool = din("w_pool", [4, 128, 128])
    w_out = din("w_out", [D, D])
    w1 = din("w1", [D, DFF])
    w2 = din("w2", [DFF, D])
    out = nc.dram_tensor("out", [TOK, D], F32, kind="ExternalOutput").ap()
    w1s = nc.dram_tensor("w1s", [128, 8, DFF], BF16, kind="Internal").ap()
    w2s = nc.dram_tensor("w2s", [128, 32, D], BF16, kind="Internal").ap()

    S = Sched()
    with ExitStack() as st:
        def sb(name, shape, dt=F32):
            return st.enter_context(nc.sbuf_tensor(name, shape, dt))

        ps = [st.enter_context(nc.psum_tensor("ps%d" % k, [128, 512], F32)) for k in range(8)]
        psb = [p.bitcast(BF16) for p in ps]

        vec = sb("vec", [128, NV])
        idf = sb("idf", [128, 128])
        idb = sb("idb", [128, 128], BF16)
        ones_f = sb("ones_f", [128, 128])
        cond = sb("cond", [128, 8])
        condrep = sb("condrep", [128, 8, 128], BF16)
        w_in_sb = sb("w_in_sb", [128, 8, 2048], BF16)
        w_out_sb = sb("w_out_sb", [128, 8, D], BF16)
        w_pool_sb = sb("w_pool_sb", [128, 4, 128], BF16)
        g1p_bc = sb("g1p_bc", [128, D])
        g2p_bc = sb("g2p_bc", [128, D])
        ag_bc = sb("ag_bc", [128, D])
        ab_bc = sb("ab_bc", [128, D])
        ln2g_bc = sb("ln2g_bc", [128, D])
        ln2b_bc = sb("ln2b_bc", [128, D])
        modT = sb("modT", [128, 32])
        GB = sb("GB", [128, 24])
        xt = [sb("xt%d" % i, [128, NSUB, D]) for i in range(2)]
        xbf = sb("xbf", [128, NSUB, D], BF16)
        h1T = sb("h1T", [128, 8, T], BF16)
        h1halo = sb("h1halo", [128, 8, HALO], BF16)
        vc = [sb("vc%d" % i, [128, T]) for i in range(2)]
        uw = [sb("uw%d" % i, [128, T + HALO]) for i in range(4)]
        cv = [sb("cv%d" % i, [128, T]) for i in range(2)]
        vpw = [sb("vpw%d" % i, [128, T + HALO]) for i in range(4)]
        sA = [sb("sA%d" % i, [128, T + HALO]) for i in range(2)]
        sB = [sb("sB%d" % i, [128, T + HALO]) for i in range(2)]
        pbf = [sb("pbf%d" % i, [128, T], BF16) for i in range(2)]
        tmp16 = sb("tmp16", [128, HALO])
        ycatT = sb("ycatT", [128, 8, T], BF16)
        NR = 4
        rb = [sb("rb%d" % i, [128, D]) for i in range(NR)]
        ub2 = [sb("ub%d" % i, [128, NSUB, D]) for i in range(2)]
        ubf = [sb("ubf%d" % i, [128, D], BF16) for i in range(2)]
        h2T2 = [sb("h2T%d" % i, [128, 8, T], BF16) for i in range(2)]
        NW = 2
        w1r = [sb("w1r%d" % i, [128, 8, 512], BF16) for i in range(NW)]
        w2r = [sb("w2r%d" % i, [128, 4, D], BF16) for i in range(NW)]
        rl = [sb("rl%d" % i, [128, T]) for i in range(2)]
        aT = [sb("aT%d" % i, [128, 4, T], BF16) for i in range(2)]
        NLN = 4
        stats = [sb("stats%d" % i, [128, 12]) for i in range(NLN)]
        mv = [sb("mv%d" % i, [128, 8]) for i in range(NLN)]

        wada_r = [w1r[i] for i in range(2)]
        bada_r = [vc[i] for i in range(2)]
        tmpbc = [cv[i] for i in range(2)]
        rot = [0]

        def rbank():
            k = rot[0]
            rot[0] = (k + 1) % 4
            return k

        S.op("sp", lambda e: e.dma_start(out=vec[:], in_=vecs), w=["vec"], dma="ld_vec")
        S.op("pool", lambda e: e.memset(idf[:], 0.0), w=["idf"])
        S.op("pool", lambda e: e.affine_select(out=idf[:], in_=idf[:], pattern=[[-1, 128]],
                                               compare_op=ALU.not_equal, fill=1.0, base=0,
                                               channel_multiplier=1), r=["idf"], w=["idf"])
        S.op("pool", lambda e: e.tensor_copy(out=idb[:], in_=idf[:]), r=["idf"], w=["idb"])
        S.op("pool", lambda e: e.memset(ones_f[:], 1.0), w=["ones_f"])
        S.op("act", lambda e: e.activation(out=cond[:], in_=vec[:, C_C:C_C + 8], func=AF.Silu),
             r=["vec"], w=["cond"])
        for kc in range(8):
            S.op("dve", lambda e, kc=kc: e.tensor_scalar(out=condrep[:, kc, :], in0=ones_f[:],
                                                         scalar1=cond[:, kc:kc + 1], scalar2=None,
                                                         op0=ALU.mult),
                 r=["ones_f", "cond"], w=["condrep"])

        for i, (dst, name) in enumerate(((ag_bc, "ag_bc"), (ab_bc, "ab_bc"),
                                         (ln2g_bc, "ln2g_bc"), (ln2b_bc, "ln2b_bc"))):
            S.op("sp", lambda e, dst=dst, i=i: e.dma_start(out=dst[:], in_=rows[i].partition_broadcast(128)),
                 w=[name], dma="ld_" + name)
        S.op("act", lambda e: e.mul(out=ag_bc[:], in_=ag_bc[:], mul=ALPHA), r=["ag_bc"], w=["ag_bc"])
        S.op("act", lambda e: e.mul(out=ab_bc[:], in_=ab_bc[:], mul=ALPHA), r=["ab_bc"], w=["ab_bc"])

        def cast_dma(dst, src, wkeys, key):
            S.op("pool", lambda e: e.dma_start(out=dst, in_=src), w=wkeys, dma=key)

        def mod_block(blk):
            slot = blk % 2
            col0 = blk * 256
            vi, off = col0 // D, col0 % D
            cast_dma(wada_r[slot][:, :, 0:256], w_ada[:, col0:col0 + 256].rearrange("(kc p) n -> p kc n", p=128),
                     [("w1r", slot)], "wada%d" % slot)
            S.op("sp", lambda e: e.dma_start(out=bada_r[slot][:],
                                             in_=b_ada[col0:col0 + 256].partition_broadcast(128)),
                 w=[("vc", slot)], dma="bada%d" % slot)
            k = rbank()

            def mm(e):
                for kc in range(8):
                    ins = e.matmul(ps[k][:, 0:256], lhsT=condrep[:, kc, :], rhs=wada_r[slot][:, kc, 0:256],
                                   start=(kc == 0), stop=(kc == 7))
                return ins
            S.op("pe", mm, r=[("w1r", slot), "condrep"], w=[("ps", k)])
            if vi in (2, 5):
                dst, name = (g1p_bc, "g1p_bc") if vi == 2 else (g2p_bc, "g2p_bc")
                S.op("dve", lambda e: e.scalar_tensor_tensor(out=dst[:, off:off + 256], in0=ps[k][:, 0:256],
                                                             scalar=1.0, in1=bada_r[slot][:],
                                                             op0=ALU.add, op1=ALU.add),
                     r=[("ps", k), ("vc", slot)], w=[(name, off)])
            else:
                addc = 1.0 if vi in (1, 4) else 0.0
                S.op("dve", lambda e: e.scalar_tensor_tensor(out=tmpbc[slot][:], in0=ps[k][:, 0:256],
                                                             scalar=addc, in1=bada_r[slot][:],
                                                             op0=ALU.add, op1=ALU.add),
                     r=[("ps", k), ("vc", slot)], w=[("cv", slot)])
                vslot = {0: 0, 1: 1, 3: 2, 4: 3}[vi]

                def mmT(e):
                    for c in range(2):
                        col = vslot * 8 + off // 128 + c
                        ins = e.matmul(ps[7][:, col:col + 1], lhsT=tmpbc[slot][0:1, c * 128:(c + 1) * 128],
                                       rhs=ones_f[0:1, 0:1], start=True, stop=True)
                    return ins
                S.op("pe", mmT, r=[("cv", slot), "ones_f"], w=[("ps", 7)])

        for blk in range(8):
            mod_block(blk)
        for h in range(2):
            cast_dma(w_in_sb[:, 4 * h:4 * h + 4, :],
                     w_in[512 * h:512 * (h + 1), :].rearrange("(kc p) n -> p kc n", p=128),
                     [("w_in", h)], "c_w_in%d" % h)
        cast_dma(w_pool_sb[:], w_pool.rearrange("g c d -> c g d"), ["w_pool"], "c_w_pool")
        cast_dma(w_out_sb[:], w_out.rearrange("(kc p) n -> p kc n", p=128), ["w_out"], "c_w_out")
        for blk in range(8, 24):
            mod_block(blk)
        for b in range(8):
            cast_dma(w1s[:, :, b * 512:(b + 1) * 512],
                     w1[:, b * 512:(b + 1) * 512].rearrange("(kc p) n -> p kc n", p=128),
                     [("w1s", b)], "c_w1_%d" % b)
            cast_dma(w2s[:, 4 * b:4 * b + 4, :],
                     w2[b * 512:(b + 1) * 512, :].rearrange("(c p) n -> p c n", p=128),
                     [("w2s", b)], "c_w2_%d" % b)
        S.op("act", lambda e: e.copy(out=modT[:], in_=ps[7][:, 0:32]), r=[("ps", 7)], w=["modT"])
        S.op("dve", lambda e: e.tensor_tensor(out=GB[:, 0:8], in0=vec[:, C_G1:C_G1 + 8], in1=modT[:, 24:32],
                                              op=ALU.mult), r=["vec", "modT"], w=["G2"])
        S.op("dve", lambda e: e.tensor_tensor(out=GB[:, 16:24], in0=vec[:, C_B1:C_B1 + 8], in1=modT[:, 24:32],
                                              op=ALU.mult), r=["vec", "modT"], w=["B2t"])
        S.op("dve", lambda e: e.tensor_tensor(out=GB[:, 8:16], in0=GB[:, 16:24], in1=modT[:, 16:24],
                                              op=ALU.add), r=["B2t", "modT"], w=["B2"])
        G1PK = [("g1p_bc", o) for o in range(0, D, 256)]
        G2PK = [("g2p_bc", o) for o in range(0, D, 256)]

        def ln_stage1(src_ap, skey, par):
            st_, m_ = stats[par], mv[par]
            S.op("dve", lambda e: e.bn_stats(out=st_[:, 0:6], in_=src_ap[:, 0:512]), r=[skey], w=[("st0", par)])
            S.op("dve", lambda e: e.bn_stats(out=st_[:, 6:12], in_=src_ap[:, 512:1024]), r=[skey], w=[("st1", par)])
            S.op("dve", lambda e: e.bn_aggr(out=m_[:, 0:2], in_=st_[:, 0:12]),
                 r=[("st0", par), ("st1", par)], w=[("mv01", par)])
            S.op("dve", lambda e: e.tensor_scalar(out=m_[:, 4:5], in0=m_[:, 1:2], scalar1=EPS, scalar2=None,
                                                  op0=ALU.add), r=[("mv01", par)], w=[("mv4", par)])

        def ln_stage2(par):
            m_ = mv[par]
            S.op("act", lambda e: e.activation(out=m_[:, 5:6], in_=m_[:, 4:5], func=AF.Sqrt),
                 r=[("mv4", par)], w=[("mv5", par)])
            S.op("dve", lambda e: e.reciprocal(out=m_[:, 2:3], in_=m_[:, 5:6]), r=[("mv5", par)], w=[("mv2", par)])

        lnc = [0]
        rbc = [0]

        def load_x(i):
            slot = i % 2
            S.op("sp", lambda e: e.dma_start(
                out=xt[slot][:], in_=xh[HALO + i * T:HALO + (i + 1) * T, :].rearrange("(s p) f -> p s f", p=128)),
                w=[("xt", slot, s) for s in range(NSUB)], dma="ld_x%d" % slot)

        load_x(0)

        def make_tile(i):
            xs = i % 2
            tp = i % 2
            ub = ub2[tp]
            h2T = h2T2[tp]
            XP, YP = [], []

            def x_cast():
              for s in range(NSUB):
                S.op("act", lambda e, s=s: e.copy(out=xbf[:, s, :], in_=xt[xs][:, s, :]),
                     r=[("xt", xs, s)], w=[("xbf", s)])

            def x_front():
              for kq in range(2):
                k = rbank()

                def trx(e, kq=kq, k=k):
                    for kk in range(4):
                        kc = kq * 4 + kk
                        for s in range(NSUB):
                            ins = e.transpose(psb[k][:, kk * T + s * 128:kk * T + (s + 1) * 128],
                                              xbf[:, s, kc * 128:(kc + 1) * 128], idb[:])
                    return ins
                S.op("pe", trx, r=[("xbf", s) for s in range(NSUB)] + ["idb"], w=[("ps", k)])
                for kk in range(4):
                    kc = kq * 4 + kk
                    S.op("act", lambda e, kk=kk, kc=kc, k=k: e.activation(
                        out=h1T[:, kc, :], in_=psb[k][:, kk * T:(kk + 1) * T], func=AF.Identity,
                        bias=modT[:, kc:kc + 1], scale=modT[:, 8 + kc:9 + kc]),
                        r=[("ps", k), "modT"], w=[("h1T", kc)])
              if i == 0:
                S.op("sp", lambda e: e.dma_start(out=rb[0][0:HALO, :], in_=xh[0:HALO, :]), w=[("rb", 0)], dma="ld_halo")
                S.op("act", lambda e: e.copy(out=ubf[0][0:HALO, :], in_=rb[0][0:HALO, :]), r=[("rb", 0)], w=[("ubf", 0)])
                k = rbank()

                def trh(e, k=k):
                    for kc in range(8):
                        ins = e.transpose(psb[k][:, kc * HALO:(kc + 1) * HALO],
                                          ubf[0][0:HALO, kc * 128:(kc + 1) * 128], idb[0:HALO, 0:HALO])
                    return ins
                S.op("pe", trh, r=[("ubf", 0), "idb"], w=[("ps", k)])
                for kc in range(8):
                    S.op("dve", lambda e, kc=kc, k=k: e.tensor_scalar(
                        out=h1halo[:, kc, :], in0=psb[k][:, kc * HALO:(kc + 1) * HALO],
                        scalar1=modT[:, 8 + kc:9 + kc], scalar2=modT[:, kc:kc + 1],
                        op0=ALU.mult, op1=ALU.add), r=[("ps", k), "modT"], w=[("h1halo", kc)])
                    S.op("dve", lambda e, kc=kc: e.tensor_scalar(
                        out=h1halo[:, kc, :], in0=h1halo[:, kc, :], scalar1=vec[:, C_FLAG:C_FLAG + 1],
                        scalar2=None, op0=ALU.mult), r=[("h1halo", kc), "vec"], w=[("h1halo", kc)])
              if i + 1 < NT:
                load_x(i + 1)
            XP.append(x_front)

            H1K = [("h1T", kc) for kc in range(8)]
            H1HK = [("h1halo", kc) for kc in range(8)]
            WINK = [("w_in", 0), ("w_in", 1)]

            def inproj(col0, halo=False):
                k = rbank()
                n = HALO if halo else T

                def mm(e):
                    for kc in range(8):
                        rhs = h1halo[:, kc, :] if halo else h1T[:, kc, :]
                        ins = e.matmul(ps[k][:, 0:n], lhsT=w_in_sb[:, kc, col0:col0 + 128], rhs=rhs,
                                       start=(kc == 0), stop=(kc == 7))
                    return ins
                S.op("pe", mm, r=WINK + (H1HK if halo else H1K), w=[("ps", k)])
                return k

            def chunk_q(j):
                par = j % 2
                k_q = rbank()
                S.op("pe", lambda e, k=k_q: e.matmul(ps[k][:, 0:T], lhsT=w_pool_sb[:, j, :], rhs=pbf[par][:],
                                                     start=True, stop=True),
                     r=["w_pool", ("pbf", par)], w=[("ps", k_q)])
                S.op("act", lambda e, k=k_q: e.mul(out=ycatT[:, 4 + j, :], in_=ps[k][:, 0:T],
                                                   mul=vec[:, C_PS + j:C_PS + j + 1]),
                     r=[("ps", k_q), "vec"], w=[("ycatT", 4 + j)])

            def do_chunk(j):
                par = j % 2
                if j > 0:
                    chunk_q(j - 1)
                U, V = uw[j], vpw[j]
                if i > 0:
                    S.op("act", lambda e: e.copy(out=U[:, 0:HALO], in_=U[:, T:T + HALO]),
                         r=[("uw", j)], w=[("uwh", j)])
                    S.op("act", lambda e: e.copy(out=V[:, 0:HALO], in_=V[:, T:T + HALO]),
                         r=[("vpw", j)], w=[("vpwh", j)])
                k_vc = inproj(1024 + 128 * j)
                S.op("act", lambda e, k=k_vc: e.copy(out=vc[par][:], in_=ps[k][:, 0:T]),
                     r=[("ps", k_vc)], w=[("vc", par)])
                k_gc = inproj(512 + 128 * j)
                S.op("dve", lambda e, k=k_gc: e.tensor_tensor(out=U[:, HALO:HALO + T], in0=ps[k][:, 0:T],
                                                              in1=vc[par][:], op=ALU.mult),
                     r=[("ps", k_gc), ("vc", par)], w=[("uw", j)])
                if i == 0:
                    k_h = inproj(1024 + 128 * j, halo=True)
                    S.op("act", lambda e, k=k_h: e.copy(out=tmp16[:], in_=ps[k][:, 0:HALO]),
                         r=[("ps", k_h)], w=["tmp16"])
                    k_h2 = inproj(512 + 128 * j, halo=True)
                    S.op("dve", lambda e, k=k_h2: e.tensor_tensor(out=U[:, 0:HALO], in0=ps[k][:, 0:HALO],
                                                                  in1=tmp16[:], op=ALU.mult),
                         r=[("ps", k_h2), "tmp16"], w=[("uwh", j)])
                k_vp = inproj(1536 + 128 * j)
                S.op("act", lambda e, k=k_vp: e.copy(out=V[:, HALO:HALO + T], in_=ps[k][:, 0:T]),
                     r=[("ps", k_vp)], w=[("vpw", j)])
                if i == 0:
                    k_h3 = inproj(1536 + 128 * j, halo=True)
                    S.op("act", lambda e, k=k_h3: e.copy(out=V[:, 0:HALO], in_=ps[k][:, 0:HALO]),
                         r=[("ps", k_h3)], w=[("vpwh", j)])
                cw = C_CW + 3 * j
                S.op("act", lambda e: e.mul(out=cv[par][:], in_=U[:, HALO:HALO + T], mul=vec[:, cw + 2:cw + 3]),
                     r=[("uw", j), "vec"], w=[("cv", par)])
                S.op("dve", lambda e: e.scalar_tensor_tensor(out=cv[par][:], in0=U[:, HALO - 1:HALO - 1 + T],
                                                             scalar=vec[:, cw + 1:cw + 2], in1=cv[par][:],
                                                             op0=ALU.mult, op1=ALU.add),
                     r=[("uw", j), ("uwh", j), ("cv", par), "vec"], w=[("cv", par)])
                S.op("dve", lambda e: e.scalar_tensor_tensor(out=cv[par][:], in0=U[:, HALO - 2:HALO - 2 + T],
                                                             scalar=vec[:, cw:cw + 1], in1=cv[par][:],
                                                             op0=ALU.mult, op1=ALU.add),
                     r=[("uw", j), ("uwh", j), ("cv", par), "vec"], w=[("cv", par)])
                k_gb = inproj(128 * j)
                S.op("dve", lambda e, k=k_gb: e.tensor_tensor(out=ycatT[:, j, :], in0=ps[k][:, 0:T],
                                                              in1=cv[par][:], op=ALU.mult),
                     r=[("ps", k_gb), ("cv", par)], w=[("ycatT", j)])
                L = T + HALO
                VK = [("vpw", j), ("vpwh", j)]
                S.op("dve", lambda e: e.tensor_tensor(out=sA[par][:, 1:L], in0=V[:, 1:L], in1=V[:, 0:L - 1], op=ALU.add),
                     r=VK, w=[("sA", par)])
                cur, curk = sA, "sA"
                if j >= 1:
                    S.op("dve", lambda e: e.tensor_tensor(out=sB[par][:, 3:L], in0=sA[par][:, 3:L],
                                                          in1=sA[par][:, 1:L - 2], op=ALU.add),
                         r=[("sA", par)], w=[("sB", par)])
                    cur, curk = sB, "sB"
                if j >= 2:
                    S.op("dve", lambda e: e.tensor_tensor(out=sA[par][:, 7:L], in0=sB[par][:, 7:L],
                                                          in1=sB[par][:, 3:L - 4], op=ALU.add),
                         r=[("sB", par)], w=[("sA", par)])
                    cur, curk = sA, "sA"
                if j >= 3:
                    S.op("dve", lambda e: e.tensor_tensor(out=sB[par][:, 15:L], in0=sA[par][:, 15:L],
                                                          in1=sA[par][:, 7:L - 8], op=ALU.add),
                         r=[("sA", par)], w=[("sB", par)])
                    cur, curk = sB, "sB"
                win = float(WINS[j])
                S.op("dve", lambda e, cur=cur: e.scalar_tensor_tensor(
                    out=pbf[par][:], in0=cur[par][:, HALO:HALO + T], scalar=1.0 / win,
                    in1=V[:, HALO:HALO + T], op0=ALU.mult, op1=ALU.subtract),
                    r=[(curk, par), ("vpw", j)], w=[("pbf", par)])
                if i == 0:
                    ic = C_INV + 16 * j
                    S.op("dve", lambda e, cur=cur: e.tensor_tensor(
                        out=tmp16[:], in0=cur[par][:, HALO:2 * HALO], in1=vec[:, ic:ic + 16], op=ALU.mult),
                        r=[(curk, par), "vec"], w=["tmp16"])
                    S.op("dve", lambda e: e.tensor_tensor(
                        out=pbf[par][:, 0:HALO], in0=tmp16[:], in1=V[:, HALO:2 * HALO], op=ALU.subtract),
                        r=["tmp16", ("vpw", j), ("pbf", par)], w=[("pbf", par)])

            for j in range(4):
                XP.append(lambda j=j: do_chunk(j))
            XP.append(lambda: chunk_q(3))

            YK = [("ycatT", c) for c in range(8)]
            ln1_state = {}
            ln2_state = {}

            def sub1A(s):
                ri = rbc[0] % NR
                rbc[0] += 1
                r_ = rb[ri]
                for hf in range(2):
                    ka = rbank()

                    def mmo(e, ka=ka, hf=hf, s=s):
                        for kc in range(8):
                            ins = e.matmul(ps[ka][:, :], lhsT=ycatT[:, kc, s * 128:(s + 1) * 128],
                                           rhs=w_out_sb[:, kc, hf * 512:(hf + 1) * 512],
                                           start=(kc == 0), stop=(kc == 7))
                        return ins
                    S.op("pe", mmo, r=YK + ["w_out"], w=[("ps", ka)])
                    S.op("dve", lambda e, ka=ka, hf=hf, r_=r_: e.tensor_tensor(
                        out=r_[:, hf * 512:(hf + 1) * 512], in0=ps[ka][:, :], in1=g1p_bc[:, hf * 512:(hf + 1) * 512],
                        op=ALU.mult), r=[("ps", ka)] + G1PK, w=[("rb", ri, hf), ("rb", ri)])
                S.op("dve", lambda e, r_=r_, s=s: e.scalar_tensor_tensor(
                    out=r_[:], in0=xt[xs][:, s, :], scalar=ALPHA, in1=r_[:], op0=ALU.mult, op1=ALU.add),
                    r=[("xt", xs, s), ("rb", ri, 0), ("rb", ri, 1)], w=[("rb", ri)])
                par = lnc[0] % NLN
                lnc[0] += 1
                ln_stage1(r_, ("rb", ri), par)
                ln1_state[s] = (ri, par)

            def sub1A2(s):
                ri, par = ln1_state[s]
                r_ = rb[ri]
                m_ = mv[par]
                ln_stage2(par)
                MK = [("mv01", par), ("mv2", par)]
                up = s % 2
                S.op("dve", lambda e: e.tensor_scalar(out=ubf[up][:], in0=r_[:], scalar1=m_[:, 0:1], scalar2=m_[:, 2:3],
                                                      op0=ALU.subtract, op1=ALU.mult),
                     r=[("rb", ri)] + MK, w=[("ubf", up)])
                S.op("dve", lambda e: e.scalar_tensor_tensor(out=ub[:, s, :], in0=r_[:], scalar=m_[:, 0:1], in1=ag_bc[:],
                                                             op0=ALU.subtract, op1=ALU.mult),
                     r=[("rb", ri), "ag_bc"] + MK, w=[("ub", tp, s)])
                S.op("dve", lambda e: e.scalar_tensor_tensor(out=ub[:, s, :], in0=ub[:, s, :], scalar=m_[:, 2:3],
                                                             in1=ab_bc[:], op0=ALU.mult, op1=ALU.add),
                     r=[("ub", tp, s), "ab_bc"] + MK, w=[("ub", tp, s)])

            def sub1B(s):
                up = s % 2
                for kq in range(2):
                    k = rbank()

                    def tru(e, kq=kq, k=k, up=up, s=s):
                        for kk in range(4):
                            kc = kq * 4 + kk
                            ins = e.transpose(psb[k][:, kk * 128:(kk + 1) * 128],
                                              ubf[up][:, kc * 128:(kc + 1) * 128], idb[:])
                        return ins
                    S.op("pe", tru, r=[("ubf", up), "idb"], w=[("ps", k)])
                    for kk in range(4):
                        kc = kq * 4 + kk
                        S.op("act", lambda e, kk=kk, kc=kc, k=k, s=s: e.activation(
                            out=h2T[:, kc, s * 128:(s + 1) * 128], in_=psb[k][:, kk * 128:(kk + 1) * 128],
                            func=AF.Identity, bias=GB[:, 8 + kc:9 + kc], scale=GB[:, kc:kc + 1]),
                            r=[("ps", k), "G2", "B2"], w=[("h2T", tp, kc, s)])
            XP.append(lambda: sub1A(0))

            def p7():
                sub1A2(0)
                sub1A(1)
            XP.append(p7)

            def p8():
                sub1B(0)
                sub1A2(1)
            XP.append(p8)
            XP.append(lambda: sub1B(1))


            H2K = [("h2T", tp, kc, s) for kc in range(8) for s in range(NSUB)]

            def load_w(b):
                slot = (i * 8 + b) % NW
                S.op("sp", lambda e: e.dma_start(out=w1r[slot][:], in_=w1s[:, :, b * 512:(b + 1) * 512]),
                     r=[("w1s", b)], w=[("w1r", slot)], dma="ld_w1_%d" % slot)
                S.op("sp", lambda e: e.dma_start(out=w2r[slot][:], in_=w2s[:, 4 * b:4 * b + 4, :]),
                     r=[("w2s", b)], w=[("w2r", slot)], dma="ld_w2_%d" % slot)

            def stage_a(b):
                slot = (i * 8 + b) % NW
                ap_ = b % 2
                for c in range(4):
                    k = rbank()
                    rp = c % 2

                    def mma(e, k=k, c=c):
                        for kc in range(8):
                            ins = e.matmul(ps[k][:, 0:T], lhsT=w1r[slot][:, kc, c * 128:(c + 1) * 128],
                                           rhs=h2T[:, kc, :], start=(kc == 0), stop=(kc == 7))
                        return ins
                    S.op("pe", mma, r=[("w1r", slot)] + H2K, w=[("ps", k)])
                    S.op("act", lambda e, k=k, rp=rp: e.activation(out=rl[rp][:], in_=ps[k][:, 0:T], func=AF.Relu),
                         r=[("ps", k)], w=[("rl", rp)])
                    S.op("act", lambda e, rp=rp, c=c: e.activation(out=aT[ap_][:, c, :], in_=rl[rp][:], func=AF.Square),
                         r=[("rl", rp)], w=[("aT", ap_, c)])

            def stage_b(b):
                slot = (i * 8 + b) % NW
                ap_ = b % 2
                for s in range(NSUB):
                    for hf in range(2):
                        ka = 4 + (2 * s + hf) % 4

                        def mmb(e, ka=ka, s=s, hf=hf):
                            for c in range(4):
                                ins = e.matmul(ps[ka][:, :], lhsT=aT[ap_][:, c, s * 128:(s + 1) * 128],
                                               rhs=w2r[slot][:, c, hf * 512:(hf + 1) * 512],
                                               start=(b == 0 and c == 0), stop=(b == 7 and c == 3))
                            return ins
                        S.op("pe", mmb, r=[("w2r", slot)] + [("aT", ap_, c) for c in range(4)], w=[("ps", ka)])

            def y_first():
                load_w(0)
                stage_a(0)
            YP.append(y_first)

            def y_mid(b):
                load_w(b)
                stage_a(b)
                stage_b(b - 1)
            for b in range(1, 8):
                YP.append(lambda b=b: y_mid(b))
            def y8():
                stage_b(7)
                for s in range(NSUB):
                    do_sub2(s)
            YP.append(y8)

            def do_sub2(s):
                ri = rbc[0] % NR
                rbc[0] += 1
                r_ = rb[ri]
                for hf in range(2):
                    ka = 4 + (2 * s + hf) % 4
                    S.op("dve", lambda e, ka=ka, hf=hf, r_=r_: e.tensor_tensor(
                        out=r_[:, hf * 512:(hf + 1) * 512], in0=ps[ka][:, :], in1=g2p_bc[:, hf * 512:(hf + 1) * 512],
                        op=ALU.mult), r=[("ps", ka)] + G2PK, w=[("rb", ri, hf), ("rb", ri)])
                S.op("dve", lambda e, r_=r_, s=s: e.tensor_tensor(out=r_[:], in0=r_[:], in1=ub[:, s, :], op=ALU.add),
                     r=[("ub", tp, s), ("rb", ri, 0), ("rb", ri, 1)], w=[("rb", ri)])
                par = lnc[0] % NLN
                lnc[0] += 1
                ln_stage1(r_, ("rb", ri), par)
                ln2_state[s] = (ri, par)

            def do_sub2b(s):
                ri, par = ln2_state[s]
                r_ = rb[ri]
                m_ = mv[par]
                ln_stage2(par)
                MK = [("mv01", par), ("mv2", par)]
                S.op("dve", lambda e: e.scalar_tensor_tensor(out=r_[:], in0=r_[:], scalar=m_[:, 0:1], in1=ln2g_bc[:],
                                                             op0=ALU.subtract, op1=ALU.mult),
                     r=[("rb", ri), "ln2g_bc"] + MK, w=[("rb", ri)])
                S.op("dve", lambda e: e.scalar_tensor_tensor(out=r_[:], in0=r_[:], scalar=m_[:, 2:3], in1=ln2b_bc[:],
                                                             op0=ALU.mult, op1=ALU.add),
                     r=[("rb", ri), "ln2b_bc"] + MK, w=[("rb", ri)])
                row0 = i * T + s * 128
                S.op("pool", lambda e: e.dma_start(out=out[row0:row0 + 128, :], in_=r_[:]),
                     r=[("rb", ri)], w=[("out", row0)], dma="st_%d" % ri)

            def y_tail():
                for s in range(NSUB):
                    do_sub2b(s)
            YP.append(y_tail)
            return XP, YP, x_cast

        tiles = [make_tile(i) for i in range(NT)]
        tiles[0][2]()
        for rnd in range(NT + 1):
            XP = list(tiles[rnd][0]) if rnd < NT else []
            if rnd + 1 < NT:
                p8_, cast_ = XP[8], tiles[rnd + 1][2]
                XP[8] = lambda p8_=p8_, cast_=cast_: (cast_(), p8_())
            YP = tiles[rnd - 1][1] if rnd >= 1 else []
            n = max(len(XP), len(YP))
            for q in range(n):
                if q < len(XP) and XP[q] is not None:
                    XP[q]()
                if q < len(YP):
                    YP[q]()

        S.emit(nc)
    return nc


_CACHE = {}


def _layout_inputs(x, c, w_ada, b_ada, w_in, conv_w, w_pool, pool_scale, w_out, ln1_g, ln1_b,
                   w_mlp_in, w_mlp_out, ln2_g, ln2_b):
    f = np.float32
    x = np.asarray(x, f)
    c = np.asarray(c, f)
    shared = {
        "rows": np.ascontiguousarray(np.stack([np.asarray(ln1_g, f)[0], np.asarray(ln1_b, f)[0],
                                               np.asarray(ln2_g, f)[0], np.asarray(ln2_b, f)[0]])),
        "b_ada": np.ascontiguousarray(np.asarray(b_ada, f)[0]),
        "w_ada": np.ascontiguousarray(np.asarray(w_ada, f)[0]),
        "w_in": np.ascontiguousarray(np.asarray(w_in, f)[0]),
        "w_pool": np.ascontiguousarray(np.asarray(w_pool, f)[0]),
        "w_out": np.ascontiguousarray(np.asarray(w_out, f)[0]),
        "w1": np.ascontiguousarray(np.asarray(w_mlp_in, f)[0]),
        "w2": np.ascontiguousarray(np.asarray(w_mlp_out, f)[0]),
    }
    g1T = np.asarray(ln1_g, f)[0].reshape(8, 128).T
    b1T = np.asarray(ln1_b, f)[0].reshape(8, 128).T
    cw = np.asarray(conv_w, f)[0]
    cwT = cw.reshape(3, 4, 128).transpose(2, 1, 0).reshape(128, 12)
    psT = np.asarray(pool_scale, f)[0].reshape(4, 128).T
    in_maps = []
    for core in range(NCORES):
        b, half = core // 2, core % 2
        start = half * TOK
        xh = np.zeros((TOK + HALO, D), f)
        xh[HALO:] = x[b, start:start + TOK]
        if half:
            xh[:HALO] = x[b, start - HALO:start]
        vecs = np.zeros((128, NV), f)
        vecs[:, C_C:C_C + 8] = c[b].reshape(8, 128).T
        vecs[:, C_G1:C_G1 + 8] = g1T
        vecs[:, C_B1:C_B1 + 8] = b1T
        vecs[:, C_CW:C_CW + 12] = cwT
        vecs[:, C_PS:C_PS + 4] = psT
        vecs[:, C_FLAG] = 1.0 if half else 0.0
        for g, win in enumerate(WINS):
            for t in range(16):
                vecs[:, C_INV + 16 * g + t] = (1.0 / win) if half else (1.0 / min(t + 1, win))
        m = dict(shared)
        m["xh"] = xh
        m["vecs"] = vecs
        in_maps.append(m)
    return in_maps


def kernel(**inputs):
    if "nc" not in _CACHE:
        _CACHE["nc"] = build_program()
    nc = _CACHE["nc"]
    in_maps = _layout_inputs(**inputs)
    res = run_bass_kernel_spmd(nc, in_maps, core_ids=list(range(NCORES)))
    outp = np.empty((BATCH, SEQ, D), np.float32)
    for core in range(NCORES):
        b, half = core // 2, core % 2
        outp[b, half * TOK:(half + 1) * TOK] = res.results[core]["out"]
    return outp
```

```python
import numpy as np
from contextlib import ExitStack
import concourse.bass as bass
import concourse.mybir as mybir
from concourse.bass_utils import run_bass_kernel_spmd

F32 = mybir.dt.float32
BF16 = mybir.dt.bfloat16
ALU = mybir.AluOpType
AF = mybir.ActivationFunctionType

D = 1024
SEQ = 8192
BATCH = 4
NCORES = 8
TOK = 4096
HALO = 16
T = 256
NSUB = T // 128
NT = TOK // T
DFF = 4096
ALPHA = 2.0 ** 0.25
EPS = 1e-5
WINS = (2, 4, 8, 16)

C_C, C_G1, C_B1, C_CW, C_PS, C_FLAG, C_INV, NV = 0, 8, 16, 24, 36, 40, 41, 105

ENGS = ("pe", "act", "dve", "pool", "sp")


class _Op:
    __slots__ = ("idx", "eng", "fn", "deps", "dma", "dma_val", "sig", "sig_val")

    def __init__(self, idx, eng, fn, dma):
        self.idx = idx
        self.eng = eng
        self.fn = fn
        self.deps = {}
        self.dma = dma
        self.dma_val = 0
        self.sig = False
        self.sig_val = 0


class Sched:
    def __init__(self):
        self.ops = []
        self.last_w = {}
        self.readers = {}
        self.dma_cnt = {}

    def op(self, eng, fn, r=(), w=(), dma=None):
        o = _Op(len(self.ops), eng, fn, dma)
        for k in r:
            lw = self.last_w.get(k)
            if lw is not None:
                o.deps[lw] = True
        for k in w:
            lw = self.last_w.get(k)
            if lw is not None and lw not in o.deps:
                o.deps[lw] = False
            for rd in self.readers.get(k, ()):
                if rd not in o.deps:
                    o.deps[rd] = False
        for k in r:
            self.readers.setdefault(k, []).append(o.idx)
        for k in w:
            self.last_w[k] = o.idx
            self.readers[k] = []
        if dma is not None:
            c = self.dma_cnt.get(dma, 0) + 1
            self.dma_cnt[dma] = c
            o.dma_val = 16 * c
        self.ops.append(o)
        return o

    def emit(self, nc, final_eng="sp"):
        ops = self.ops
        need = []
        for o in ops:
            lst = []
            for d, is_raw in o.deps.items():
                p = ops[d]
                if p.dma is not None:
                    lst.append(p)
                elif p.eng == o.eng and o.dma is None:
                    if is_raw and o.eng != "pe":
                        lst.append(p)
                else:
                    lst.append(p)
            need.append(lst)
            for p in lst:
                if p.dma is None:
                    p.sig = True
        cnt = {e: 0 for e in ENGS}
        for o in ops:
            if o.sig:
                cnt[o.eng] += 1
                o.sig_val = cnt[o.eng]
        dma_keys = sorted(self.dma_cnt.keys())
        with ExitStack() as st:
            esem = {e: st.enter_context(nc.semaphore("sem_" + e)) for e in ENGS}
            dsem = {k: st.enter_context(nc.semaphore("dsem_" + str(k))) for k in dma_keys}
            block = st.enter_context(nc.Block())
            per_eng = {e: [o for o in ops if o.eng == e] for e in ENGS}

            def run(eng_name, eng):
                waited = {}
                for o in per_eng[eng_name]:
                    for p in need[o.idx]:
                        if p.dma is not None:
                            s, v, key = dsem[p.dma], p.dma_val, ("d", p.dma)
                        else:
                            s, v, key = esem[p.eng], p.sig_val, ("e", p.eng)
                        if waited.get(key, 0) >= v:
                            continue
                        waited[key] = v
                        eng.wait_ge(s, v)
                    ins = o.fn(eng)
                    if o.dma is not None:
                        ins.then_inc(dsem[o.dma], 16)
                    elif o.sig:
                        ins.then_inc(esem[o.eng], 1)
                if eng_name == final_eng:
                    for k in dma_keys:
                        eng.wait_ge(dsem[k], 16 * self.dma_cnt[k])

            @block.tensor
            def _(e):
                run("pe", e)

            @block.scalar
            def _(e):
                run("act", e)

            @block.vector
            def _(e):
                run("dve", e)

            @block.gpsimd
            def _(e):
                run("pool", e)

            @block.sync
            def _(e):
                run("sp", e)


def build_program():
    nc = bass.Bass("TRN2", target_bir_lowering=False)

    def din(name, shape, dt=F32):
        return nc.dram_tensor(name, shape, dt, kind="ExternalInput").ap()

    xh = din("xh", [TOK + HALO, D])
    vecs = din("vecs", [128, NV])
    rows = din("rows", [4, D])
    b_ada = din("b_ada", [6 * D])
    w_ada = din("w_ada", [D, 6 * D])
    w_in = din("w_in", [D, 2048])
    w_pool = din("w_pool", [4, 128, 128])
    w_out = din("w_out", [D, D])
    w1 = din("w1", [D, DFF])
    w2 = din("w2", [DFF, D])
    out = nc.dram_tensor("out", [TOK, D], F32, kind="ExternalOutput").ap()
    w1s = nc.dram_tensor("w1s", [128, 8, DFF], BF16, kind="Internal").ap()
    w2s = nc.dram_tensor("w2s", [128, 32, D], BF16, kind="Internal").ap()

    S = Sched()
    with ExitStack() as st:
        def sb(name, shape, dt=F32):
            return st.enter_context(nc.sbuf_tensor(name, shape, dt))

        ps = [st.enter_context(nc.psum_tensor("ps%d" % k, [128, 512], F32)) for k in range(8)]
        psb = [p.bitcast(BF16) for p in ps]

        vec = sb("vec", [128, NV])
        idf = sb("idf", [128, 128])
        idb = sb("idb", [128, 128], BF16)
        ones_f = sb("ones_f", [128, 128])
        cond = sb("cond", [128, 8])
        condrep = sb("condrep", [128, 8, 128], BF16)
        w_in_sb = sb("w_in_sb", [128, 8, 2048], BF16)
        w_out_sb = sb("w_out_sb", [128, 8, D], BF16)
        w_pool_sb = sb("w_pool_sb", [128, 4, 128], BF16)
        g1p_bc = sb("g1p_bc", [128, D])
        g2p_bc = sb("g2p_bc", [128, D])
        ag_bc = sb("ag_bc", [128, D])
        ab_bc = sb("ab_bc", [128, D])
        ln2g_bc = sb("ln2g_bc", [128, D])
        ln2b_bc = sb("ln2b_bc", [128, D])
        modT = sb("modT", [128, 32])
        GB = sb("GB", [128, 24])
        xt = [sb("xt%d" % i, [128, NSUB, D]) for i in range(2)]
        xbf = sb("xbf", [128, NSUB, D], BF16)
        h1T = sb("h1T", [128, 8, T], BF16)
        h1halo = sb("h1halo", [128, 8, HALO], BF16)
        vc = [sb("vc%d" % i, [128, T]) for i in range(2)]
        uw = [sb("uw%d" % i, [128, T + HALO]) for i in range(4)]
        cv = [sb("cv%d" % i, [128, T]) for i in range(2)]
        vpw = [sb("vpw%d" % i, [128, T + HALO]) for i in range(4)]
        sA = [sb("sA%d" % i, [128, T + HALO]) for i in range(2)]
        sB = [sb("sB%d" % i, [128, T + HALO]) for i in range(2)]
        pbf = [sb("pbf%d" % i, [128, T], BF16) for i in range(2)]
        tmp16 = sb("tmp16", [128, HALO])
        ycatT = sb("ycatT", [128, 8, T], BF16)
        NR = 4
        rb = [sb("rb%d" % i, [128, D]) for i in range(NR)]
        ub2 = [sb("ub%d" % i, [128, NSUB, D]) for i in range(2)]
        ubf = [sb("ubf%d" % i, [128, D], BF16) for i in range(2)]
        h2T2 = [sb("h2T%d" % i, [128, 8, T], BF16) for i in range(2)]
        NW = 2
        w1r = [sb("w1r%d" % i, [128, 8, 512], BF16) for i in range(NW)]
        w2r = [sb("w2r%d" % i, [128, 4, D], BF16) for i in range(NW)]
        rl = [sb("rl%d" % i, [128, T]) for i in range(2)]
        aT = [sb("aT%d" % i, [128, 4, T], BF16) for i in range(2)]
        NLN = 4
        stats = [sb("stats%d" % i, [128, 12]) for i in range(NLN)]
        mv = [sb("mv%d" % i, [128, 8]) for i in range(NLN)]

        wada_r = [w1r[i] for i in range(2)]
        bada_r = [vc[i] for i in range(2)]
        tmpbc = [cv[i] for i in range(2)]
        rot = [0]

        def rbank():
            k = rot[0]
            rot[0] = (k + 1) % 4
            return k

        S.op("sp", lambda e: e.dma_start(out=vec[:], in_=vecs), w=["vec"], dma="ld_vec")
        S.op("pool", lambda e: e.memset(idf[:], 0.0), w=["idf"])
        S.op("pool", lambda e: e.affine_select(out=idf[:], in_=idf[:], pattern=[[-1, 128]],
                                               compare_op=ALU.not_equal, fill=1.0, base=0,
                                               channel_multiplier=1), r=["idf"], w=["idf"])
        S.op("pool", lambda e: e.tensor_copy(out=idb[:], in_=idf[:]), r=["idf"], w=["idb"])
        S.op("pool", lambda e: e.memset(ones_f[:], 1.0), w=["ones_f"])
        S.op("act", lambda e: e.activation(out=cond[:], in_=vec[:, C_C:C_C + 8], func=AF.Silu),
             r=["vec"], w=["cond"])
        for kc in range(8):
            S.op("dve", lambda e, kc=kc: e.tensor_scalar(out=condrep[:, kc, :], in0=ones_f[:],
                                                         scalar1=cond[:, kc:kc + 1], scalar2=None,
                                                         op0=ALU.mult),
                 r=["ones_f", "cond"], w=["condrep"])

        for i, (dst, name) in enumerate(((ag_bc, "ag_bc"), (ab_bc, "ab_bc"),
                                         (ln2g_bc, "ln2g_bc"), (ln2b_bc, "ln2b_bc"))):
            S.op("sp", lambda e, dst=dst, i=i: e.dma_start(out=dst[:], in_=rows[i].partition_broadcast(128)),
                 w=[name], dma="ld_" + name)
        S.op("act", lambda e: e.mul(out=ag_bc[:], in_=ag_bc[:], mul=ALPHA), r=["ag_bc"], w=["ag_bc"])
        S.op("act", lambda e: e.mul(out=ab_bc[:], in_=ab_bc[:], mul=ALPHA), r=["ab_bc"], w=["ab_bc"])

        def cast_dma(dst, src, wkeys, key):
            S.op("pool", lambda e: e.dma_start(out=dst, in_=src), w=wkeys, dma=key)

        def mod_block(blk):
            slot = blk % 2
            col0 = blk * 256
            vi, off = col0 // D, col0 % D
            cast_dma(wada_r[slot][:, :, 0:256], w_ada[:, col0:col0 + 256].rearrange("(kc p) n -> p kc n", p=128),
                     [("w1r", slot)], "wada%d" % slot)
            S.op("sp", lambda e: e.dma_start(out=bada_r[slot][:],
                                             in_=b_ada[col0:col0 + 256].partition_broadcast(128)),
                 w=[("vc", slot)], dma="bada%d" % slot)
            k = rbank()

            def mm(e):
                for kc in range(8):
                    ins = e.matmul(ps[k][:, 0:256], lhsT=condrep[:, kc, :], rhs=wada_r[slot][:, kc, 0:256],
                                   start=(kc == 0), stop=(kc == 7))
                return ins
            S.op("pe", mm, r=[("w1r", slot), "condrep"], w=[("ps", k)])
            if vi in (2, 5):
                dst, name = (g1p_bc, "g1p_bc") if vi == 2 else (g2p_bc, "g2p_bc")
                S.op("dve", lambda e: e.scalar_tensor_tensor(out=dst[:, off:off + 256], in0=ps[k][:, 0:256],
                                                             scalar=1.0, in1=bada_r[slot][:],
                                                             op0=ALU.add, op1=ALU.add),
                     r=[("ps", k), ("vc", slot)], w=[(name, off)])
            else:
                addc = 1.0 if vi in (1, 4) else 0.0
                S.op("dve", lambda e: e.scalar_tensor_tensor(out=tmpbc[slot][:], in0=ps[k][:, 0:256],
                                                             scalar=addc, in1=bada_r[slot][:],
                                                             op0=ALU.add, op1=ALU.add),
                     r=[("ps", k), ("vc", slot)], w=[("cv", slot)])
                vslot = {0: 0, 1: 1, 3: 2, 4: 3}[vi]

                def mmT(e):
                    for c in range(2):
                        col = vslot * 8 + off // 128 + c
                        ins = e.matmul(ps[7][:, col:col + 1], lhsT=tmpbc[slot][0:1, c * 128:(c + 1) * 128],
                                       rhs=ones_f[0:1, 0:1], start=True, stop=True)
                    return ins
                S.op("pe", mmT, r=[("cv", slot), "ones_f"], w=[("ps", 7)])

        for blk in range(8):
            mod_block(blk)
        for h in range(2):
            cast_dma(w_in_sb[:, 4 * h:4 * h + 4, :],
                     w_in[512 * h:512 * (h + 1), :].rearrange("(kc p) n -> p kc n", p=128),
                     [("w_in", h)], "c_w_in%d" % h)
        cast_dma(w_pool_sb[:], w_pool.rearrange("g c d -> c g d"), ["w_pool"], "c_w_pool")
        cast_dma(w_out_sb[:], w_out.rearrange("(kc p) n -> p kc n", p=128), ["w_out"], "c_w_out")
        for blk in range(8, 24):
            mod_block(blk)
        for b in range(8):
            cast_dma(w1s[:, :, b * 512:(b + 1) * 512],
                     w1[:, b * 512:(b + 1) * 512].rearrange("(kc p) n -> p kc n", p=128),
                     [("w1s", b)], "c_w1_%d" % b)
            cast_dma(w2s[:, 4 * b:4 * b + 4, :],
                     w2[b * 512:(b + 1) * 512, :].rearrange("(c p) n -> p c n", p=128),
                     [("w2s", b)], "c_w2_%d" % b)
        S.op("act", lambda e: e.copy(out=modT[:], in_=ps[7][:, 0:32]), r=[("ps", 7)], w=["modT"])
        S.op("dve", lambda e: e.tensor_tensor(out=GB[:, 0:8], in0=vec[:, C_G1:C_G1 + 8], in1=modT[:, 24:32],
                                              op=ALU.mult), r=["vec", "modT"], w=["G2"])
        S.op("dve", lambda e: e.tensor_tensor(out=GB[:, 16:24], in0=vec[:, C_B1:C_B1 + 8], in1=modT[:, 24:32],
                                              op=ALU.mult), r=["vec", "modT"], w=["B2t"])
        S.op("dve", lambda e: e.tensor_tensor(out=GB[:, 8:16], in0=GB[:, 16:24], in1=modT[:, 16:24],
                                              op=ALU.add), r=["B2t", "modT"], w=["B2"])
        G1PK = [("g1p_bc", o) for o in range(0, D, 256)]
        G2PK = [("g2p_bc", o) for o in range(0, D, 256)]

        def ln_stage1(src_ap, skey, par):
            st_, m_ = stats[par], mv[par]
            S.op("dve", lambda e: e.bn_stats(out=st_[:, 0:6], in_=src_ap[:, 0:512]), r=[skey], w=[("st0", par)])
            S.op("dve", lambda e: e.bn_stats(out=st_[:, 6:12], in_=src_ap[:, 512:1024]), r=[skey], w=[("st1", par)])
            S.op("dve", lambda e: e.bn_aggr(out=m_[:, 0:2], in_=st_[:, 0:12]),
                 r=[("st0", par), ("st1", par)], w=[("mv01", par)])
            S.op("dve", lambda e: e.tensor_scalar(out=m_[:, 4:5], in0=m_[:, 1:2], scalar1=EPS, scalar2=None,
                                                  op0=ALU.add), r=[("mv01", par)], w=[("mv4", par)])

        def ln_stage2(par):
            m_ = mv[par]
            S.op("act", lambda e: e.activation(out=m_[:, 5:6], in_=m_[:, 4:5], func=AF.Sqrt),
                 r=[("mv4", par)], w=[("mv5", par)])
            S.op("dve", lambda e: e.reciprocal(out=m_[:, 2:3], in_=m_[:, 5:6]), r=[("mv5", par)], w=[("mv2", par)])

        lnc = [0]
        rbc = [0]

        def load_x(i):
            slot = i % 2
            S.op("sp", lambda e: e.dma_start(
                out=xt[slot][:], in_=xh[HALO + i * T:HALO + (i + 1) * T, :].rearrange("(s p) f -> p s f", p=128)),
                w=[("xt", slot, s) for s in range(NSUB)], dma="ld_x%d" % slot)

        load_x(0)

        def make_tile(i):
            xs = i % 2
            tp = i % 2
            ub = ub2[tp]
            h2T = h2T2[tp]
            XP, YP = [], []

            def x_cast():
              for s in range(NSUB):
                S.op("act", lambda e, s=s: e.copy(out=xbf[:, s, :], in_=xt[xs][:, s, :]),
                     r=[("xt", xs, s)], w=[("xbf", s)])

            def x_front():
              for kq in range(2):
                k = rbank()

                def trx(e, kq=kq, k=k):
                    for kk in range(4):
                        kc = kq * 4 + kk
                        for s in range(NSUB):
                            ins = e.transpose(psb[k][:, kk * T + s * 128:kk * T + (s + 1) * 128],
                                              xbf[:, s, kc * 128:(kc + 1) * 128], idb[:])
                    return ins
                S.op("pe", trx, r=[("xbf", s) for s in range(NSUB)] + ["idb"], w=[("ps", k)])
                for kk in range(4):
                    kc = kq * 4 + kk
                    S.op("act", lambda e, kk=kk, kc=kc, k=k: e.activation(
                        out=h1T[:, kc, :], in_=psb[k][:, kk * T:(kk + 1) * T], func=AF.Identity,
                        bias=modT[:, kc:kc + 1], scale=modT[:, 8 + kc:9 + kc]),
                        r=[("ps", k), "modT"], w=[("h1T", kc)])
              if i == 0:
                S.op("sp", lambda e: e.dma_start(out=rb[0][0:HALO, :], in_=xh[0:HALO, :]), w=[("rb", 0)], dma="ld_halo")
                S.op("act", lambda e: e.copy(out=ubf[0][0:HALO, :], in_=rb[0][0:HALO, :]), r=[("rb", 0)], w=[("ubf", 0)])
                k = rbank()

                def trh(e, k=k):
                    for kc in range(8):
                        ins = e.transpose(psb[k][:, kc * HALO:(kc + 1) * HALO],
                                          ubf[0][0:HALO, kc * 128:(kc + 1) * 128], idb[0:HALO, 0:HALO])
                    return ins
                S.op("pe", trh, r=[("ubf", 0), "idb"], w=[("ps", k)])
                for kc in range(8):
                    S.op("dve", lambda e, kc=kc, k=k: e.tensor_scalar(
                        out=h1halo[:, kc, :], in0=psb[k][:, kc * HALO:(kc + 1) * HALO],
                        scalar1=modT[:, 8 + kc:9 + kc], scalar2=modT[:, kc:kc + 1],
                        op0=ALU.mult, op1=ALU.add), r=[("ps", k), "modT"], w=[("h1halo", kc)])
                    S.op("dve", lambda e, kc=kc: e.tensor_scalar(
                        out=h1halo[:, kc, :], in0=h1halo[:, kc, :], scalar1=vec[:, C_FLAG:C_FLAG + 1],
                        scalar2=None, op0=ALU.mult), r=[("h1halo", kc), "vec"], w=[("h1halo", kc)])
              if i + 1 < NT:
                load_x(i + 1)
            XP.append(x_front)

            H1K = [("h1T", kc) for kc in range(8)]
            H1HK = [("h1halo", kc) for kc in range(8)]
            WINK = [("w_in", 0), ("w_in", 1)]

            def inproj(col0, halo=False):
                k = rbank()
                n = HALO if halo else T

                def mm(e):
                    for kc in range(8):
                        rhs = h1halo[:, kc, :] if halo else h1T[:, kc, :]
                        ins = e.matmul(ps[k][:, 0:n], lhsT=w_in_sb[:, kc, col0:col0 + 128], rhs=rhs,
                                       start=(kc == 0), stop=(kc == 7))
                    return ins
                S.op("pe", mm, r=WINK + (H1HK if halo else H1K), w=[("ps", k)])
                return k

            def chunk_q(j):
                par = j % 2
                k_q = rbank()
                S.op("pe", lambda e, k=k_q: e.matmul(ps[k][:, 0:T], lhsT=w_pool_sb[:, j, :], rhs=pbf[par][:],
                                                     start=True, stop=True),
                     r=["w_pool", ("pbf", par)], w=[("ps", k_q)])
                S.op("act", lambda e, k=k_q: e.mul(out=ycatT[:, 4 + j, :], in_=ps[k][:, 0:T],
                                                   mul=vec[:, C_PS + j:C_PS + j + 1]),
                     r=[("ps", k_q), "vec"], w=[("ycatT", 4 + j)])

            def do_chunk(j):
                par = j % 2
                if j > 0:
                    chunk_q(j - 1)
                U, V = uw[j], vpw[j]
                if i > 0:
                    S.op("act", lambda e: e.copy(out=U[:, 0:HALO], in_=U[:, T:T + HALO]),
                         r=[("uw", j)], w=[("uwh", j)])
                    S.op("act", lambda e: e.copy(out=V[:, 0:HALO], in_=V[:, T:T + HALO]),
                         r=[("vpw", j)], w=[("vpwh", j)])
                k_vc = inproj(1024 + 128 * j)
                S.op("act", lambda e, k=k_vc: e.copy(out=vc[par][:], in_=ps[k][:, 0:T]),
                     r=[("ps", k_vc)], w=[("vc", par)])
                k_gc = inproj(512 + 128 * j)
                S.op("dve", lambda e, k=k_gc: e.tensor_tensor(out=U[:, HALO:HALO + T], in0=ps[k][:, 0:T],
                                                              in1=vc[par][:], op=ALU.mult),
                     r=[("ps", k_gc), ("vc", par)], w=[("uw", j)])
                if i == 0:
                    k_h = inproj(1024 + 128 * j, halo=True)
                    S.op("act", lambda e, k=k_h: e.copy(out=tmp16[:], in_=ps[k][:, 0:HALO]),
                         r=[("ps", k_h)], w=["tmp16"])
                    k_h2 = inproj(512 + 128 * j, halo=True)
                    S.op("dve", lambda e, k=k_h2: e.tensor_tensor(out=U[:, 0:HALO], in0=ps[k][:, 0:HALO],
                                                                  in1=tmp16[:], op=ALU.mult),
                         r=[("ps", k_h2), "tmp16"], w=[("uwh", j)])
                k_vp = inproj(1536 + 128 * j)
                S.op("act", lambda e, k=k_vp: e.copy(out=V[:, HALO:HALO + T], in_=ps[k][:, 0:T]),
                     r=[("ps", k_vp)], w=[("vpw", j)])
                if i == 0:
                    k_h3 = inproj(1536 + 128 * j, halo=True)
                    S.op("act", lambda e, k=k_h3: e.copy(out=V[:, 0:HALO], in_=ps[k][:, 0:HALO]),
                         r=[("ps", k_h3)], w=[("vpwh", j)])
                cw = C_CW + 3 * j
                S.op("act", lambda e: e.mul(out=cv[par][:], in_=U[:, HALO:HALO + T], mul=vec[:, cw + 2:cw + 3]),
                     r=[("uw", j), "vec"], w=[("cv", par)])
                S.op("dve", lambda e: e.scalar_tensor_tensor(out=cv[par][:], in0=U[:, HALO - 1:HALO - 1 + T],
                                                             scalar=vec[:, cw + 1:cw + 2], in1=cv[par][:],
                                                             op0=ALU.mult, op1=ALU.add),
                     r=[("uw", j), ("uwh", j), ("cv", par), "vec"], w=[("cv", par)])
                S.op("dve", lambda e: e.scalar_tensor_tensor(out=cv[par][:], in0=U[:, HALO - 2:HALO - 2 + T],
                                                             scalar=vec[:, cw:cw + 1], in1=cv[par][:],
                                                             op0=ALU.mult, op1=ALU.add),
                     r=[("uw", j), ("uwh", j), ("cv", par), "vec"], w=[("cv", par)])
                k_gb = inproj(128 * j)
                S.op("dve", lambda e, k=k_gb: e.tensor_tensor(out=ycatT[:, j, :], in0=ps[k][:, 0:T],
                                                              in1=cv[par][:], op=ALU.mult),
                     r=[("ps", k_gb), ("cv", par)], w=[("ycatT", j)])
                L = T + HALO
                VK = [("vpw", j), ("vpwh", j)]
                S.op("dve", lambda e: e.tensor_tensor(out=sA[par][:, 1:L], in0=V[:, 1:L], in1=V[:, 0:L - 1], op=ALU.add),
                     r=VK, w=[("sA", par)])
                cur, curk = sA, "sA"
                if j >= 1:
                    S.op("dve", lambda e: e.tensor_tensor(out=sB[par][:, 3:L], in0=sA[par][:, 3:L],
                                                          in1=sA[par][:, 1:L - 2], op=ALU.add),
                         r=[("sA", par)], w=[("sB", par)])
                    cur, curk = sB, "sB"
                if j >= 2:
                    S.op("dve", lambda e: e.tensor_tensor(out=sA[par][:, 7:L], in0=sB[par][:, 7:L],
                                                          in1=sB[par][:, 3:L - 4], op=ALU.add),
                         r=[("sB", par)], w=[("sA", par)])
                    cur, curk = sA, "sA"
                if j >= 3:
                    S.op("dve", lambda e: e.tensor_tensor(out=sB[par][:, 15:L], in0=sA[par][:, 15:L],
                                                          in1=sA[par][:, 7:L - 8], op=ALU.add),
                         r=[("sA", par)], w=[("sB", par)])
                    cur, curk = sB, "sB"
                win = float(WINS[j])
                S.op("dve", lambda e, cur=cur: e.scalar_tensor_tensor(
                    out=pbf[par][:], in0=cur[par][:, HALO:HALO + T], scalar=1.0 / win,
                    in1=V[:, HALO:HALO + T], op0=ALU.mult, op1=ALU.subtract),
                    r=[(curk, par), ("vpw", j)], w=[("pbf", par)])
                if i == 0:
                    ic = C_INV + 16 * j
                    S.op("dve", lambda e, cur=cur: e.tensor_tensor(
                        out=tmp16[:], in0=cur[par][:, HALO:2 * HALO], in1=vec[:, ic:ic + 16], op=ALU.mult),
                        r=[(curk, par), "vec"], w=["tmp16"])
                    S.op("dve", lambda e: e.tensor_tensor(
                        out=pbf[par][:, 0:HALO], in0=tmp16[:], in1=V[:, HALO:2 * HALO], op=ALU.subtract),
                        r=["tmp16", ("vpw", j), ("pbf", par)], w=[("pbf", par)])

            for j in range(4):
                XP.append(lambda j=j: do_chunk(j))
            XP.append(lambda: chunk_q(3))

            YK = [("ycatT", c) for c in range(8)]
            ln1_state = {}
            ln2_state = {}

            def sub1A(s):
                ri = rbc[0] % NR
                rbc[0] += 1
                r_ = rb[ri]
                for hf in range(2):
                    ka = rbank()

                    def mmo(e, ka=ka, hf=hf, s=s):
                        for kc in range(8):
                            ins = e.matmul(ps[ka][:, :], lhsT=ycatT[:, kc, s * 128:(s + 1) * 128],
                                           rhs=w_out_sb[:, kc, hf * 512:(hf + 1) * 512],
                                           start=(kc == 0), stop=(kc == 7))
                        return ins
                    S.op("pe", mmo, r=YK + ["w_out"], w=[("ps", ka)])
                    S.op("dve", lambda e, ka=ka, hf=hf, r_=r_: e.tensor_tensor(
                        out=r_[:, hf * 512:(hf + 1) * 512], in0=ps[ka][:, :], in1=g1p_bc[:, hf * 512:(hf + 1) * 512],
                        op=ALU.mult), r=[("ps", ka)] + G1PK, w=[("rb", ri, hf), ("rb", ri)])
                S.op("dve", lambda e, r_=r_, s=s: e.scalar_tensor_tensor(
                    out=r_[:], in0=xt[xs][:, s, :], scalar=ALPHA, in1=r_[:], op0=ALU.mult, op1=ALU.add),
                    r=[("xt", xs, s), ("rb", ri, 0), ("rb", ri, 1)], w=[("rb", ri)])
                par = lnc[0] % NLN
                lnc[0] += 1
                ln_stage1(r_, ("rb", ri), par)
                ln1_state[s] = (ri, par)

            def sub1A2(s):
                ri, par = ln1_state[s]
                r_ = rb[ri]
                m_ = mv[par]
                ln_stage2(par)
                MK = [("mv01", par), ("mv2", par)]
                up = s % 2
                S.op("dve", lambda e: e.tensor_scalar(out=ubf[up][:], in0=r_[:], scalar1=m_[:, 0:1], scalar2=m_[:, 2:3],
                                                      op0=ALU.subtract, op1=ALU.mult),
                     r=[("rb", ri)] + MK, w=[("ubf", up)])
                S.op("dve", lambda e: e.scalar_tensor_tensor(out=ub[:, s, :], in0=r_[:], scalar=m_[:, 0:1], in1=ag_bc[:],
                                                             op0=ALU.subtract, op1=ALU.mult),
                     r=[("rb", ri), "ag_bc"] + MK, w=[("ub", tp, s)])
                S.op("dve", lambda e: e.scalar_tensor_tensor(out=ub[:, s, :], in0=ub[:, s, :], scalar=m_[:, 2:3],
                                                             in1=ab_bc[:], op0=ALU.mult, op1=ALU.add),
                     r=[("ub", tp, s), "ab_bc"] + MK, w=[("ub", tp, s)])

            def sub1B(s):
                up = s % 2
                for kq in range(2):
                    k = rbank()

                    def tru(e, kq=kq, k=k, up=up, s=s):
                        for kk in range(4):
                            kc = kq * 4 + kk
                            ins = e.transpose(psb[k][:, kk * 128:(kk + 1) * 128],
                                              ubf[up][:, kc * 128:(kc + 1) * 128], idb[:])
                        return ins
                    S.op("pe", tru, r=[("ubf", up), "idb"], w=[("ps", k)])
                    for kk in range(4):
                        kc = kq * 4 + kk
                        S.op("act", lambda e, kk=kk, kc=kc, k=k, s=s: e.activation(
                            out=h2T[:, kc, s * 128:(s + 1) * 128], in_=psb[k][:, kk * 128:(kk + 1) * 128],
                            func=AF.Identity, bias=GB[:, 8 + kc:9 + kc], scale=GB[:, kc:kc + 1]),
                            r=[("ps", k), "G2", "B2"], w=[("h2T", tp, kc, s)])
            XP.append(lambda: sub1A(0))

            def p7():
                sub1A2(0)
                sub1A(1)
            XP.append(p7)

            def p8():
                sub1B(0)
                sub1A2(1)
            XP.append(p8)
            XP.append(lambda: sub1B(1))


            H2K = [("h2T", tp, kc, s) for kc in range(8) for s in range(NSUB)]

            def load_w(b):
                slot = (i * 8 + b) % NW
                S.op("sp", lambda e: e.dma_start(out=w1r[slot][:], in_=w1s[:, :, b * 512:(b + 1) * 512]),
                     r=[("w1s", b)], w=[("w1r", slot)], dma="ld_w1_%d" % slot)
                S.op("sp", lambda e: e.dma_start(out=w2r[slot][:], in_=w2s[:, 4 * b:4 * b + 4, :]),
                     r=[("w2s", b)], w=[("w2r", slot)], dma="ld_w2_%d" % slot)

            def stage_a(b):
                slot = (i * 8 + b) % NW
                ap_ = b % 2
                for c in range(4):
                    k = rbank()
                    rp = c % 2

                    def mma(e, k=k, c=c):
                        for kc in range(8):
                            ins = e.matmul(ps[k][:, 0:T], lhsT=w1r[slot][:, kc, c * 128:(c + 1) * 128],
                                           rhs=h2T[:, kc, :], start=(kc == 0), stop=(kc == 7))
                        return ins
                    S.op("pe", mma, r=[("w1r", slot)] + H2K, w=[("ps", k)])
                    S.op("act", lambda e, k=k, rp=rp: e.activation(out=rl[rp][:], in_=ps[k][:, 0:T], func=AF.Relu),
                         r=[("ps", k)], w=[("rl", rp)])
                    S.op("act", lambda e, rp=rp, c=c: e.activation(out=aT[ap_][:, c, :], in_=rl[rp][:], func=AF.Square),
                         r=[("rl", rp)], w=[("aT", ap_, c)])

            def stage_b(b):
                slot = (i * 8 + b) % NW
                ap_ = b % 2
                for s in range(NSUB):
                    for hf in range(2):
                        ka = 4 + (2 * s + hf) % 4

                        def mmb(e, ka=ka, s=s, hf=hf):
                            for c in range(4):
                                ins = e.matmul(ps[ka][:, :], lhsT=aT[ap_][:, c, s * 128:(s + 1) * 128],
                                               rhs=w2r[slot][:, c, hf * 512:(hf + 1) * 512],
                                               start=(b == 0 and c == 0), stop=(b == 7 and c == 3))
                            return ins
                        S.op("pe", mmb, r=[("w2r", slot)] + [("aT", ap_, c) for c in range(4)], w=[("ps", ka)])

            def y_first():
                load_w(0)
                stage_a(0)
            YP.append(y_first)

            def y_mid(b):
                load_w(b)
                stage_a(b)
                stage_b(b - 1)
            for b in range(1, 8):
                YP.append(lambda b=b: y_mid(b))
            YP.append(lambda: stage_b(7))

            def y_ln2a():
                for s in range(NSUB):
                    do_sub2(s)

            def do_sub2(s):
                ri = rbc[0] % NR
                rbc[0] += 1
                r_ = rb[ri]
                for hf in range(2):
                    ka = 4 + (2 * s + hf) % 4
                    S.op("dve", lambda e, ka=ka, hf=hf, r_=r_: e.tensor_tensor(
                        out=r_[:, hf * 512:(hf + 1) * 512], in0=ps[ka][:, :], in1=g2p_bc[:, hf * 512:(hf + 1) * 512],
                        op=ALU.mult), r=[("ps", ka)] + G2PK, w=[("rb", ri, hf), ("rb", ri)])
                S.op("dve", lambda e, r_=r_, s=s: e.tensor_tensor(out=r_[:], in0=r_[:], in1=ub[:, s, :], op=ALU.add),
                     r=[("ub", tp, s), ("rb", ri, 0), ("rb", ri, 1)], w=[("rb", ri)])
                par = lnc[0] % NLN
                lnc[0] += 1
                ln_stage1(r_, ("rb", ri), par)
                ln2_state[s] = (ri, par)

            def do_sub2b(s):
                ri, par = ln2_state[s]
                r_ = rb[ri]
                m_ = mv[par]
                ln_stage2(par)
                MK = [("mv01", par), ("mv2", par)]
                S.op("dve", lambda e: e.scalar_tensor_tensor(out=r_[:], in0=r_[:], scalar=m_[:, 0:1], in1=ln2g_bc[:],
                                                             op0=ALU.subtract, op1=ALU.mult),
                     r=[("rb", ri), "ln2g_bc"] + MK, w=[("rb", ri)])
                S.op("dve", lambda e: e.scalar_tensor_tensor(out=r_[:], in0=r_[:], scalar=m_[:, 2:3], in1=ln2b_bc[:],
                                                             op0=ALU.mult, op1=ALU.add),
                     r=[("rb", ri), "ln2b_bc"] + MK, w=[("rb", ri)])
                row0 = i * T + s * 128
                S.op("pool", lambda e: e.dma_start(out=out[row0:row0 + 128, :], in_=r_[:]),
                     r=[("rb", ri)], w=[("out", row0)], dma="st_%d" % ri)

            def y_tail():
                for s in range(NSUB):
                    do_sub2b(s)
            return XP, YP, x_cast, y_ln2a, y_tail

        tiles = [make_tile(i) for i in range(NT)]
        tiles[0][2]()
        for rnd in range(NT + 1):
            XP = list(tiles[rnd][0]) if rnd < NT else []
            if rnd + 1 < NT:
                p8_, cast_ = XP[8], tiles[rnd + 1][2]
                XP[8] = lambda p8_=p8_, cast_=cast_: (cast_(), p8_())
            YP = tiles[rnd - 1][1] if rnd >= 1 else []
            n = max(len(XP), len(YP), 3)
            for q in range(n):
                if q < len(XP) and XP[q] is not None:
                    XP[q]()
                if q < len(YP):
                    YP[q]()
                if rnd >= 2 and q == 0:
                    tiles[rnd - 2][3]()
                if rnd >= 2 and q == 2:
                    tiles[rnd - 2][4]()
        tiles[NT - 1][3]()
        tiles[NT - 1][4]()

        S.emit(nc)
    return nc


_CACHE = {}


def _layout_inputs(x, c, w_ada, b_ada, w_in, conv_w, w_pool, pool_scale, w_out, ln1_g, ln1_b,
                   w_mlp_in, w_mlp_out, ln2_g, ln2_b):
    f = np.float32
    x = np.asarray(x, f)
    c = np.asarray(c, f)
    shared = {
        "rows": np.ascontiguousarray(np.stack([np.asarray(ln1_g, f)[0], np.asarray(ln1_b, f)[0],
                                               np.asarray(ln2_g, f)[0], np.asarray(ln2_b, f)[0]])),
        "b_ada": np.ascontiguousarray(np.asarray(b_ada, f)[0]),
        "w_ada": np.ascontiguousarray(np.asarray(w_ada, f)[0]),
        "w_in": np.ascontiguousarray(np.asarray(w_in, f)[0]),
        "w_pool": np.ascontiguousarray(np.asarray(w_pool, f)[0]),
        "w_out": np.ascontiguousarray(np.asarray(w_out, f)[0]),
        "w1": np.ascontiguousarray(np.asarray(w_mlp_in, f)[0]),
        "w2": np.ascontiguousarray(np.asarray(w_mlp_out, f)[0]),
    }
    g1T = np.asarray(ln1_g, f)[0].reshape(8, 128).T
    b1T = np.asarray(ln1_b, f)[0].reshape(8, 128).T
    cw = np.asarray(conv_w, f)[0]
    cwT = cw.reshape(3, 4, 128).transpose(2, 1, 0).reshape(128, 12)
    psT = np.asarray(pool_scale, f)[0].reshape(4, 128).T
    in_maps = []
    for core in range(NCORES):
        b, half = core // 2, core % 2
        start = half * TOK
        xh = np.zeros((TOK + HALO, D), f)
        xh[HALO:] = x[b, start:start + TOK]
        if half:
            xh[:HALO] = x[b, start - HALO:start]
        vecs = np.zeros((128, NV), f)
        vecs[:, C_C:C_C + 8] = c[b].reshape(8, 128).T
        vecs[:, C_G1:C_G1 + 8] = g1T
        vecs[:, C_B1:C_B1 + 8] = b1T
        vecs[:, C_CW:C_CW + 12] = cwT
        vecs[:, C_PS:C_PS + 4] = psT
        vecs[:, C_FLAG] = 1.0 if half else 0.0
        for g, win in enumerate(WINS):
            for t in range(16):
                vecs[:, C_INV + 16 * g + t] = (1.0 / win) if half else (1.0 / min(t + 1, win))
        m = dict(shared)
        m["xh"] = xh
        m["vecs"] = vecs
        in_maps.append(m)
    return in_maps


def kernel(**inputs):
    if "nc" not in _CACHE:
        _CACHE["nc"] = build_program()
    nc = _CACHE["nc"]
    in_maps = _layout_inputs(**inputs)
    res = run_bass_kernel_spmd(nc, in_maps, core_ids=list(range(NCORES)))
    outp = np.empty((BATCH, SEQ, D), np.float32)
    for core in range(NCORES):
        b, half = core // 2, core % 2
        outp[b, half * TOK:(half + 1) * TOK] = res.results[core]["out"]
    return outp
```

```python
import numpy as np
from contextlib import ExitStack
import concourse.bass as bass
import concourse.mybir as mybir
from concourse.bass_utils import run_bass_kernel_spmd

F32 = mybir.dt.float32
BF16 = mybir.dt.bfloat16
ALU = mybir.AluOpType
AF = mybir.ActivationFunctionType

D = 1024
SEQ = 8192
BATCH = 4
NCORES = 8
TOK = 4096
HALO = 16
T = 256
NSUB = T // 128
NT = TOK // T
DFF = 4096
ALPHA = 2.0 ** 0.25
EPS = 1e-5
WINS = (2, 4, 8, 16)

C_C, C_G1, C_B1, C_CW, C_PS, C_FLAG, C_INV, NV = 0, 8, 16, 24, 36, 40, 41, 105

ENGS = ("pe", "act", "dve", "pool", "sp")


class _Op:
    __slots__ = ("idx", "eng", "fn", "deps", "dma", "dma_val", "sig", "sig_val")

    def __init__(self, idx, eng, fn, dma):
        self.idx = idx
        self.eng = eng
        self.fn = fn
        self.deps = {}
        self.dma = dma
        self.dma_val = 0
        self.sig = False
        self.sig_val = 0


class Sched:
    def __init__(self):
        self.ops = []
        self.last_w = {}
        self.readers = {}
        self.dma_cnt = {}

    def op(self, eng, fn, r=(), w=(), dma=None):
        o = _Op(len(self.ops), eng, fn, dma)
        for k in r:
            lw = self.last_w.get(k)
            if lw is not None:
                o.deps[lw] = True
        for k in w:
            lw = self.last_w.get(k)
            if lw is not None and lw not in o.deps:
                o.deps[lw] = False
            for rd in self.readers.get(k, ()):
                if rd not in o.deps:
                    o.deps[rd] = False
        for k in r:
            self.readers.setdefault(k, []).append(o.idx)
        for k in w:
            self.last_w[k] = o.idx
            self.readers[k] = []
        if dma is not None:
            c = self.dma_cnt.get(dma, 0) + 1
            self.dma_cnt[dma] = c
            o.dma_val = 16 * c
        self.ops.append(o)
        return o

    def emit(self, nc, final_eng="sp"):
        ops = self.ops
        need = []
        for o in ops:
            lst = []
            for d, is_raw in o.deps.items():
                p = ops[d]
                if p.dma is not None:
                    lst.append(p)
                elif p.eng == o.eng and o.dma is None:
                    if is_raw and o.eng != "pe":
                        lst.append(p)
                else:
                    lst.append(p)
            need.append(lst)
            for p in lst:
                if p.dma is None:
                    p.sig = True
        cnt = {e: 0 for e in ENGS}
        for o in ops:
            if o.sig:
                cnt[o.eng] += 1
                o.sig_val = cnt[o.eng]
        dma_keys = sorted(self.dma_cnt.keys())
        with ExitStack() as st:
            esem = {e: st.enter_context(nc.semaphore("sem_" + e)) for e in ENGS}
            dsem = {k: st.enter_context(nc.semaphore("dsem_" + str(k))) for k in dma_keys}
            block = st.enter_context(nc.Block())
            per_eng = {e: [o for o in ops if o.eng == e] for e in ENGS}

            def run(eng_name, eng):
                waited = {}
                for o in per_eng[eng_name]:
                    for p in need[o.idx]:
                        if p.dma is not None:
                            s, v, key = dsem[p.dma], p.dma_val, ("d", p.dma)
                        else:
                            s, v, key = esem[p.eng], p.sig_val, ("e", p.eng)
                        if waited.get(key, 0) >= v:
                            continue
                        waited[key] = v
                        eng.wait_ge(s, v)
                    ins = o.fn(eng)
                    if o.dma is not None:
                        ins.then_inc(dsem[o.dma], 16)
                    elif o.sig:
                        ins.then_inc(esem[o.eng], 1)
                if eng_name == final_eng:
                    for k in dma_keys:
                        eng.wait_ge(dsem[k], 16 * self.dma_cnt[k])

            @block.tensor
            def _(e):
                run("pe", e)

            @block.scalar
            def _(e):
                run("act", e)

            @block.vector
            def _(e):
                run("dve", e)

            @block.gpsimd
            def _(e):
                run("pool", e)

            @block.sync
            def _(e):
                run("sp", e)


def build_program():
    nc = bass.Bass("TRN2", target_bir_lowering=False)

    def din(name, shape, dt=F32):
        return nc.dram_tensor(name, shape, dt, kind="ExternalInput").ap()

    xh = din("xh", [TOK + HALO, D])
    vecs = din("vecs", [128, NV])
    rows = din("rows", [4, D])
    b_ada = din("b_ada", [6 * D])
    w_ada = din("w_ada", [D, 6 * D])
    w_in = din("w_in", [D, 2048])
    w_pool = din("w_pool", [4, 128, 128])
    w_out = din("w_out", [D, D])
    w1 = din("w1", [D, DFF])
    w2 = din("w2", [DFF, D])
    out = nc.dram_tensor("out", [TOK, D], F32, kind="ExternalOutput").ap()
    w1s = nc.dram_tensor("w1s", [128, 8, DFF], BF16, kind="Internal").ap()
    w2s = nc.dram_tensor("w2s", [128, 32, D], BF16, kind="Internal").ap()

    S = Sched()
    with ExitStack() as st:
        def sb(name, shape, dt=F32):
            return st.enter_context(nc.sbuf_tensor(name, shape, dt))

        ps = [st.enter_context(nc.psum_tensor("ps%d" % k, [128, 512], F32)) for k in range(8)]
        psb = [p.bitcast(BF16) for p in ps]

        vec = sb("vec", [128, NV])
        idf = sb("idf", [128, 128])
        idb = sb("idb", [128, 128], BF16)
        ones_f = sb("ones_f", [128, 128])
        cond = sb("cond", [128, 8])
        condrep = sb("condrep", [128, 8, 128], BF16)
        w_in_sb = sb("w_in_sb", [128, 8, 2048], BF16)
        w_out_sb = sb("w_out_sb", [128, 8, D], BF16)
        w_pool_sb = sb("w_pool_sb", [128, 4, 128], BF16)
        g1p_bc = sb("g1p_bc", [128, D])
        g2p_bc = sb("g2p_bc", [128, D])
        ag_bc = sb("ag_bc", [128, D])
        ab_bc = sb("ab_bc", [128, D])
        ln2g_bc = sb("ln2g_bc", [128, D])
        ln2b_bc = sb("ln2b_bc", [128, D])
        modT = sb("modT", [128, 32])
        GB = sb("GB", [128, 24])
        xt = [sb("xt%d" % i, [128, NSUB, D]) for i in range(2)]
        xbf = sb("xbf", [128, NSUB, D], BF16)
        h1T = sb("h1T", [128, 8, T], BF16)
        h1halo = sb("h1halo", [128, 8, HALO], BF16)
        vc = [sb("vc%d" % i, [128, T]) for i in range(2)]
        uw = [sb("uw%d" % i, [128, T + HALO]) for i in range(4)]
        cv = [sb("cv%d" % i, [128, T]) for i in range(2)]
        vpw = [sb("vpw%d" % i, [128, T + HALO]) for i in range(4)]
        sA = [sb("sA%d" % i, [128, T + HALO]) for i in range(2)]
        sB = [sb("sB%d" % i, [128, T + HALO]) for i in range(2)]
        pbf = [sb("pbf%d" % i, [128, T], BF16) for i in range(2)]
        tmp16 = sb("tmp16", [128, HALO])
        ycatT = sb("ycatT", [128, 8, T], BF16)
        NR = 3
        rb = [sb("rb%d" % i, [128, D]) for i in range(NR)]
        ub2 = [sb("ub%d" % i, [128, NSUB, D]) for i in range(2)]
        ubf = [sb("ubf%d" % i, [128, D], BF16) for i in range(2)]
        h2T2 = [sb("h2T%d" % i, [128, 8, T], BF16) for i in range(2)]
        NW = 2
        w1r = [sb("w1r%d" % i, [128, 8, 512], BF16) for i in range(NW)]
        w2r = [sb("w2r%d" % i, [128, 4, D], BF16) for i in range(NW)]
        rl = [sb("rl%d" % i, [128, T]) for i in range(2)]
        aT = [sb("aT%d" % i, [128, 4, T], BF16) for i in range(2)]
        NLN = 4
        stats = [sb("stats%d" % i, [128, 12]) for i in range(NLN)]
        mv = [sb("mv%d" % i, [128, 8]) for i in range(NLN)]

        wada_r = [w1r[i] for i in range(2)]
        bada_r = [sb("bada%d" % i, [128, 256]) for i in range(2)]
        tmpbc = [sb("tmpbc%d" % i, [128, 256]) for i in range(2)]
        rot = [0]

        def rbank():
            k = rot[0]
            rot[0] = (k + 1) % 4
            return k

        S.op("sp", lambda e: e.dma_start(out=vec[:], in_=vecs), w=["vec"], dma="ld_vec")
        S.op("pool", lambda e: e.memset(idf[:], 0.0), w=["idf"])
        S.op("pool", lambda e: e.affine_select(out=idf[:], in_=idf[:], pattern=[[-1, 128]],
                                               compare_op=ALU.not_equal, fill=1.0, base=0,
                                               channel_multiplier=1), r=["idf"], w=["idf"])
        S.op("pool", lambda e: e.tensor_copy(out=idb[:], in_=idf[:]), r=["idf"], w=["idb"])
        S.op("pool", lambda e: e.memset(ones_f[:], 1.0), w=["ones_f"])
        S.op("act", lambda e: e.activation(out=cond[:], in_=vec[:, C_C:C_C + 8], func=AF.Silu),
             r=["vec"], w=["cond"])
        for kc in range(8):
            S.op("dve", lambda e, kc=kc: e.tensor_scalar(out=condrep[:, kc, :], in0=ones_f[:],
                                                         scalar1=cond[:, kc:kc + 1], scalar2=None,
                                                         op0=ALU.mult),
                 r=["ones_f", "cond"], w=["condrep"])

        for i, (dst, name) in enumerate(((ag_bc, "ag_bc"), (ab_bc, "ab_bc"),
                                         (ln2g_bc, "ln2g_bc"), (ln2b_bc, "ln2b_bc"))):
            S.op("sp", lambda e, dst=dst, i=i: e.dma_start(out=dst[:], in_=rows[i].partition_broadcast(128)),
                 w=[name], dma="ld_" + name)
        S.op("act", lambda e: e.mul(out=ag_bc[:], in_=ag_bc[:], mul=ALPHA), r=["ag_bc"], w=["ag_bc"])
        S.op("act", lambda e: e.mul(out=ab_bc[:], in_=ab_bc[:], mul=ALPHA), r=["ab_bc"], w=["ab_bc"])

        def cast_dma(dst, src, wkeys, key):
            S.op("pool", lambda e: e.dma_start(out=dst, in_=src), w=wkeys, dma=key)

        def mod_block(blk):
            slot = blk % 2
            col0 = blk * 256
            vi, off = col0 // D, col0 % D
            cast_dma(wada_r[slot][:, :, 0:256], w_ada[:, col0:col0 + 256].rearrange("(kc p) n -> p kc n", p=128),
                     [("w1r", slot)], "wada%d" % slot)
            S.op("sp", lambda e: e.dma_start(out=bada_r[slot][:],
                                             in_=b_ada[col0:col0 + 256].partition_broadcast(128)),
                 w=[("bada", slot)], dma="bada%d" % slot)
            k = rbank()

            def mm(e):
                for kc in range(8):
                    ins = e.matmul(ps[k][:, 0:256], lhsT=condrep[:, kc, :], rhs=wada_r[slot][:, kc, 0:256],
                                   start=(kc == 0), stop=(kc == 7))
                return ins
            S.op("pe", mm, r=[("w1r", slot), "condrep"], w=[("ps", k)])
            if vi in (2, 5):
                dst, name = (g1p_bc, "g1p_bc") if vi == 2 else (g2p_bc, "g2p_bc")
                S.op("dve", lambda e: e.scalar_tensor_tensor(out=dst[:, off:off + 256], in0=ps[k][:, 0:256],
                                                             scalar=1.0, in1=bada_r[slot][:],
                                                             op0=ALU.add, op1=ALU.add),
                     r=[("ps", k), ("bada", slot)], w=[(name, off)])
            else:
                addc = 1.0 if vi in (1, 4) else 0.0
                S.op("dve", lambda e: e.scalar_tensor_tensor(out=tmpbc[slot][:], in0=ps[k][:, 0:256],
                                                             scalar=addc, in1=bada_r[slot][:],
                                                             op0=ALU.add, op1=ALU.add),
                     r=[("ps", k), ("bada", slot)], w=[("tmpbc", slot)])
                vslot = {0: 0, 1: 1, 3: 2, 4: 3}[vi]

                def mmT(e):
                    for c in range(2):
                        col = vslot * 8 + off // 128 + c
                        ins = e.matmul(ps[7][:, col:col + 1], lhsT=tmpbc[slot][0:1, c * 128:(c + 1) * 128],
                                       rhs=ones_f[0:1, 0:1], start=True, stop=True)
                    return ins
                S.op("pe", mmT, r=[("tmpbc", slot), "ones_f"], w=[("ps", 7)])

        hooks = {}
        for blk in range(8):
            mod_block(blk)
        S.op("act", lambda e: e.copy(out=modT[:, 0:16], in_=ps[7][:, 0:16]), r=[("ps", 7)], w=["modT1"])
        for h in range(2):
            cast_dma(w_in_sb[:, 4 * h:4 * h + 4, :],
                     w_in[512 * h:512 * (h + 1), :].rearrange("(kc p) n -> p kc n", p=128),
                     [("w_in", h)], "c_w_in%d" % h)
        cast_dma(w_pool_sb[:], w_pool.rearrange("g c d -> c g d"), ["w_pool"], "c_w_pool")
        cast_dma(w_out_sb[:], w_out.rearrange("(kc p) n -> p kc n", p=128), ["w_out"], "c_w_out")

        def mod_rest(q):
            for blk in (8 + 2 * q, 9 + 2 * q):
                mod_block(blk)
            if q == 5:
                S.op("act", lambda e: e.copy(out=modT[:, 16:32], in_=ps[7][:, 16:32]), r=[("ps", 7)], w=["modT2"])
                S.op("dve", lambda e: e.tensor_tensor(out=GB[:, 0:8], in0=vec[:, C_G1:C_G1 + 8], in1=modT[:, 24:32],
                                                      op=ALU.mult), r=["vec", "modT2"], w=["G2"])
                S.op("dve", lambda e: e.tensor_tensor(out=GB[:, 16:24], in0=vec[:, C_B1:C_B1 + 8], in1=modT[:, 24:32],
                                                      op=ALU.mult), r=["vec", "modT2"], w=["B2t"])
                S.op("dve", lambda e: e.tensor_tensor(out=GB[:, 8:16], in0=GB[:, 16:24], in1=modT[:, 16:24],
                                                      op=ALU.add), r=["B2t", "modT2"], w=["B2"])
        for q in range(8):
            hooks[(0, q)] = (lambda q=q: mod_rest(q))
        G1PK = [("g1p_bc", o) for o in range(0, D, 256)]
        G2PK = [("g2p_bc", o) for o in range(0, D, 256)]

        def ln_stage1(src_ap, skey, par):
            st_, m_ = stats[par], mv[par]
            S.op("dve", lambda e: e.bn_stats(out=st_[:, 0:6], in_=src_ap[:, 0:512]), r=[skey], w=[("st0", par)])
            S.op("dve", lambda e: e.bn_stats(out=st_[:, 6:12], in_=src_ap[:, 512:1024]), r=[skey], w=[("st1", par)])
            S.op("dve", lambda e: e.bn_aggr(out=m_[:, 0:2], in_=st_[:, 0:12]),
                 r=[("st0", par), ("st1", par)], w=[("mv01", par)])
            S.op("dve", lambda e: e.tensor_scalar(out=m_[:, 4:5], in0=m_[:, 1:2], scalar1=EPS, scalar2=None,
                                                  op0=ALU.add), r=[("mv01", par)], w=[("mv4", par)])

        def ln_stage2(par):
            m_ = mv[par]
            S.op("act", lambda e: e.activation(out=m_[:, 5:6], in_=m_[:, 4:5], func=AF.Sqrt),
                 r=[("mv4", par)], w=[("mv5", par)])
            S.op("dve", lambda e: e.reciprocal(out=m_[:, 2:3], in_=m_[:, 5:6]), r=[("mv5", par)], w=[("mv2", par)])

        lnc = [0]
        rbc = [0]

        def load_x(i):
            slot = i % 2
            S.op("sp", lambda e: e.dma_start(
                out=xt[slot][:], in_=xh[HALO + i * T:HALO + (i + 1) * T, :].rearrange("(s p) f -> p s f", p=128)),
                w=[("xt", slot, s) for s in range(NSUB)], dma="ld_x%d" % slot)

        load_x(0)

        def make_tile(i):
            xs = i % 2
            tp = i % 2
            ub = ub2[tp]
            h2T = h2T2[tp]
            XP, YP = [], []

            def x_cast():
              for s in range(NSUB):
                S.op("act", lambda e, s=s: e.copy(out=xbf[:, s, :], in_=xt[xs][:, s, :]),
                     r=[("xt", xs, s)], w=[("xbf", s)])

            def x_front():
              for kq in range(2):
                k = rbank()

                def trx(e, kq=kq, k=k):
                    for kk in range(4):
                        kc = kq * 4 + kk
                        for s in range(NSUB):
                            ins = e.transpose(psb[k][:, kk * T + s * 128:kk * T + (s + 1) * 128],
                                              xbf[:, s, kc * 128:(kc + 1) * 128], idb[:])
                    return ins
                S.op("pe", trx, r=[("xbf", s) for s in range(NSUB)] + ["idb"], w=[("ps", k)])
                for kk in range(4):
                    kc = kq * 4 + kk
                    S.op("act", lambda e, kk=kk, kc=kc, k=k: e.activation(
                        out=h1T[:, kc, :], in_=psb[k][:, kk * T:(kk + 1) * T], func=AF.Identity,
                        bias=modT[:, kc:kc + 1], scale=modT[:, 8 + kc:9 + kc]),
                        r=[("ps", k), "modT1"], w=[("h1T", kc)])
              if i == 0:
                S.op("sp", lambda e: e.dma_start(out=rb[0][0:HALO, :], in_=xh[0:HALO, :]), w=[("rb", 0)], dma="ld_halo")
                S.op("act", lambda e: e.copy(out=ubf[0][0:HALO, :], in_=rb[0][0:HALO, :]), r=[("rb", 0)], w=[("ubf", 0)])
                k = rbank()

                def trh(e, k=k):
                    for kc in range(8):
                        ins = e.transpose(psb[k][:, kc * HALO:(kc + 1) * HALO],
                                          ubf[0][0:HALO, kc * 128:(kc + 1) * 128], idb[0:HALO, 0:HALO])
                    return ins
                S.op("pe", trh, r=[("ubf", 0), "idb"], w=[("ps", k)])
                for kc in range(8):
                    S.op("dve", lambda e, kc=kc, k=k: e.tensor_scalar(
                        out=h1halo[:, kc, :], in0=psb[k][:, kc * HALO:(kc + 1) * HALO],
                        scalar1=modT[:, 8 + kc:9 + kc], scalar2=modT[:, kc:kc + 1],
                        op0=ALU.mult, op1=ALU.add), r=[("ps", k), "modT1"], w=[("h1halo", kc)])
                    S.op("dve", lambda e, kc=kc: e.tensor_scalar(
                        out=h1halo[:, kc, :], in0=h1halo[:, kc, :], scalar1=vec[:, C_FLAG:C_FLAG + 1],
                        scalar2=None, op0=ALU.mult), r=[("h1halo", kc), "vec"], w=[("h1halo", kc)])
              if i + 1 < NT:
                load_x(i + 1)
            XP.append(x_front)

            H1K = [("h1T", kc) for kc in range(8)]
            H1HK = [("h1halo", kc) for kc in range(8)]
            WINK = [("w_in", 0), ("w_in", 1)]

            def inproj(col0, halo=False):
                k = rbank()
                n = HALO if halo else T

                def mm(e):
                    for kc in range(8):
                        rhs = h1halo[:, kc, :] if halo else h1T[:, kc, :]
                        ins = e.matmul(ps[k][:, 0:n], lhsT=w_in_sb[:, kc, col0:col0 + 128], rhs=rhs,
                                       start=(kc == 0), stop=(kc == 7))
                    return ins
                S.op("pe", mm, r=WINK + (H1HK if halo else H1K), w=[("ps", k)])
                return k

            def chunk_q(j):
                par = j % 2
                k_q = rbank()
                S.op("pe", lambda e, k=k_q: e.matmul(ps[k][:, 0:T], lhsT=w_pool_sb[:, j, :], rhs=pbf[par][:],
                                                     start=True, stop=True),
                     r=["w_pool", ("pbf", par)], w=[("ps", k_q)])
                S.op("act", lambda e, k=k_q: e.mul(out=ycatT[:, 4 + j, :], in_=ps[k][:, 0:T],
                                                   mul=vec[:, C_PS + j:C_PS + j + 1]),
                     r=[("ps", k_q), "vec"], w=[("ycatT", 4 + j)])

            def do_chunk(j):
                par = j % 2
                if j > 0:
                    chunk_q(j - 1)
                U, V = uw[j], vpw[j]
                if i > 0:
                    S.op("act", lambda e: e.copy(out=U[:, 0:HALO], in_=U[:, T:T + HALO]),
                         r=[("uw", j)], w=[("uwh", j)])
                    S.op("act", lambda e: e.copy(out=V[:, 0:HALO], in_=V[:, T:T + HALO]),
                         r=[("vpw", j)], w=[("vpwh", j)])
                k_vc = inproj(1024 + 128 * j)
                S.op("act", lambda e, k=k_vc: e.copy(out=vc[par][:], in_=ps[k][:, 0:T]),
                     r=[("ps", k_vc)], w=[("vc", par)])
                k_gc = inproj(512 + 128 * j)
                S.op("dve", lambda e, k=k_gc: e.tensor_tensor(out=U[:, HALO:HALO + T], in0=ps[k][:, 0:T],
                                                              in1=vc[par][:], op=ALU.mult),
                     r=[("ps", k_gc), ("vc", par)], w=[("uw", j)])
                if i == 0:
                    k_h = inproj(1024 + 128 * j, halo=True)
                    S.op("act", lambda e, k=k_h: e.copy(out=tmp16[:], in_=ps[k][:, 0:HALO]),
                         r=[("ps", k_h)], w=["tmp16"])
                    k_h2 = inproj(512 + 128 * j, halo=True)
                    S.op("dve", lambda e, k=k_h2: e.tensor_tensor(out=U[:, 0:HALO], in0=ps[k][:, 0:HALO],
                                                                  in1=tmp16[:], op=ALU.mult),
                         r=[("ps", k_h2), "tmp16"], w=[("uwh", j)])
                k_vp = inproj(1536 + 128 * j)
                S.op("act", lambda e, k=k_vp: e.copy(out=V[:, HALO:HALO + T], in_=ps[k][:, 0:T]),
                     r=[("ps", k_vp)], w=[("vpw", j)])
                if i == 0:
                    k_h3 = inproj(1536 + 128 * j, halo=True)
                    S.op("act", lambda e, k=k_h3: e.copy(out=V[:, 0:HALO], in_=ps[k][:, 0:HALO]),
                         r=[("ps", k_h3)], w=[("vpwh", j)])
                cw = C_CW + 3 * j
                S.op("act", lambda e: e.mul(out=cv[par][:], in_=U[:, HALO:HALO + T], mul=vec[:, cw + 2:cw + 3]),
                     r=[("uw", j), "vec"], w=[("cv", par)])
                S.op("dve", lambda e: e.scalar_tensor_tensor(out=cv[par][:], in0=U[:, HALO - 1:HALO - 1 + T],
                                                             scalar=vec[:, cw + 1:cw + 2], in1=cv[par][:],
                                                             op0=ALU.mult, op1=ALU.add),
                     r=[("uw", j), ("uwh", j), ("cv", par), "vec"], w=[("cv", par)])
                S.op("dve", lambda e: e.scalar_tensor_tensor(out=cv[par][:], in0=U[:, HALO - 2:HALO - 2 + T],
                                                             scalar=vec[:, cw:cw + 1], in1=cv[par][:],
                                                             op0=ALU.mult, op1=ALU.add),
                     r=[("uw", j), ("uwh", j), ("cv", par), "vec"], w=[("cv", par)])
                k_gb = inproj(128 * j)
                S.op("dve", lambda e, k=k_gb: e.tensor_tensor(out=ycatT[:, j, :], in0=ps[k][:, 0:T],
                                                              in1=cv[par][:], op=ALU.mult),
                     r=[("ps", k_gb), ("cv", par)], w=[("ycatT", j)])
                L = T + HALO
                VK = [("vpw", j), ("vpwh", j)]
                S.op("dve", lambda e: e.tensor_tensor(out=sA[par][:, 1:L], in0=V[:, 1:L], in1=V[:, 0:L - 1], op=ALU.add),
                     r=VK, w=[("sA", par)])
                cur, curk = sA, "sA"
                if j >= 1:
                    S.op("dve", lambda e: e.tensor_tensor(out=sB[par][:, 3:L], in0=sA[par][:, 3:L],
                                                          in1=sA[par][:, 1:L - 2], op=ALU.add),
                         r=[("sA", par)], w=[("sB", par)])
                    cur, curk = sB, "sB"
                if j >= 2:
                    S.op("dve", lambda e: e.tensor_tensor(out=sA[par][:, 7:L], in0=sB[par][:, 7:L],
                                                          in1=sB[par][:, 3:L - 4], op=ALU.add),
                         r=[("sB", par)], w=[("sA", par)])
                    cur, curk = sA, "sA"
                if j >= 3:
                    S.op("dve", lambda e: e.tensor_tensor(out=sB[par][:, 15:L], in0=sA[par][:, 15:L],
                                                          in1=sA[par][:, 7:L - 8], op=ALU.add),
                         r=[("sA", par)], w=[("sB", par)])
                    cur, curk = sB, "sB"
                win = float(WINS[j])
                S.op("dve", lambda e, cur=cur: e.scalar_tensor_tensor(
                    out=pbf[par][:], in0=cur[par][:, HALO:HALO + T], scalar=1.0 / win,
                    in1=V[:, HALO:HALO + T], op0=ALU.mult, op1=ALU.subtract),
                    r=[(curk, par), ("vpw", j)], w=[("pbf", par)])
                if i == 0:
                    ic = C_INV + 16 * j
                    S.op("dve", lambda e, cur=cur: e.tensor_tensor(
                        out=tmp16[:], in0=cur[par][:, HALO:2 * HALO], in1=vec[:, ic:ic + 16], op=ALU.mult),
                        r=[(curk, par), "vec"], w=["tmp16"])
                    S.op("dve", lambda e: e.tensor_tensor(
                        out=pbf[par][:, 0:HALO], in0=tmp16[:], in1=V[:, HALO:2 * HALO], op=ALU.subtract),
                        r=["tmp16", ("vpw", j), ("pbf", par)], w=[("pbf", par)])

            for j in range(4):
                XP.append(lambda j=j: do_chunk(j))
            XP.append(lambda: chunk_q(3))

            YK = [("ycatT", c) for c in range(8)]
            ln1_state = {}
            ln2_state = {}

            def sub1A(s):
                ri = rbc[0] % NR
                rbc[0] += 1
                r_ = rb[ri]
                for hf in range(2):
                    ka = rbank()

                    def mmo(e, ka=ka, hf=hf, s=s):
                        for kc in range(8):
                            ins = e.matmul(ps[ka][:, :], lhsT=ycatT[:, kc, s * 128:(s + 1) * 128],
                                           rhs=w_out_sb[:, kc, hf * 512:(hf + 1) * 512],
                                           start=(kc == 0), stop=(kc == 7))
                        return ins
                    S.op("pe", mmo, r=YK + ["w_out"], w=[("ps", ka)])
                    S.op("dve", lambda e, ka=ka, hf=hf, r_=r_: e.tensor_tensor(
                        out=r_[:, hf * 512:(hf + 1) * 512], in0=ps[ka][:, :], in1=g1p_bc[:, hf * 512:(hf + 1) * 512],
                        op=ALU.mult), r=[("ps", ka)] + G1PK, w=[("rb", ri, hf), ("rb", ri)])
                S.op("dve", lambda e, r_=r_, s=s: e.scalar_tensor_tensor(
                    out=r_[:], in0=xt[xs][:, s, :], scalar=ALPHA, in1=r_[:], op0=ALU.mult, op1=ALU.add),
                    r=[("xt", xs, s), ("rb", ri, 0), ("rb", ri, 1)], w=[("rb", ri)])
                par = lnc[0] % NLN
                lnc[0] += 1
                ln_stage1(r_, ("rb", ri), par)
                ln1_state[s] = (ri, par)

            def sub1A2(s):
                ri, par = ln1_state[s]
                r_ = rb[ri]
                m_ = mv[par]
                ln_stage2(par)
                MK = [("mv01", par), ("mv2", par)]
                up = s % 2
                S.op("dve", lambda e: e.tensor_scalar(out=ubf[up][:], in0=r_[:], scalar1=m_[:, 0:1], scalar2=m_[:, 2:3],
                                                      op0=ALU.subtract, op1=ALU.mult),
                     r=[("rb", ri)] + MK, w=[("ubf", up)])
                S.op("dve", lambda e: e.scalar_tensor_tensor(out=ub[:, s, :], in0=r_[:], scalar=m_[:, 0:1], in1=ag_bc[:],
                                                             op0=ALU.subtract, op1=ALU.mult),
                     r=[("rb", ri), "ag_bc"] + MK, w=[("ub", tp, s)])
                S.op("dve", lambda e: e.scalar_tensor_tensor(out=ub[:, s, :], in0=ub[:, s, :], scalar=m_[:, 2:3],
                                                             in1=ab_bc[:], op0=ALU.mult, op1=ALU.add),
                     r=[("ub", tp, s), "ab_bc"] + MK, w=[("ub", tp, s)])

            def sub1B(s):
                up = s % 2
                for kq in range(2):
                    k = rbank()

                    def tru(e, kq=kq, k=k, up=up, s=s):
                        for kk in range(4):
                            kc = kq * 4 + kk
                            ins = e.transpose(psb[k][:, kk * 128:(kk + 1) * 128],
                                              ubf[up][:, kc * 128:(kc + 1) * 128], idb[:])
                        return ins
                    S.op("pe", tru, r=[("ubf", up), "idb"], w=[("ps", k)])
                    for kk in range(4):
                        kc = kq * 4 + kk
                        S.op("act", lambda e, kk=kk, kc=kc, k=k, s=s: e.activation(
                            out=h2T[:, kc, s * 128:(s + 1) * 128], in_=psb[k][:, kk * 128:(kk + 1) * 128],
                            func=AF.Identity, bias=GB[:, 8 + kc:9 + kc], scale=GB[:, kc:kc + 1]),
                            r=[("ps", k), "G2", "B2"], w=[("h2T", tp, kc, s)])
            XP.append(lambda: sub1A(0))

            def p7():
                sub1A2(0)
                sub1A(1)
            XP.append(p7)

            def p8():
                sub1B(0)
                sub1A2(1)
            XP.append(p8)
            XP.append(lambda: sub1B(1))


            H2K = [("h2T", tp, kc, s) for kc in range(8) for s in range(NSUB)]

            def load_w(b):
                slot = (i * 8 + b) % NW
                if i == 0:
                    S.op("pool", lambda e: e.dma_start(
                        out=w1r[slot][:], in_=w1[:, b * 512:(b + 1) * 512].rearrange("(kc p) n -> p kc n", p=128)),
                        w=[("w1r", slot)], dma="ld_w1_%d" % slot)
                    S.op("sp", lambda e: e.dma_start(out=w1s[:, :, b * 512:(b + 1) * 512], in_=w1r[slot][:]),
                         r=[("w1r", slot)], w=[("w1s", b)], dma="sv_w1_%d" % slot)
                    S.op("pool", lambda e: e.dma_start(
                        out=w2r[slot][:], in_=w2[b * 512:(b + 1) * 512, :].rearrange("(c p) n -> p c n", p=128)),
                        w=[("w2r", slot)], dma="ld_w2_%d" % slot)
                    S.op("sp", lambda e: e.dma_start(out=w2s[:, 4 * b:4 * b + 4, :], in_=w2r[slot][:]),
                         r=[("w2r", slot)], w=[("w2s", b)], dma="sv_w2_%d" % slot)
                    return
                S.op("sp", lambda e: e.dma_start(out=w1r[slot][:], in_=w1s[:, :, b * 512:(b + 1) * 512]),
                     r=[("w1s", b)], w=[("w1r", slot)], dma="ld_w1_%d" % slot)
                S.op("sp", lambda e: e.dma_start(out=w2r[slot][:], in_=w2s[:, 4 * b:4 * b + 4, :]),
                     r=[("w2s", b)], w=[("w2r", slot)], dma="ld_w2_%d" % slot)

            def stage_a(b):
                slot = (i * 8 + b) % NW
                ap_ = b % 2
                for c in range(4):
                    k = rbank()
                    rp = c % 2

                    def mma(e, k=k, c=c):
                        for kc in range(8):
                            ins = e.matmul(ps[k][:, 0:T], lhsT=w1r[slot][:, kc, c * 128:(c + 1) * 128],
                                           rhs=h2T[:, kc, :], start=(kc == 0), stop=(kc == 7))
                        return ins
                    S.op("pe", mma, r=[("w1r", slot)] + H2K, w=[("ps", k)])
                    S.op("act", lambda e, k=k, rp=rp: e.activation(out=rl[rp][:], in_=ps[k][:, 0:T], func=AF.Relu),
                         r=[("ps", k)], w=[("rl", rp)])
                    S.op("act", lambda e, rp=rp, c=c: e.activation(out=aT[ap_][:, c, :], in_=rl[rp][:], func=AF.Square),
                         r=[("rl", rp)], w=[("aT", ap_, c)])

            def stage_b(b):
                slot = (i * 8 + b) % NW
                ap_ = b % 2
                for s in range(NSUB):
                    for hf in range(2):
                        ka = 4 + (2 * s + hf) % 4

                        def mmb(e, ka=ka, s=s, hf=hf):
                            for c in range(4):
                                ins = e.matmul(ps[ka][:, :], lhsT=aT[ap_][:, c, s * 128:(s + 1) * 128],
                                               rhs=w2r[slot][:, c, hf * 512:(hf + 1) * 512],
                                               start=(b == 0 and c == 0), stop=(b == 7 and c == 3))
                            return ins
                        S.op("pe", mmb, r=[("w2r", slot)] + [("aT", ap_, c) for c in range(4)], w=[("ps", ka)])

            def y_first():
                load_w(0)
                stage_a(0)
            YP.append(y_first)

            def y_mid(b):
                load_w(b)
                stage_a(b)
                stage_b(b - 1)
            for b in range(1, 8):
                YP.append(lambda b=b: y_mid(b))
            YP.append(lambda: stage_b(7))

            def y_ln2a():
                for s in range(NSUB):
                    do_sub2(s)

            def do_sub2(s):
                ri = rbc[0] % NR
                rbc[0] += 1
                r_ = rb[ri]
                for hf in range(2):
                    ka = 4 + (2 * s + hf) % 4
                    S.op("dve", lambda e, ka=ka, hf=hf, r_=r_: e.tensor_tensor(
                        out=r_[:, hf * 512:(hf + 1) * 512], in0=ps[ka][:, :], in1=g2p_bc[:, hf * 512:(hf + 1) * 512],
                        op=ALU.mult), r=[("ps", ka)] + G2PK, w=[("rb", ri, hf), ("rb", ri)])
                S.op("dve", lambda e, r_=r_, s=s: e.tensor_tensor(out=r_[:], in0=r_[:], in1=ub[:, s, :], op=ALU.add),
                     r=[("ub", tp, s), ("rb", ri, 0), ("rb", ri, 1)], w=[("rb", ri)])
                par = lnc[0] % NLN
                lnc[0] += 1
                ln_stage1(r_, ("rb", ri), par)
                ln2_state[s] = (ri, par)

            def do_sub2b(s):
                ri, par = ln2_state[s]
                r_ = rb[ri]
                m_ = mv[par]
                ln_stage2(par)
                MK = [("mv01", par), ("mv2", par)]
                S.op("dve", lambda e: e.scalar_tensor_tensor(out=r_[:], in0=r_[:], scalar=m_[:, 0:1], in1=ln2g_bc[:],
                                                             op0=ALU.subtract, op1=ALU.mult),
                     r=[("rb", ri), "ln2g_bc"] + MK, w=[("rb", ri)])
                S.op("dve", lambda e: e.scalar_tensor_tensor(out=r_[:], in0=r_[:], scalar=m_[:, 2:3], in1=ln2b_bc[:],
                                                             op0=ALU.mult, op1=ALU.add),
                     r=[("rb", ri), "ln2b_bc"] + MK, w=[("rb", ri)])
                row0 = i * T + s * 128
                S.op("pool", lambda e: e.dma_start(out=out[row0:row0 + 128, :], in_=r_[:]),
                     r=[("rb", ri)], w=[("out", row0)], dma="st_%d" % ri)

            def y_tail():
                for s in range(NSUB):
                    do_sub2b(s)
            return XP, YP, x_cast, y_ln2a, y_tail

        tiles = [make_tile(i) for i in range(NT)]
        tiles[0][2]()
        for rnd in range(NT + 1):
            XP = list(tiles[rnd][0]) if rnd < NT else []
            if rnd + 1 < NT:
                p8_, cast_ = XP[8], tiles[rnd + 1][2]
                XP[8] = lambda p8_=p8_, cast_=cast_: (cast_(), p8_())
            YP = tiles[rnd - 1][1] if rnd >= 1 else []
            n = max(len(XP), len(YP), 3)
            for q in range(n):
                if q < len(XP) and XP[q] is not None:
                    XP[q]()
                if q < len(YP):
                    YP[q]()
                if rnd >= 2 and q == 0:
                    tiles[rnd - 2][3]()
                if rnd >= 2 and q == 2:
                    tiles[rnd - 2][4]()
                if (rnd, q) in hooks:
                    hooks[(rnd, q)]()
        tiles[NT - 1][3]()
        tiles[NT - 1][4]()

        S.emit(nc)
    return nc


_CACHE = {}


def _layout_inputs(x, c, w_ada, b_ada, w_in, conv_w, w_pool, pool_scale, w_out, ln1_g, ln1_b,
                   w_mlp_in, w_mlp_out, ln2_g, ln2_b):
    f = np.float32
    x = np.asarray(x, f)
    c = np.asarray(c, f)
    shared = {
        "rows": np.ascontiguousarray(np.stack([np.asarray(ln1_g, f)[0], np.asarray(ln1_b, f)[0],
                                               np.asarray(ln2_g, f)[0], np.asarray(ln2_b, f)[0]])),
        "b_ada": np.ascontiguousarray(np.asarray(b_ada, f)[0]),
        "w_ada": np.ascontiguousarray(np.asarray(w_ada, f)[0]),
        "w_in": np.ascontiguousarray(np.asarray(w_in, f)[0]),
        "w_pool": np.ascontiguousarray(np.asarray(w_pool, f)[0]),
        "w_out": np.ascontiguousarray(np.asarray(w_out, f)[0]),
        "w1": np.ascontiguousarray(np.asarray(w_mlp_in, f)[0]),
        "w2": np.ascontiguousarray(np.asarray(w_mlp_out, f)[0]),
    }
    g1T = np.asarray(ln1_g, f)[0].reshape(8, 128).T
    b1T = np.asarray(ln1_b, f)[0].reshape(8, 128).T
    cw = np.asarray(conv_w, f)[0]
    cwT = cw.reshape(3, 4, 128).transpose(2, 1, 0).reshape(128, 12)
    psT = np.asarray(pool_scale, f)[0].reshape(4, 128).T
    in_maps = []
    for core in range(NCORES):
        b, half = core // 2, core % 2
        start = half * TOK
        xh = np.zeros((TOK + HALO, D), f)
        xh[HALO:] = x[b, start:start + TOK]
        if half:
            xh[:HALO] = x[b, start - HALO:start]
        vecs = np.zeros((128, NV), f)
        vecs[:, C_C:C_C + 8] = c[b].reshape(8, 128).T
        vecs[:, C_G1:C_G1 + 8] = g1T
        vecs[:, C_B1:C_B1 + 8] = b1T
        vecs[:, C_CW:C_CW + 12] = cwT
        vecs[:, C_PS:C_PS + 4] = psT
        vecs[:, C_FLAG] = 1.0 if half else 0.0
        for g, win in enumerate(WINS):
            for t in range(16):
                vecs[:, C_INV + 16 * g + t] = (1.0 / win) if half else (1.0 / min(t + 1, win))
        m = dict(shared)
        m["xh"] = xh
        m["vecs"] = vecs
        in_maps.append(m)
    return in_maps


def kernel(**inputs):
    if "nc" not in _CACHE:
        _CACHE["nc"] = build_program()
    nc = _CACHE["nc"]
    in_maps = _layout_inputs(**inputs)
    res = run_bass_kernel_spmd(nc, in_maps, core_ids=list(range(NCORES)))
    outp = np.empty((BATCH, SEQ, D), np.float32)
    for core in range(NCORES):
        b, half = core // 2, core % 2
        outp[b, half * TOK:(half + 1) * TOK] = res.results[core]["out"]
    return outp
```

```python
import numpy as np
from contextlib import ExitStack
import concourse.bass as bass
import concourse.mybir as mybir
from concourse.bass_utils import run_bass_kernel_spmd

F32 = mybir.dt.float32
BF16 = mybir.dt.bfloat16
ALU = mybir.AluOpType
AF = mybir.ActivationFunctionType

D = 1024
SEQ = 8192
BATCH = 4
NCORES = 8
TOK = 4096
HALO = 16
T = 256
NSUB = T // 128
NT = TOK // T
DFF = 4096
ALPHA = 2.0 ** 0.25
EPS = 1e-5
WINS = (2, 4, 8, 16)

C_C, C_G1, C_B1, C_CW, C_PS, C_FLAG, C_INV, NV = 0, 8, 16, 24, 36, 40, 41, 105

ENGS = ("pe", "act", "dve", "pool", "sp")


class _Op:
    __slots__ = ("idx", "eng", "fn", "deps", "dma", "dma_val", "sig", "sig_val")

    def __init__(self, idx, eng, fn, dma):
        self.idx = idx
        self.eng = eng
        self.fn = fn
        self.deps = {}
        self.dma = dma
        self.dma_val = 0
        self.sig = False
        self.sig_val = 0


class Sched:
    def __init__(self):
        self.ops = []
        self.last_w = {}
        self.readers = {}
        self.dma_cnt = {}

    def op(self, eng, fn, r=(), w=(), dma=None):
        o = _Op(len(self.ops), eng, fn, dma)
        for k in r:
            lw = self.last_w.get(k)
            if lw is not None:
                o.deps[lw] = True
        for k in w:
            lw = self.last_w.get(k)
            if lw is not None and lw not in o.deps:
                o.deps[lw] = False
            for rd in self.readers.get(k, ()):
                if rd not in o.deps:
                    o.deps[rd] = False
        for k in r:
            self.readers.setdefault(k, []).append(o.idx)
        for k in w:
            self.last_w[k] = o.idx
            self.readers[k] = []
        if dma is not None:
            c = self.dma_cnt.get(dma, 0) + 1
            self.dma_cnt[dma] = c
            o.dma_val = 16 * c
        self.ops.append(o)
        return o

    def emit(self, nc, final_eng="sp"):
        ops = self.ops
        need = []
        for o in ops:
            lst = []
            for d, is_raw in o.deps.items():
                p = ops[d]
                if p.dma is not None:
                    lst.append(p)
                elif p.eng == o.eng and o.dma is None:
                    if is_raw and o.eng != "pe":
                        lst.append(p)
                else:
                    lst.append(p)
            need.append(lst)
            for p in lst:
                if p.dma is None:
                    p.sig = True
        cnt = {e: 0 for e in ENGS}
        for o in ops:
            if o.sig:
                cnt[o.eng] += 1
                o.sig_val = cnt[o.eng]
        dma_keys = sorted(self.dma_cnt.keys())
        with ExitStack() as st:
            esem = {e: st.enter_context(nc.semaphore("sem_" + e)) for e in ENGS}
            dsem = {k: st.enter_context(nc.semaphore("dsem_" + str(k))) for k in dma_keys}
            block = st.enter_context(nc.Block())
            per_eng = {e: [o for o in ops if o.eng == e] for e in ENGS}

            def run(eng_name, eng):
                waited = {}
                for o in per_eng[eng_name]:
                    for p in need[o.idx]:
                        if p.dma is not None:
                            s, v, key = dsem[p.dma], p.dma_val, ("d", p.dma)
                        else:
                            s, v, key = esem[p.eng], p.sig_val, ("e", p.eng)
                        if waited.get(key, 0) >= v:
                            continue
                        waited[key] = v
                        eng.wait_ge(s, v)
                    ins = o.fn(eng)
                    if o.dma is not None:
                        ins.then_inc(dsem[o.dma], 16)
                    elif o.sig:
                        ins.then_inc(esem[o.eng], 1)
                if eng_name == final_eng:
                    for k in dma_keys:
                        eng.wait_ge(dsem[k], 16 * self.dma_cnt[k])

            @block.tensor
            def _(e):
                run("pe", e)

            @block.scalar
            def _(e):
                run("act", e)

            @block.vector
            def _(e):
                run("dve", e)

            @block.gpsimd
            def _(e):
                run("pool", e)

            @block.sync
            def _(e):
                run("sp", e)


def build_program():
    nc = bass.Bass("TRN2", target_bir_lowering=False)

    def din(name, shape, dt=F32):
        return nc.dram_tensor(name, shape, dt, kind="ExternalInput").ap()

    xh = din("xh", [TOK + HALO, D])
    vecs = din("vecs", [128, NV])
    rows = din("rows", [4, D])
    b_ada = din("b_ada", [6 * D])
    w_ada = din("w_ada", [D, 6 * D])
    w_in = din("w_in", [D, 2048])
    w_pool = din("w_pool", [4, 128, 128])
    w_out = din("w_out", [D, D])
    w1 = din("w1", [D, DFF])
    w2 = din("w2", [DFF, D])
    out = nc.dram_tensor("out", [TOK, D], F32, kind="ExternalOutput").ap()
    w1s = nc.dram_tensor("w1s", [128, 8, DFF], BF16, kind="Internal").ap()
    w2s = nc.dram_tensor("w2s", [128, 32, D], BF16, kind="Internal").ap()

    S = Sched()
    with ExitStack() as st:
        def sb(name, shape, dt=F32):
            return st.enter_context(nc.sbuf_tensor(name, shape, dt))

        ps = [st.enter_context(nc.psum_tensor("ps%d" % k, [128, 512], F32)) for k in range(8)]
        psb = [p.bitcast(BF16) for p in ps]

        vec = sb("vec", [128, NV])
        idf = sb("idf", [128, 128])
        idb = sb("idb", [128, 128], BF16)
        ones_f = sb("ones_f", [128, 128])
        cond = sb("cond", [128, 8])
        condrep = sb("condrep", [128, 8, 128], BF16)
        w_in_sb = sb("w_in_sb", [128, 8, 2048], BF16)
        w_out_sb = sb("w_out_sb", [128, 8, D], BF16)
        w_pool_sb = sb("w_pool_sb", [128, 4, 128], BF16)
        g1p_bc = sb("g1p_bc", [128, D])
        g2p_bc = sb("g2p_bc", [128, D])
        ag_bc = sb("ag_bc", [128, D])
        ab_bc = sb("ab_bc", [128, D])
        ln2g_bc = sb("ln2g_bc", [128, D])
        ln2b_bc = sb("ln2b_bc", [128, D])
        modT = sb("modT", [128, 32])
        GB = sb("GB", [128, 24])
        xt = [sb("xt%d" % i, [128, NSUB, D]) for i in range(2)]
        xbf = sb("xbf", [128, NSUB, D], BF16)
        h1T = sb("h1T", [128, 8, T], BF16)
        h1halo = sb("h1halo", [128, 8, HALO], BF16)
        vc = [sb("vc%d" % i, [128, T]) for i in range(2)]
        uw = [sb("uw%d" % i, [128, T + HALO]) for i in range(4)]
        cv = [sb("cv%d" % i, [128, T]) for i in range(2)]
        vpw = [sb("vpw%d" % i, [128, T + HALO]) for i in range(4)]
        sA = [sb("sA%d" % i, [128, T + HALO]) for i in range(2)]
        sB = [sb("sB%d" % i, [128, T + HALO]) for i in range(2)]
        pbf = [sb("pbf%d" % i, [128, T], BF16) for i in range(2)]
        tmp16 = sb("tmp16", [128, HALO])
        ycatT = sb("ycatT", [128, 8, T], BF16)
        NR = 3
        rb = [sb("rb%d" % i, [128, D]) for i in range(NR)]
        ub2 = [sb("ub%d" % i, [128, NSUB, D]) for i in range(2)]
        ubf = [sb("ubf%d" % i, [128, D], BF16) for i in range(2)]
        h2T2 = [sb("h2T%d" % i, [128, 8, T], BF16) for i in range(2)]
        NW = 2
        w1r = [sb("w1r%d" % i, [128, 8, 512], BF16) for i in range(NW)]
        w2r = [sb("w2r%d" % i, [128, 4, D], BF16) for i in range(NW)]
        rl = [sb("rl%d" % i, [128, T]) for i in range(2)]
        aT = [sb("aT%d" % i, [128, 4, T], BF16) for i in range(2)]
        NLN = 4
        stats = [sb("stats%d" % i, [128, 12]) for i in range(NLN)]
        mv = [sb("mv%d" % i, [128, 8]) for i in range(NLN)]

        wada_v = []
        for n in range(8):
            if n < 4:
                wada_v.append((w1r[n // 2], (n % 2) * 256, ("w1r", n // 2)))
            else:
                wada_v.append((w2r[(n - 4) // 2].bitcast(BF16) if False else w2r[(n - 4) // 2], (n % 2), ("w2r", (n - 4) // 2)))
        bada_r = [sb("bada%d" % i, [128, 256]) for i in range(2)]
        tmpbc = [sb("tmpbc%d" % i, [128, 256]) for i in range(2)]
        rot = [0]

        def rbank():
            k = rot[0]
            rot[0] = (k + 1) % 4
            return k

        S.op("sp", lambda e: e.dma_start(out=vec[:], in_=vecs), w=["vec"], dma="ld_vec")
        S.op("pool", lambda e: e.memset(idf[:], 0.0), w=["idf"])
        S.op("pool", lambda e: e.affine_select(out=idf[:], in_=idf[:], pattern=[[-1, 128]],
                                               compare_op=ALU.not_equal, fill=1.0, base=0,
                                               channel_multiplier=1), r=["idf"], w=["idf"])
        S.op("pool", lambda e: e.tensor_copy(out=idb[:], in_=idf[:]), r=["idf"], w=["idb"])
        S.op("pool", lambda e: e.memset(ones_f[:], 1.0), w=["ones_f"])
        S.op("act", lambda e: e.activation(out=cond[:], in_=vec[:, C_C:C_C + 8], func=AF.Silu),
             r=["vec"], w=["cond"])
        for kc in range(8):
            S.op("dve", lambda e, kc=kc: e.tensor_scalar(out=condrep[:, kc, :], in0=ones_f[:],
                                                         scalar1=cond[:, kc:kc + 1], scalar2=None,
                                                         op0=ALU.mult),
                 r=["ones_f", "cond"], w=["condrep"])

        for i, (dst, name) in enumerate(((ag_bc, "ag_bc"), (ab_bc, "ab_bc"),
                                         (ln2g_bc, "ln2g_bc"), (ln2b_bc, "ln2b_bc"))):
            S.op("sp", lambda e, dst=dst, i=i: e.dma_start(out=dst[:], in_=rows[i].partition_broadcast(128)),
                 w=[name], dma="ld_" + name)
        S.op("act", lambda e: e.mul(out=ag_bc[:], in_=ag_bc[:], mul=ALPHA), r=["ag_bc"], w=["ag_bc"])
        S.op("act", lambda e: e.mul(out=ab_bc[:], in_=ab_bc[:], mul=ALPHA), r=["ab_bc"], w=["ab_bc"])

        def cast_dma(dst, src, wkeys, key):
            S.op("pool", lambda e: e.dma_start(out=dst, in_=src), w=wkeys, dma=key)

        def mod_block(blk):
            slot = blk % 2
            wn = blk % 8
            wbuf, wsel, wkey = wada_v[wn]
            col0 = blk * 256
            vi, off = col0 // D, col0 % D
            if wn < 4:
                wdst = wbuf[:, :, wsel:wsel + 256]

                def wsl(kc):
                    return wbuf[:, kc, wsel:wsel + 256]
            else:
                wdst = wbuf[:, 2 * wsel:2 * wsel + 2, :].rearrange("p a (b n) -> p (a b) n", n=256)

                def wsl(kc):
                    return wbuf[:, 2 * wsel + kc // 4, (kc % 4) * 256:(kc % 4 + 1) * 256]
            cast_dma(wdst, w_ada[:, col0:col0 + 256].rearrange("(kc p) n -> p kc n", p=128),
                     [("wada", wn)], "wada%d" % wn)
            S.op("sp", lambda e: e.dma_start(out=bada_r[slot][:],
                                             in_=b_ada[col0:col0 + 256].partition_broadcast(128)),
                 w=[("bada", slot)], dma="bada%d" % slot)
            k = rbank()

            def mm(e):
                for kc in range(8):
                    ins = e.matmul(ps[k][:, 0:256], lhsT=condrep[:, kc, :], rhs=wsl(kc),
                                   start=(kc == 0), stop=(kc == 7))
                return ins
            S.op("pe", mm, r=[("wada", wn), wkey, "condrep"], w=[("ps", k)])
            if vi in (2, 5):
                dst, name = (g1p_bc, "g1p_bc") if vi == 2 else (g2p_bc, "g2p_bc")
                S.op("dve", lambda e: e.scalar_tensor_tensor(out=dst[:, off:off + 256], in0=ps[k][:, 0:256],
                                                             scalar=1.0, in1=bada_r[slot][:],
                                                             op0=ALU.add, op1=ALU.add),
                     r=[("ps", k), ("bada", slot)], w=[(name, off)])
            else:
                addc = 1.0 if vi in (1, 4) else 0.0
                S.op("dve", lambda e: e.scalar_tensor_tensor(out=tmpbc[slot][:], in0=ps[k][:, 0:256],
                                                             scalar=addc, in1=bada_r[slot][:],
                                                             op0=ALU.add, op1=ALU.add),
                     r=[("ps", k), ("bada", slot)], w=[("tmpbc", slot)])
                vslot = {0: 0, 1: 1, 3: 2, 4: 3}[vi]

                def mmT(e):
                    for c in range(2):
                        col = vslot * 8 + off // 128 + c
                        ins = e.matmul(ps[7][:, col:col + 1], lhsT=tmpbc[slot][0:1, c * 128:(c + 1) * 128],
                                       rhs=ones_f[0:1, 0:1], start=True, stop=True)
                    return ins
                S.op("pe", mmT, r=[("tmpbc", slot), "ones_f"], w=[("ps", 7)])

        hooks = {}
        for blk in range(8):
            mod_block(blk)
        S.op("act", lambda e: e.copy(out=modT[:, 0:16], in_=ps[7][:, 0:16]), r=[("ps", 7)], w=["modT1"])
        for h in range(2):
            cast_dma(w_in_sb[:, 4 * h:4 * h + 4, :],
                     w_in[512 * h:512 * (h + 1), :].rearrange("(kc p) n -> p kc n", p=128),
                     [("w_in", h)], "c_w_in%d" % h)
        cast_dma(w_pool_sb[:], w_pool.rearrange("g c d -> c g d"), ["w_pool"], "c_w_pool")
        cast_dma(w_out_sb[:], w_out.rearrange("(kc p) n -> p kc n", p=128), ["w_out"], "c_w_out")

        def mod_rest(q):
            for blk in (8 + 2 * q, 9 + 2 * q):
                mod_block(blk)
            if q == 5:
                S.op("act", lambda e: e.copy(out=modT[:, 16:32], in_=ps[7][:, 16:32]), r=[("ps", 7)], w=["modT2"])
                S.op("dve", lambda e: e.tensor_tensor(out=GB[:, 0:8], in0=vec[:, C_G1:C_G1 + 8], in1=modT[:, 24:32],
                                                      op=ALU.mult), r=["vec", "modT2"], w=["G2"])
                S.op("dve", lambda e: e.tensor_tensor(out=GB[:, 16:24], in0=vec[:, C_B1:C_B1 + 8], in1=modT[:, 24:32],
                                                      op=ALU.mult), r=["vec", "modT2"], w=["B2t"])
                S.op("dve", lambda e: e.tensor_tensor(out=GB[:, 8:16], in0=GB[:, 16:24], in1=modT[:, 16:24],
                                                      op=ALU.add), r=["B2t", "modT2"], w=["B2"])
        for q in range(8):
            hooks[(0, q)] = (lambda q=q: mod_rest(q))
        G1PK = [("g1p_bc", o) for o in range(0, D, 256)]
        G2PK = [("g2p_bc", o) for o in range(0, D, 256)]

        def ln_stage1(src_ap, skey, par):
            st_, m_ = stats[par], mv[par]
            S.op("dve", lambda e: e.bn_stats(out=st_[:, 0:6], in_=src_ap[:, 0:512]), r=[skey], w=[("st0", par)])
            S.op("dve", lambda e: e.bn_stats(out=st_[:, 6:12], in_=src_ap[:, 512:1024]), r=[skey], w=[("st1", par)])
            S.op("dve", lambda e: e.bn_aggr(out=m_[:, 0:2], in_=st_[:, 0:12]),
                 r=[("st0", par), ("st1", par)], w=[("mv01", par)])
            S.op("dve", lambda e: e.tensor_scalar(out=m_[:, 4:5], in0=m_[:, 1:2], scalar1=EPS, scalar2=None,
                                                  op0=ALU.add), r=[("mv01", par)], w=[("mv4", par)])

        def ln_stage2(par):
            m_ = mv[par]
            S.op("act", lambda e: e.activation(out=m_[:, 5:6], in_=m_[:, 4:5], func=AF.Sqrt),
                 r=[("mv4", par)], w=[("mv5", par)])
            S.op("dve", lambda e: e.reciprocal(out=m_[:, 2:3], in_=m_[:, 5:6]), r=[("mv5", par)], w=[("mv2", par)])

        lnc = [0]
        rbc = [0]

        def load_x(i):
            slot = i % 2
            S.op("sp", lambda e: e.dma_start(
                out=xt[slot][:], in_=xh[HALO + i * T:HALO + (i + 1) * T, :].rearrange("(s p) f -> p s f", p=128)),
                w=[("xt", slot, s) for s in range(NSUB)], dma="ld_x%d" % slot)

        load_x(0)

        def make_tile(i):
            xs = i % 2
            tp = i % 2
            ub = ub2[tp]
            h2T = h2T2[tp]
            XP, YP = [], []

            def x_cast():
              for s in range(NSUB):
                S.op("act", lambda e, s=s: e.copy(out=xbf[:, s, :], in_=xt[xs][:, s, :]),
                     r=[("xt", xs, s)], w=[("xbf", s)])

            def x_front():
              for kq in range(2):
                k = rbank()

                def trx(e, kq=kq, k=k):
                    for kk in range(4):
                        kc = kq * 4 + kk
                        for s in range(NSUB):
                            ins = e.transpose(psb[k][:, kk * T + s * 128:kk * T + (s + 1) * 128],
                                              xbf[:, s, kc * 128:(kc + 1) * 128], idb[:])
                    return ins
                S.op("pe", trx, r=[("xbf", s) for s in range(NSUB)] + ["idb"], w=[("ps", k)])
                for kk in range(4):
                    kc = kq * 4 + kk
                    S.op("act", lambda e, kk=kk, kc=kc, k=k: e.activation(
                        out=h1T[:, kc, :], in_=psb[k][:, kk * T:(kk + 1) * T], func=AF.Identity,
                        bias=modT[:, kc:kc + 1], scale=modT[:, 8 + kc:9 + kc]),
                        r=[("ps", k), "modT1"], w=[("h1T", kc)])
              if i == 0:
                S.op("sp", lambda e: e.dma_start(out=rb[0][0:HALO, :], in_=xh[0:HALO, :]), w=[("rb", 0)], dma="ld_halo")
                S.op("act", lambda e: e.copy(out=ubf[0][0:HALO, :], in_=rb[0][0:HALO, :]), r=[("rb", 0)], w=[("ubf", 0)])
                k = rbank()

                def trh(e, k=k):
                    for kc in range(8):
                        ins = e.transpose(psb[k][:, kc * HALO:(kc + 1) * HALO],
                                          ubf[0][0:HALO, kc * 128:(kc + 1) * 128], idb[0:HALO, 0:HALO])
                    return ins
                S.op("pe", trh, r=[("ubf", 0), "idb"], w=[("ps", k)])
                for kc in range(8):
                    S.op("dve", lambda e, kc=kc, k=k: e.tensor_scalar(
                        out=h1halo[:, kc, :], in0=psb[k][:, kc * HALO:(kc + 1) * HALO],
                        scalar1=modT[:, 8 + kc:9 + kc], scalar2=modT[:, kc:kc + 1],
                        op0=ALU.mult, op1=ALU.add), r=[("ps", k), "modT1"], w=[("h1halo", kc)])
                    S.op("dve", lambda e, kc=kc: e.tensor_scalar(
                        out=h1halo[:, kc, :], in0=h1halo[:, kc, :], scalar1=vec[:, C_FLAG:C_FLAG + 1],
                        scalar2=None, op0=ALU.mult), r=[("h1halo", kc), "vec"], w=[("h1halo", kc)])
              if i + 1 < NT:
                load_x(i + 1)
            XP.append(x_front)

            H1K = [("h1T", kc) for kc in range(8)]
            H1HK = [("h1halo", kc) for kc in range(8)]
            WINK = [("w_in", 0), ("w_in", 1)]

            def inproj(col0, halo=False):
                k = rbank()
                n = HALO if halo else T

                def mm(e):
                    for kc in range(8):
                        rhs = h1halo[:, kc, :] if halo else h1T[:, kc, :]
                        ins = e.matmul(ps[k][:, 0:n], lhsT=w_in_sb[:, kc, col0:col0 + 128], rhs=rhs,
                                       start=(kc == 0), stop=(kc == 7))
                    return ins
                S.op("pe", mm, r=WINK + (H1HK if halo else H1K), w=[("ps", k)])
                return k

            def chunk_q(j):
                par = j % 2
                k_q = rbank()
                S.op("pe", lambda e, k=k_q: e.matmul(ps[k][:, 0:T], lhsT=w_pool_sb[:, j, :], rhs=pbf[par][:],
                                                     start=True, stop=True),
                     r=["w_pool", ("pbf", par)], w=[("ps", k_q)])
                S.op("act", lambda e, k=k_q: e.mul(out=ycatT[:, 4 + j, :], in_=ps[k][:, 0:T],
                                                   mul=vec[:, C_PS + j:C_PS + j + 1]),
                     r=[("ps", k_q), "vec"], w=[("ycatT", 4 + j)])

            def do_chunk(j):
                par = j % 2
                if j > 0:
                    chunk_q(j - 1)
                U, V = uw[j], vpw[j]
                if i > 0:
                    S.op("act", lambda e: e.copy(out=U[:, 0:HALO], in_=U[:, T:T + HALO]),
                         r=[("uw", j)], w=[("uwh", j)])
                    S.op("act", lambda e: e.copy(out=V[:, 0:HALO], in_=V[:, T:T + HALO]),
                         r=[("vpw", j)], w=[("vpwh", j)])
                k_vc = inproj(1024 + 128 * j)
                S.op("act", lambda e, k=k_vc: e.copy(out=vc[par][:], in_=ps[k][:, 0:T]),
                     r=[("ps", k_vc)], w=[("vc", par)])
                k_gc = inproj(512 + 128 * j)
                S.op("dve", lambda e, k=k_gc: e.tensor_tensor(out=U[:, HALO:HALO + T], in0=ps[k][:, 0:T],
                                                              in1=vc[par][:], op=ALU.mult),
                     r=[("ps", k_gc), ("vc", par)], w=[("uw", j)])
                if i == 0:
                    k_h = inproj(1024 + 128 * j, halo=True)
                    S.op("act", lambda e, k=k_h: e.copy(out=tmp16[:], in_=ps[k][:, 0:HALO]),
                         r=[("ps", k_h)], w=["tmp16"])
                    k_h2 = inproj(512 + 128 * j, halo=True)
                    S.op("dve", lambda e, k=k_h2: e.tensor_tensor(out=U[:, 0:HALO], in0=ps[k][:, 0:HALO],
                                                                  in1=tmp16[:], op=ALU.mult),
                         r=[("ps", k_h2), "tmp16"], w=[("uwh", j)])
                k_vp = inproj(1536 + 128 * j)
                S.op("act", lambda e, k=k_vp: e.copy(out=V[:, HALO:HALO + T], in_=ps[k][:, 0:T]),
                     r=[("ps", k_vp)], w=[("vpw", j)])
                if i == 0:
                    k_h3 = inproj(1536 + 128 * j, halo=True)
                    S.op("act", lambda e, k=k_h3: e.copy(out=V[:, 0:HALO], in_=ps[k][:, 0:HALO]),
                         r=[("ps", k_h3)], w=[("vpwh", j)])
                cw = C_CW + 3 * j
                S.op("act", lambda e: e.mul(out=cv[par][:], in_=U[:, HALO:HALO + T], mul=vec[:, cw + 2:cw + 3]),
                     r=[("uw", j), "vec"], w=[("cv", par)])
                S.op("dve", lambda e: e.scalar_tensor_tensor(out=cv[par][:], in0=U[:, HALO - 1:HALO - 1 + T],
                                                             scalar=vec[:, cw + 1:cw + 2], in1=cv[par][:],
                                                             op0=ALU.mult, op1=ALU.add),
                     r=[("uw", j), ("uwh", j), ("cv", par), "vec"], w=[("cv", par)])
                S.op("dve", lambda e: e.scalar_tensor_tensor(out=cv[par][:], in0=U[:, HALO - 2:HALO - 2 + T],
                                                             scalar=vec[:, cw:cw + 1], in1=cv[par][:],
                                                             op0=ALU.mult, op1=ALU.add),
                     r=[("uw", j), ("uwh", j), ("cv", par), "vec"], w=[("cv", par)])
                k_gb = inproj(128 * j)
                S.op("dve", lambda e, k=k_gb: e.tensor_tensor(out=ycatT[:, j, :], in0=ps[k][:, 0:T],
                                                              in1=cv[par][:], op=ALU.mult),
                     r=[("ps", k_gb), ("cv", par)], w=[("ycatT", j)])
                L = T + HALO
                VK = [("vpw", j), ("vpwh", j)]
                S.op("dve", lambda e: e.tensor_tensor(out=sA[par][:, 1:L], in0=V[:, 1:L], in1=V[:, 0:L - 1], op=ALU.add),
                     r=VK, w=[("sA", par)])
                cur, curk = sA, "sA"
                if j >= 1:
                    S.op("dve", lambda e: e.tensor_tensor(out=sB[par][:, 3:L], in0=sA[par][:, 3:L],
                                                          in1=sA[par][:, 1:L - 2], op=ALU.add),
                         r=[("sA", par)], w=[("sB", par)])
                    cur, curk = sB, "sB"
                if j >= 2:
                    S.op("dve", lambda e: e.tensor_tensor(out=sA[par][:, 7:L], in0=sB[par][:, 7:L],
                                                          in1=sB[par][:, 3:L - 4], op=ALU.add),
                         r=[("sB", par)], w=[("sA", par)])
                    cur, curk = sA, "sA"
                if j >= 3:
                    S.op("dve", lambda e: e.tensor_tensor(out=sB[par][:, 15:L], in0=sA[par][:, 15:L],
                                                          in1=sA[par][:, 7:L - 8], op=ALU.add),
                         r=[("sA", par)], w=[("sB", par)])
                    cur, curk = sB, "sB"
                win = float(WINS[j])
                S.op("dve", lambda e, cur=cur: e.scalar_tensor_tensor(
                    out=pbf[par][:], in0=cur[par][:, HALO:HALO + T], scalar=1.0 / win,
                    in1=V[:, HALO:HALO + T], op0=ALU.mult, op1=ALU.subtract),
                    r=[(curk, par), ("vpw", j)], w=[("pbf", par)])
                if i == 0:
                    ic = C_INV + 16 * j
                    S.op("dve", lambda e, cur=cur: e.tensor_tensor(
                        out=tmp16[:], in0=cur[par][:, HALO:2 * HALO], in1=vec[:, ic:ic + 16], op=ALU.mult),
                        r=[(curk, par), "vec"], w=["tmp16"])
                    S.op("dve", lambda e: e.tensor_tensor(
                        out=pbf[par][:, 0:HALO], in0=tmp16[:], in1=V[:, HALO:2 * HALO], op=ALU.subtract),
                        r=["tmp16", ("vpw", j), ("pbf", par)], w=[("pbf", par)])

            for j in range(4):
                XP.append(lambda j=j: do_chunk(j))
            XP.append(lambda: chunk_q(3))

            YK = [("ycatT", c) for c in range(8)]
            ln1_state = {}
            ln2_state = {}

            def sub1A(s):
                ri = rbc[0] % NR
                rbc[0] += 1
                r_ = rb[ri]
                for hf in range(2):
                    ka = rbank()

                    def mmo(e, ka=ka, hf=hf, s=s):
                        for kc in range(8):
                            ins = e.matmul(ps[ka][:, :], lhsT=ycatT[:, kc, s * 128:(s + 1) * 128],
                                           rhs=w_out_sb[:, kc, hf * 512:(hf + 1) * 512],
                                           start=(kc == 0), stop=(kc == 7))
                        return ins
                    S.op("pe", mmo, r=YK + ["w_out"], w=[("ps", ka)])
                    S.op("dve", lambda e, ka=ka, hf=hf, r_=r_: e.tensor_tensor(
                        out=r_[:, hf * 512:(hf + 1) * 512], in0=ps[ka][:, :], in1=g1p_bc[:, hf * 512:(hf + 1) * 512],
                        op=ALU.mult), r=[("ps", ka)] + G1PK, w=[("rb", ri, hf), ("rb", ri)])
                S.op("dve", lambda e, r_=r_, s=s: e.scalar_tensor_tensor(
                    out=r_[:], in0=xt[xs][:, s, :], scalar=ALPHA, in1=r_[:], op0=ALU.mult, op1=ALU.add),
                    r=[("xt", xs, s), ("rb", ri, 0), ("rb", ri, 1)], w=[("rb", ri)])
                par = lnc[0] % NLN
                lnc[0] += 1
                ln_stage1(r_, ("rb", ri), par)
                ln1_state[s] = (ri, par)

            def sub1A2(s):
                ri, par = ln1_state[s]
                r_ = rb[ri]
                m_ = mv[par]
                ln_stage2(par)
                MK = [("mv01", par), ("mv2", par)]
                up = s % 2
                S.op("dve", lambda e: e.tensor_scalar(out=ubf[up][:], in0=r_[:], scalar1=m_[:, 0:1], scalar2=m_[:, 2:3],
                                                      op0=ALU.subtract, op1=ALU.mult),
                     r=[("rb", ri)] + MK, w=[("ubf", up)])
                S.op("dve", lambda e: e.scalar_tensor_tensor(out=ub[:, s, :], in0=r_[:], scalar=m_[:, 0:1], in1=ag_bc[:],
                                                             op0=ALU.subtract, op1=ALU.mult),
                     r=[("rb", ri), "ag_bc"] + MK, w=[("ub", tp, s)])
                S.op("dve", lambda e: e.scalar_tensor_tensor(out=ub[:, s, :], in0=ub[:, s, :], scalar=m_[:, 2:3],
                                                             in1=ab_bc[:], op0=ALU.mult, op1=ALU.add),
                     r=[("ub", tp, s), "ab_bc"] + MK, w=[("ub", tp, s)])

            def sub1B(s):
                up = s % 2
                for kq in range(2):
                    k = rbank()

                    def tru(e, kq=kq, k=k, up=up, s=s):
                        for kk in range(4):
                            kc = kq * 4 + kk
                            ins = e.transpose(psb[k][:, kk * 128:(kk + 1) * 128],
                                              ubf[up][:, kc * 128:(kc + 1) * 128], idb[:])
                        return ins
                    S.op("pe", tru, r=[("ubf", up), "idb"], w=[("ps", k)])
                    for kk in range(4):
                        kc = kq * 4 + kk
                        S.op("act", lambda e, kk=kk, kc=kc, k=k, s=s: e.activation(
                            out=h2T[:, kc, s * 128:(s + 1) * 128], in_=psb[k][:, kk * 128:(kk + 1) * 128],
                            func=AF.Identity, bias=GB[:, 8 + kc:9 + kc], scale=GB[:, kc:kc + 1]),
                            r=[("ps", k), "G2", "B2"], w=[("h2T", tp, kc, s)])
            XP.append(lambda: sub1A(0))

            def p7():
                sub1A2(0)
                sub1A(1)
            XP.append(p7)

            def p8():
                sub1B(0)
                sub1A2(1)
            XP.append(p8)
            XP.append(lambda: sub1B(1))


            H2K = [("h2T", tp, kc, s) for kc in range(8) for s in range(NSUB)]

            def load_w(b):
                slot = (i * 8 + b) % NW
                if i == 0:
                    S.op("pool", lambda e: e.dma_start(
                        out=w1r[slot][:], in_=w1[:, b * 512:(b + 1) * 512].rearrange("(kc p) n -> p kc n", p=128)),
                        w=[("w1r", slot)], dma="ld_w1_%d" % slot)
                    S.op("sp", lambda e: e.dma_start(out=w1s[:, :, b * 512:(b + 1) * 512], in_=w1r[slot][:]),
                         r=[("w1r", slot)], w=[("w1s", b)], dma="sv_w1_%d" % slot)
                    S.op("pool", lambda e: e.dma_start(
                        out=w2r[slot][:], in_=w2[b * 512:(b + 1) * 512, :].rearrange("(c p) n -> p c n", p=128)),
                        w=[("w2r", slot)], dma="ld_w2_%d" % slot)
                    S.op("sp", lambda e: e.dma_start(out=w2s[:, 4 * b:4 * b + 4, :], in_=w2r[slot][:]),
                         r=[("w2r", slot)], w=[("w2s", b)], dma="sv_w2_%d" % slot)
                    return
                S.op("sp", lambda e: e.dma_start(out=w1r[slot][:], in_=w1s[:, :, b * 512:(b + 1) * 512]),
                     r=[("w1s", b)], w=[("w1r", slot)], dma="ld_w1_%d" % slot)
                S.op("sp", lambda e: e.dma_start(out=w2r[slot][:], in_=w2s[:, 4 * b:4 * b + 4, :]),
                     r=[("w2s", b)], w=[("w2r", slot)], dma="ld_w2_%d" % slot)

            def stage_a(b):
                slot = (i * 8 + b) % NW
                ap_ = b % 2
                for c in range(4):
                    k = rbank()
                    rp = c % 2

                    def mma(e, k=k, c=c):
                        for kc in range(8):
                            ins = e.matmul(ps[k][:, 0:T], lhsT=w1r[slot][:, kc, c * 128:(c + 1) * 128],
                                           rhs=h2T[:, kc, :], start=(kc == 0), stop=(kc == 7))
                        return ins
                    S.op("pe", mma, r=[("w1r", slot)] + H2K, w=[("ps", k)])
                    S.op("act", lambda e, k=k, rp=rp: e.activation(out=rl[rp][:], in_=ps[k][:, 0:T], func=AF.Relu),
                         r=[("ps", k)], w=[("rl", rp)])
                    S.op("act", lambda e, rp=rp, c=c: e.activation(out=aT[ap_][:, c, :], in_=rl[rp][:], func=AF.Square),
                         r=[("rl", rp)], w=[("aT", ap_, c)])

            def stage_b(b):
                slot = (i * 8 + b) % NW
                ap_ = b % 2
                for s in range(NSUB):
                    for hf in range(2):
                        ka = 4 + (2 * s + hf) % 4

                        def mmb(e, ka=ka, s=s, hf=hf):
                            for c in range(4):
                                ins = e.matmul(ps[ka][:, :], lhsT=aT[ap_][:, c, s * 128:(s + 1) * 128],
                                               rhs=w2r[slot][:, c, hf * 512:(hf + 1) * 512],
                                               start=(b == 0 and c == 0), stop=(b == 7 and c == 3))
                            return ins
                        S.op("pe", mmb, r=[("w2r", slot)] + [("aT", ap_, c) for c in range(4)], w=[("ps", ka)])

            def y_first():
                load_w(0)
                stage_a(0)
            YP.append(y_first)

            def y_mid(b):
                load_w(b)
                stage_a(b)
                stage_b(b - 1)
            for b in range(1, 8):
                YP.append(lambda b=b: y_mid(b))
            YP.append(lambda: stage_b(7))

            def y_ln2a():
                for s in range(NSUB):
                    do_sub2(s)

            def do_sub2(s):
                ri = rbc[0] % NR
                rbc[0] += 1
                r_ = rb[ri]
                for hf in range(2):
                    ka = 4 + (2 * s + hf) % 4
                    S.op("dve", lambda e, ka=ka, hf=hf, r_=r_: e.tensor_tensor(
                        out=r_[:, hf * 512:(hf + 1) * 512], in0=ps[ka][:, :], in1=g2p_bc[:, hf * 512:(hf + 1) * 512],
                        op=ALU.mult), r=[("ps", ka)] + G2PK, w=[("rb", ri, hf), ("rb", ri)])
                S.op("dve", lambda e, r_=r_, s=s: e.tensor_tensor(out=r_[:], in0=r_[:], in1=ub[:, s, :], op=ALU.add),
                     r=[("ub", tp, s), ("rb", ri, 0), ("rb", ri, 1)], w=[("rb", ri)])
                par = lnc[0] % NLN
                lnc[0] += 1
                ln_stage1(r_, ("rb", ri), par)
                ln2_state[s] = (ri, par)

            def do_sub2b(s):
                ri, par = ln2_state[s]
                r_ = rb[ri]
                m_ = mv[par]
                ln_stage2(par)
                MK = [("mv01", par), ("mv2", par)]
                S.op("dve", lambda e: e.scalar_tensor_tensor(out=r_[:], in0=r_[:], scalar=m_[:, 0:1], in1=ln2g_bc[:],
                                                             op0=ALU.subtract, op1=ALU.mult),
                     r=[("rb", ri), "ln2g_bc"] + MK, w=[("rb", ri)])
                S.op("dve", lambda e: e.scalar_tensor_tensor(out=r_[:], in0=r_[:], scalar=m_[:, 2:3], in1=ln2b_bc[:],
                                                             op0=ALU.mult, op1=ALU.add),
                     r=[("rb", ri), "ln2b_bc"] + MK, w=[("rb", ri)])
                row0 = i * T + s * 128
                S.op("pool", lambda e: e.dma_start(out=out[row0:row0 + 128, :], in_=r_[:]),
                     r=[("rb", ri)], w=[("out", row0)], dma="st_%d" % ri)

            def y_tail():
                for s in range(NSUB):
                    do_sub2b(s)
            return XP, YP, x_cast, y_ln2a, y_tail

        tiles = [make_tile(i) for i in range(NT)]
        tiles[0][2]()
        for rnd in range(NT + 1):
            XP = list(tiles[rnd][0]) if rnd < NT else []
            if rnd + 1 < NT:
                p8_, cast_ = XP[8], tiles[rnd + 1][2]
                XP[8] = lambda p8_=p8_, cast_=cast_: (cast_(), p8_())
            YP = tiles[rnd - 1][1] if rnd >= 1 else []
            n = max(len(XP), len(YP), 3)
            for q in range(n):
                if q < len(XP) and XP[q] is not None:
                    XP[q]()
                if q < len(YP):
                    YP[q]()
                if rnd >= 2 and q == 0:
                    tiles[rnd - 2][3]()
                if rnd >= 2 and q == 2:
                    tiles[rnd - 2][4]()
                if (rnd, q) in hooks:
                    hooks[(rnd, q)]()
        tiles[NT - 1][3]()
        tiles[NT - 1][4]()

        S.emit(nc)
    return nc


_CACHE = {}


def _layout_inputs(x, c, w_ada, b_ada, w_in, conv_w, w_pool, pool_scale, w_out, ln1_g, ln1_b,
                   w_mlp_in, w_mlp_out, ln2_g, ln2_b):
    f = np.float32
    x = np.asarray(x, f)
    c = np.asarray(c, f)
    shared = {
        "rows": np.ascontiguousarray(np.stack([np.asarray(ln1_g, f)[0], np.asarray(ln1_b, f)[0],
                                               np.asarray(ln2_g, f)[0], np.asarray(ln2_b, f)[0]])),
        "b_ada": np.ascontiguousarray(np.asarray(b_ada, f)[0]),
        "w_ada": np.ascontiguousarray(np.asarray(w_ada, f)[0]),
        "w_in": np.ascontiguousarray(np.asarray(w_in, f)[0]),
        "w_pool": np.ascontiguousarray(np.asarray(w_pool, f)[0]),
        "w_out": np.ascontiguousarray(np.asarray(w_out, f)[0]),
        "w1": np.ascontiguousarray(np.asarray(w_mlp_in, f)[0]),
        "w2": np.ascontiguousarray(np.asarray(w_mlp_out, f)[0]),
    }
    g1T = np.asarray(ln1_g, f)[0].reshape(8, 128).T
    b1T = np.asarray(ln1_b, f)[0].reshape(8, 128).T
    cw = np.asarray(conv_w, f)[0]
    cwT = cw.reshape(3, 4, 128).transpose(2, 1, 0).reshape(128, 12)
    psT = np.asarray(pool_scale, f)[0].reshape(4, 128).T
    in_maps = []
    for core in range(NCORES):
        b, half = core // 2, core % 2
        start = half * TOK
        xh = np.zeros((TOK + HALO, D), f)
        xh[HALO:] = x[b, start:start + TOK]
        if half:
            xh[:HALO] = x[b, start - HALO:start]
        vecs = np.zeros((128, NV), f)
        vecs[:, C_C:C_C + 8] = c[b].reshape(8, 128).T
        vecs[:, C_G1:C_G1 + 8] = g1T
        vecs[:, C_B1:C_B1 + 8] = b1T
        vecs[:, C_CW:C_CW + 12] = cwT
        vecs[:, C_PS:C_PS + 4] = psT
        vecs[:, C_FLAG] = 1.0 if half else 0.0
        for g, win in enumerate(WINS):
            for t in range(16):
                vecs[:, C_INV + 16 * g + t] = (1.0 / win) if half else (1.0 / min(t + 1, win))
        m = dict(shared)
        m["xh"] = xh
        m["vecs"] = vecs
        in_maps.append(m)
    return in_maps


def kernel(**inputs):
    if "nc" not in _CACHE:
        _CACHE["nc"] = build_program()
    nc = _CACHE["nc"]
    in_maps = _layout_inputs(**inputs)
    res = run_bass_kernel_spmd(nc, in_maps, core_ids=list(range(NCORES)))
    outp = np.empty((BATCH, SEQ, D), np.float32)
    for core in range(NCORES):
        b, half = core // 2, core % 2
        outp[b, half * TOK:(half + 1) * TOK] = res.results[core]["out"]
    return outp
```
